# Optimizing a Trainium2 kernel written in Bass

```python
import math
import jax, jax.numpy as jnp
from jax import lax
import numpy as np

D_MODEL = 1024
BATCH = 32
SEQ = 2048
DEPTH = 2

D_MIX = D_MODEL
D_FF = 2816
EPS = 1e-6

CONV_W = D_MIX // 4
CONV_GROUPS = 4
CONV_K = 3

MLA_HEADS = 8
NOPE_DIM = 64
ROPE_DIM = 32
V_DIM = 64
Q_RANK = 256
KV_RANK = 128
QK_DIM = NOPE_DIM + ROPE_DIM
MLA_W = MLA_HEADS * V_DIM
ROPE_THETA = 10000.0
Q_BLOCK = 128

GMLP_W = D_MIX - CONV_W - MLA_W
GMLP_HEADS = 4
GMLP_HD = GMLP_W // GMLP_HEADS
CHUNK = 128

SPLITS = np.cumsum([CONV_W, CONV_W, CONV_W, Q_RANK, KV_RANK, ROPE_DIM, GMLP_W]).tolist()
N_IN = 3 * CONV_W + Q_RANK + KV_RANK + ROPE_DIM + 2 * GMLP_W

kernel_name = "hybrid_macaron_parallel_heads_encoder"


def rms_norm(x, g):
    xf = x.astype(jnp.float32)
    y = xf * lax.rsqrt(jnp.mean(xf * xf, axis=-1, keepdims=True) + EPS)
    return (y * g.astype(jnp.float32)).astype(x.dtype)


def swiglu_ffn(x, w_gu, w_down):
    gate, up = jnp.split(x @ w_gu, 2, axis=-1)
    return (jax.nn.silu(gate) * up) @ w_down


def rope_tables(seq, dim):
    pos = jnp.arange(seq, dtype=jnp.float32)
    inv = 1.0 / (ROPE_THETA ** (jnp.arange(0, dim, 2, dtype=jnp.float32) / dim))
    ang = pos[:, None] * inv[None, :]
    return jnp.cos(ang), jnp.sin(ang)


def apply_rope(x, cos, sin):
    cos = cos.astype(x.dtype)
    sin = sin.astype(x.dtype)
    x1, x2 = jnp.split(x, 2, axis=-1)
    return jnp.concatenate([x1 * cos - x2 * sin, x2 * cos + x1 * sin], axis=-1)


def short_conv_mixer(xc, gate_b, gate_c, conv_w, conv_b):
    z = gate_c * xc
    w = conv_w.astype(z.dtype)[:, None, :]
    conv = lax.conv_general_dilated(
        z, w, window_strides=(1,), padding=[(CONV_K // 2, CONV_K // 2)],
        dimension_numbers=("NWC", "WIO", "NWC"), feature_group_count=CONV_W)
    return gate_b * (conv + conv_b)


def mla_mixer(cq, ckv, k_rope, q_norm_g, w_uq, kv_norm_g, w_ukv, cos, sin):
    bsz, seq, _ = cq.shape
    q = (rms_norm(cq, q_norm_g) @ w_uq).reshape(bsz, seq, MLA_HEADS, QK_DIM)
    q_nope = q[..., :NOPE_DIM]
    q_rope = apply_rope(q[..., NOPE_DIM:], cos[None, :, None, :], sin[None, :, None, :])
    kv = (rms_norm(ckv, kv_norm_g) @ w_ukv).reshape(bsz, seq, MLA_HEADS, NOPE_DIM + V_DIM)
    k_nope, v = kv[..., :NOPE_DIM], kv[..., NOPE_DIM:]
    k_rope = apply_rope(k_rope, cos[None], sin[None])
    scale = 1.0 / math.sqrt(QK_DIM)
    nb = seq // Q_BLOCK
    qn_blocks = (q_nope * scale).reshape(bsz, nb, Q_BLOCK, MLA_HEADS, NOPE_DIM).transpose(1, 0, 2, 3, 4)
    qr_blocks = (q_rope * scale).reshape(bsz, nb, Q_BLOCK, MLA_HEADS, ROPE_DIM).transpose(1, 0, 2, 3, 4)

    def attend(blk):
        qn, qr = blk
        s = (jnp.einsum("bqhd,bkhd->bhqk", qn, k_nope)
             + jnp.einsum("bqhr,bkr->bhqk", qr, k_rope)).astype(jnp.float32)
        p = jax.nn.softmax(s, axis=-1).astype(v.dtype)
        return jnp.einsum("bhqk,bkhd->bqhd", p, v)

    o = lax.map(attend, (qn_blocks, qr_blocks))
    return o.transpose(1, 0, 2, 3, 4).reshape(bsz, seq, MLA_W)


def gmlp_mixer(zu, zv, norm_g, ws, bias):
    u = jax.nn.gelu(zu, approximate=False)
    v = rms_norm(jax.nn.gelu(zv, approximate=False), norm_g)
    bsz, seq, _ = v.shape
    v = v.reshape(bsz, seq // CHUNK, CHUNK, GMLP_HEADS, GMLP_HD)
    mixed = jnp.einsum("gpq,bcqgd->bcpgd", ws, v) + bias.T[None, None, :, :, None]
    return u * mixed.reshape(bsz, seq, GMLP_W)


def setup_inputs(seed: int = 0) -> dict:
    key = jax.random.key(seed)
    ks = jax.random.split(key, 24)
    L = DEPTH

    def nrm(k, shape, scale):
        return jax.random.normal(k, shape, jnp.float32) * scale

    def gain(k, shape):
        return 1.0 + 0.02 * jax.random.normal(k, shape, jnp.float32)

    return {
        "x": jax.random.normal(ks[0], (BATCH, SEQ, D_MODEL), jnp.float32),
        "ffn1_pre_g": gain(ks[1], (L, D_MODEL)),
        "ffn1_w_gu": nrm(ks[2], (L, D_MODEL, 2 * D_FF), D_MODEL ** -0.5),
        "ffn1_w_down": nrm(ks[3], (L, D_FF, D_MODEL), D_FF ** -0.5),
        "ffn1_post_g": gain(ks[4], (L, D_MODEL)),
        "mix_pre_g": gain(ks[5], (L, D_MODEL)),
        "w_in": nrm(ks[6], (L, D_MODEL, N_IN), D_MODEL ** -0.5),
        "conv_w": nrm(ks[7], (L, CONV_K, CONV_W), CONV_K ** -0.5),
        "conv_b": nrm(ks[8], (L, CONV_W), 0.02),
        "q_norm_g": gain(ks[9], (L, Q_RANK)),
        "w_uq": nrm(ks[10], (L, Q_RANK, MLA_HEADS * QK_DIM), Q_RANK ** -0.5),
        "kv_norm_g": gain(ks[11], (L, KV_RANK)),
        "w_ukv": nrm(ks[12], (L, KV_RANK, MLA_HEADS * (NOPE_DIM + V_DIM)), KV_RANK ** -0.5),
        "gmlp_norm_g": gain(ks[13], (L, GMLP_W)),
        "gmlp_ws": nrm(ks[14], (L, GMLP_HEADS, CHUNK, CHUNK), CHUNK ** -0.5),
        "gmlp_b": gain(ks[15], (L, GMLP_HEADS, CHUNK)),
        "w_out": nrm(ks[16], (L, D_MIX, D_MODEL), D_MIX ** -0.5),
        "mix_post_g": gain(ks[17], (L, D_MODEL)),
        "ffn2_pre_g": gain(ks[18], (L, D_MODEL)),
        "ffn2_w_gu": nrm(ks[19], (L, D_MODEL, 2 * D_FF), D_MODEL ** -0.5),
        "ffn2_w_down": nrm(ks[20], (L, D_FF, D_MODEL), D_FF ** -0.5),
        "ffn2_post_g": gain(ks[21], (L, D_MODEL)),
    }


def reference(x, ffn1_pre_g, ffn1_w_gu, ffn1_w_down, ffn1_post_g, mix_pre_g, w_in,
              conv_w, conv_b, q_norm_g, w_uq, kv_norm_g, w_ukv, gmlp_norm_g, gmlp_ws,
              gmlp_b, w_out, mix_post_g, ffn2_pre_g, ffn2_w_gu, ffn2_w_down, ffn2_post_g):
    cos, sin = rope_tables(x.shape[1], ROPE_DIM)
    for l in range(DEPTH):
        h = swiglu_ffn(rms_norm(x, ffn1_pre_g[l]), ffn1_w_gu[l], ffn1_w_down[l])
        x = x + 0.5 * rms_norm(h, ffn1_post_g[l])

        n = rms_norm(x, mix_pre_g[l])
        z = n @ w_in[l]
        xc, gb, gc, cq, ckv, kr, zu, zv = jnp.split(z, SPLITS, axis=-1)
        y_conv = short_conv_mixer(xc, gb, gc, conv_w[l], conv_b[l])
        y_mla = mla_mixer(cq, ckv, kr, q_norm_g[l], w_uq[l], kv_norm_g[l], w_ukv[l], cos, sin)
        y_gmlp = gmlp_mixer(zu, zv, gmlp_norm_g[l], gmlp_ws[l], gmlp_b[l])
        y = jnp.concatenate([y_conv, y_mla, y_gmlp], axis=-1) @ w_out[l]
        x = x + rms_norm(y, mix_post_g[l])

        h = swiglu_ffn(rms_norm(x, ffn2_pre_g[l]), ffn2_w_gu[l], ffn2_w_down[l])
        x = x + 0.5 * rms_norm(h, ffn2_post_g[l])
    return x
```

```python
import contextlib
import math
import numpy as np
import concourse.bass as bass
import concourse.mybir as mybir
from concourse.bass_utils import run_bass_kernel_spmd

F32 = mybir.dt.float32
BF16 = mybir.dt.bfloat16
AF = mybir.ActivationFunctionType
ALU = mybir.AluOpType

D_MODEL = 1024
BATCH = 32
SEQ = 2048
DEPTH = 2
D_FF = 2816
EPS = 1e-6
NCORES = 8
NSEQ = BATCH // NCORES
KC = D_MODEL // 128
NJ = D_FF // 128
HEADS = 8
QK_DIM = 96
SCALE = 1.0 / math.sqrt(QK_DIM)
TG = 512
NTG = SEQ // TG

WIN_ITEMS = [8 * 256, 8 * 256, 8 * 256, 8 * 256, 8 * 256, 8 * 256, 8 * 224, 8 * 96]
WO_ITEM = 8 * 256
NWUQ = HEADS * 2 * 2 * 96
NMIXW = NWUQ + 512 + 512 + 512
NMIX = sum(WIN_ITEMS) + 4 * WO_ITEM + NMIXW
NPAR = 576
SEM_CAP = 16000
ENGINES = ("pe", "act", "dve", "pool", "sp")


class T:
    __slots__ = ("name", "w", "rs", "rdma", "sem", "cnt")

    def __init__(self, name):
        self.name = name
        self.w = None
        self.rs = {}
        self.rdma = []
        self.sem = None
        self.cnt = 0


class Op:
    __slots__ = ("eng", "fn", "deps", "signal", "sem", "val", "dma")

    def __init__(self, eng, fn, dma):
        self.eng = eng
        self.fn = fn
        self.deps = []
        self.signal = False
        self.sem = None
        self.val = None
        self.dma = dma


class Sched:
    def __init__(self, nc, stack):
        self.nc = nc
        self.stack = stack
        self.ops = {e: [] for e in ENGINES}
        self.nsem = 0
        self.eng_sems = {e: [] for e in ENGINES}
        self.last = {e: None for e in ENGINES}
        self.bar = []

    def new_sem(self, name):
        self.nsem += 1
        return self.stack.enter_context(self.nc.semaphore(f"s{self.nsem}_{name}"))

    def _dep(self, op, other):
        if other is None or other is op:
            return
        if other.eng == "pe" and op.eng == "pe" and not other.dma and not op.dma:
            return
        for d in op.deps:
            if d is other:
                return
        op.deps.append(other)
        other.signal = True

    def op(self, eng, fn, reads=(), writes=(), dma=False, dma_tile=None, nobarrier=False, after=()):
        o = Op(eng, fn, dma)
        if not nobarrier:
            for b in self.bar:
                self._dep(o, b)
        for a in after:
            self._dep(o, a)
        for t in reads:
            self._dep(o, t.w)
        for t in writes:
            self._dep(o, t.w)
            for r in t.rs.values():
                self._dep(o, r)
            for r in t.rdma:
                self._dep(o, r)
        for t in reads:
            if dma:
                t.rdma.append(o)
            else:
                t.rs[eng] = o
        for t in writes:
            t.w = o
            t.rs = {}
            t.rdma = []
        if dma:
            t = dma_tile
            if t.sem is None:
                t.sem = self.new_sem("d_" + t.name)
            t.cnt += 16
            o.sem = t.sem
            o.val = t.cnt
            o.signal = True
        else:
            self.last[eng] = o
        self.ops[eng].append(o)
        return o

    def barrier(self):
        self.bar = [self.last[e] for e in ("pe", "act", "dve") if self.last[e] is not None]

    def finalize(self):
        for e in ENGINES:
            n = 0
            for o in self.ops[e]:
                if o.dma or not o.signal:
                    continue
                k = n // SEM_CAP
                while len(self.eng_sems[e]) <= k:
                    self.eng_sems[e].append(self.new_sem(f"{e}{len(self.eng_sems[e])}"))
                o.sem = self.eng_sems[e][k]
                o.val = n % SEM_CAP + 1
                n += 1

    def emit(self, eng_name, eng):
        waited = {}
        for o in self.ops[eng_name]:
            for d in o.deps:
                key = d.sem.num
                if waited.get(key, 0) >= d.val:
                    continue
                eng.wait_ge(d.sem, d.val)
                waited[key] = d.val
            ins = o.fn(eng)
            if o.signal:
                ins.then_inc(o.sem, 16 if o.dma else 1)

    def run(self):
        self.finalize()
        with self.nc.Block() as block:
            @block.tensor
            def _(e):
                self.emit("pe", e)

            @block.scalar
            def _(e):
                self.emit("act", e)

            @block.vector
            def _(e):
                self.emit("dve", e)

            @block.gpsimd
            def _(e):
                self.emit("pool", e)

            @block.sync
            def _(e):
                self.emit("sp", e)


class Arena:
    def __init__(self, base_f32):
        self.base = base_f32
        self.off = 0

    def alloc(self, nbytes):
        o = self.off
        self.off += (nbytes + 63) // 64 * 64
        return o

    def view(self, off, dtype, shape):
        n = 1
        for s in shape:
            n *= s
        esz = 2 if dtype == BF16 else 4
        assert off % 4 == 0
        w0 = off // 4
        w1 = (off + n * esz + 3) // 4
        ap = self.base[:, w0:w1]
        if dtype == BF16:
            ap = ap.bitcast(BF16)
        ap = ap[:, 0:n]
        if len(shape) == 2:
            ap = ap.rearrange("p (a b) -> p a b", a=shape[0], b=shape[1])
        elif len(shape) == 3:
            ap = ap.rearrange("p (a b c) -> p a b c", a=shape[0], b=shape[1], c=shape[2])
        elif len(shape) == 4:
            ap = ap.rearrange("p (a b c d) -> p a b c d", a=shape[0], b=shape[1], c=shape[2], d=shape[3])
        return ap


class Ring:
    def __init__(self, S, name, views, items):
        self.S = S
        self.views = views
        self.items = items
        self.n = len(views)
        self.ts = [T(f"{name}{i}") for i in range(self.n)]
        self.k = 0
        self.issued = 0
        self.rel = 0

    def _issue(self, i):
        slot = i % self.n
        src, n = self.items[i]
        dst = self.views[slot][:, 0:n]
        self.S.op("pool", lambda e: e.dma_start(out=dst, in_=src), writes=[self.ts[slot]],
                  dma=True, dma_tile=self.ts[slot], nobarrier=True)

    def _fill(self):
        lim = min(len(self.items), self.rel + self.n)
        while self.issued < lim:
            self._issue(self.issued)
            self.issued += 1

    def get(self):
        self._fill()
        i = self.k
        assert i < self.issued, "ring over-subscribed"
        self.k += 1
        slot = i % self.n
        return self.views[slot], self.ts[slot]

    def done(self, n=1):
        self.rel += n
        assert self.rel <= self.k
        self._fill()


FULL_PLAN = [(0, "ffn1"), (0, "mix"), (0, "ffn2"), (1, "ffn1"), (1, "mix"), (1, "ffn2")]


def build_program(nseq=NSEQ, plan=FULL_PLAN, NG=4, ND=2, dump_yc=False):
    nc = bass.Bass("TRN2", target_bir_lowering=False)
    L = DEPTH
    xT = nc.dram_tensor("xT", [nseq, 128, KC * SEQ], F32, kind="ExternalInput").ap()
    wgu = nc.dram_tensor("wgu", [L * 2 * NJ, 128, 2048], F32, kind="ExternalInput").ap()
    wdn = nc.dram_tensor("wdn", [L * 2 * KC, 128, NJ * 128], F32, kind="ExternalInput").ap()
    wmix = nc.dram_tensor("wmix", [L, 128, NMIX], F32, kind="ExternalInput").ap()
    par = nc.dram_tensor("par", [L, 128, NPAR], F32, kind="ExternalInput").ap()
    cst = nc.dram_tensor("cst", [128, 128], F32, kind="ExternalInput").ap()
    rope = nc.dram_tensor("rope", [128, 2 * SEQ], F32, kind="ExternalInput").ap()
    yT = nc.dram_tensor("yT", [nseq, 128, KC * SEQ], F32, kind="ExternalOutput").ap()

    with contextlib.ExitStack() as st:
        S = Sched(nc, st)
        ARENA_BYTES = 211968
        arena_t = st.enter_context(nc.sbuf_tensor("arena", [128, ARENA_BYTES // 4], F32))
        A = Arena(arena_t[:])
        PS = [st.enter_context(nc.psum_tensor(f"ps{i}", [128, 1024], F32)) for i in range(4)]
        PST = [T(f"bank{b}") for b in range(8)]

        def bank(b):
            return PS[b // 2][:, (b % 2) * 512:(b % 2) * 512 + 512]

        o_x = A.alloc(KC * SEQ * 4)
        X = A.view(o_x, F32, [KC, SEQ])
        XT = [[T(f"x{c}_{g}") for g in range(NTG)] for c in range(KC)]
        XLD = [T(f"xld{c}") for c in range(KC)]
        XST = [T(f"xst{c}") for c in range(KC)]
        o_par = A.alloc(L * NPAR * 4)
        PAR = A.view(o_par, F32, [L, NPAR])
        PART = T("par")
        o_id = A.alloc(128 * 4)
        IDENT = A.view(o_id, F32, [128])
        o_ones = A.alloc(128 * 2)
        ONES = A.view(o_ones, BF16, [128])
        o_mw = ARENA_BYTES - NMIXW * 2
        o_rope = o_mw - 2 * SEQ * 2
        MIX_LIMIT = o_rope
        ROPE = A.view(o_rope, BF16, [2, SEQ])
        CSTT = T("cst")
        ROPET = T("rope")
        o_rg = [A.alloc(4096) for _ in range(NG)]
        o_rd = [A.alloc(NJ * 128 * 2) for _ in range(ND)]
        MIXW = A.view(o_mw, BF16, [NMIXW])
        MWT = [T("wuq"), T("wk"), T("wv"), T("wst")]
        WUQ = MIXW[:, 0:NWUQ].rearrange("p (h a k c) -> p h a k c", h=HEADS, a=2, k=2, c=96)
        WK = MIXW[:, NWUQ:NWUQ + 512]
        WV = MIXW[:, NWUQ + 512:NWUQ + 1024]
        WST = MIXW[:, NWUQ + 1024:NWUQ + 1536].rearrange("p (g q) -> p g q", g=4, q=128)
        phase_base = A.off
        PHASE_BYTES = ARENA_BYTES - phase_base

        g_items, d_items = [], []
        for s in range(nseq):
            for (l, stg) in plan:
                if stg in ("ffn1", "ffn2"):
                    i = 0 if stg == "ffn1" else 1
                    for hf in range(2):
                        for j in range(NJ):
                            g_items.append((wgu[(l * 2 + i) * NJ + j], 2048))
                        for m in range(KC):
                            d_items.append((wdn[(l * 2 + i) * KC + m], NJ * 128))
                else:
                    off = 0
                    for n in WIN_ITEMS + [WO_ITEM] * 4:
                        g_items.append((wmix[l][:, off:off + n], n))
                        off += n
        RG = Ring(S, "rg", [A.view(o, BF16, [2048]) for o in o_rg], g_items)
        RD = Ring(S, "rd", [A.view(o, BF16, [NJ * 128]) for o in o_rd], d_items)

        S.op("sp", lambda e: e.dma_start(out=PAR, in_=par.rearrange("l p n -> p l n")), writes=[PART],
             dma=True, dma_tile=PART)
        S.op("sp", lambda e: e.dma_start(out=IDENT, in_=cst), writes=[CSTT], dma=True, dma_tile=CSTT)
        ONEST = T("ones")
        S.op("dve", lambda e: e.memset(ONES, 1.0), writes=[ONEST])
        COS = ROPE[:, 0, :]
        SIN = ROPE[:, 1, :]

        def pcol(l, c):
            return PAR[:, l, c:c + 1]

        state = {"sq": 0}

        def rstd_from_stats(stat_b, dst, dst_t, n, dim):
            src = bank(stat_b)[:, 0:n]
            S.op("act", lambda e: e.activation(out=dst, in_=src, func=AF.Ln, bias=EPSC, scale=1.0 / dim),
                 reads=[PST[stat_b], CSTT2], writes=[dst_t])
            S.op("act", lambda e: e.activation(out=dst, in_=dst, func=AF.Exp, scale=-0.5),
                 reads=[dst_t], writes=[dst_t])

        o_eps = A.alloc(64)
        phase_base = A.off
        PHASE_BYTES = ARENA_BYTES - phase_base
        EPSC = A.view(o_eps, F32, [1])
        CSTT2 = T("eps")
        S.op("dve", lambda e: e.memset(EPSC, EPS), writes=[CSTT2])

        def prenorm(l, gcol, t0, ntg, XN, XNT, RSTD, RSTDT, SQ, SQT):
            for tg in range(ntg):
                T0 = t0 + tg * TG
                gtg = T0 // TG
                sb = 6 + (tg % 2)
                for c in range(KC):
                    q = state["sq"] % len(SQ)
                    state["sq"] += 1
                    sqv, sqt = SQ[q], SQT[q]
                    xin = X[:, c, T0:T0 + TG]
                    S.op("act", lambda e, sqv=sqv, xin=xin: e.activation(out=sqv, in_=xin, func=AF.Square),
                         reads=[XT[c][gtg]], writes=[sqt])
                    S.op("pe", lambda e, sb=sb, sqv=sqv, c=c: e.matmul(bank(sb), lhsT=ONES, rhs=sqv,
                                                                         start=(c == 0), stop=(c == KC - 1)),
                         reads=[sqt, ONEST], writes=[PST[sb]])
                rs = RSTD[:, tg * TG:(tg + 1) * TG]
                rstd_from_stats(sb, rs, RSTDT[tg], TG, D_MODEL)
                for c in range(KC):
                    xin = X[:, c, T0:T0 + TG]
                    xo = XN[:, c, tg * TG:(tg + 1) * TG]
                    gc = pcol(l, gcol + c)
                    S.op("dve", lambda e, xo=xo, xin=xin, gc=gc, rs=rs: e.scalar_tensor_tensor(
                        out=xo, in0=xin, scalar=gc, in1=rs, op0=ALU.mult, op1=ALU.mult),
                        reads=[XT[c][gtg], RSTDT[tg], PART], writes=[XNT[tg]])

        def postnorm_apply(l, gcol, T0, DSB, DSBT, stat_b, R2, R2T, half):
            gtg = T0 // TG
            rstd_from_stats(stat_b, R2, R2T, TG, D_MODEL)
            for m in range(KC):
                dv = DSB[:, m, :]
                gc = pcol(l, gcol + m)
                S.op("dve", lambda e, dv=dv, gc=gc: e.scalar_tensor_tensor(
                    out=dv, in0=dv, scalar=gc, in1=R2, op0=ALU.mult, op1=ALU.mult),
                    reads=[R2T, PART] + DSBT, writes=DSBT)
                xv = X[:, m, T0:T0 + TG]
                S.op("dve", lambda e, dv=dv, xv=xv: e.scalar_tensor_tensor(
                    out=xv, in0=dv, scalar=half, in1=xv, op0=ALU.mult, op1=ALU.add),
                    reads=DSBT + [XT[m][gtg]], writes=[XT[m][gtg]])

        def ffn_chain(stages):
            from collections import deque
            S.barrier()
            A.off = phase_base
            o_xn = A.alloc(KC * 1024 * 2)
            o_h = A.alloc(NJ * 1024 * 2)
            o_sq = [A.alloc(TG * 2) for _ in range(4)]
            o_rstd = A.alloc(1024 * 4)
            o_r2 = [A.alloc(TG * 4) for _ in range(2)]
            o_st = [A.alloc(TG * 2) for _ in range(2)]
            o_dsb = [A.alloc(KC * TG * 4) for _ in range(2)]
            assert A.off <= ARENA_BYTES, A.off
            XN = A.view(o_xn, BF16, [KC, 1024])
            XNT = [T("xn0"), T("xn1")]
            H = A.view(o_h, BF16, [NJ, 1024])
            HT = [[T(f"h{j}_{g}") for g in range(2)] for j in range(NJ)]
            SQ = [A.view(o, BF16, [TG]) for o in o_sq]
            SQT = [T(f"sq{q}") for q in range(4)]
            RSTD = A.view(o_rstd, F32, [1024])
            RSTDT = [T("rstd0"), T("rstd1")]
            R2 = [A.view(o, F32, [TG]) for o in o_r2]
            R2T = [T("r2_0"), T("r2_1")]
            ST = [A.view(o, BF16, [TG]) for o in o_st]
            STT = [T("st0"), T("st1")]
            DSB = [A.view(o, F32, [KC, TG]) for o in o_dsb]
            DSBT = [[T("dsb0")], [T("dsb1")]]
            blocks = [(l, i, hf) for (l, i) in stages for hf in range(2)]
            nb_ = len(blocks)
            bg = deque()
            cnt = {"gu": 0, "d": 0}

            def drain(n):
                for _ in range(n):
                    if not bg:
                        return
                    bg.popleft()()

            def pre_stats(k):
                l, i, hf = blocks[k]
                t0 = hf * 1024
                out = []
                for tg in range(2):
                    T0 = t0 + tg * TG
                    gtg = T0 // TG
                    sb = 6 + tg
                    for c in range(KC):
                        def f(c=c, T0=T0, gtg=gtg, sb=sb):
                            q = state["sq"] % 4
                            state["sq"] += 1
                            sqv, sqt = SQ[q], SQT[q]
                            xin = X[:, c, T0:T0 + TG]
                            S.op("act", lambda e: e.activation(out=sqv, in_=xin, func=AF.Square),
                                 reads=[XT[c][gtg]], writes=[sqt])
                            S.op("pe", lambda e: e.matmul(bank(sb), lhsT=ONES, rhs=sqv, start=(c == 0), stop=(c == KC - 1)),
                                 reads=[sqt, ONEST], writes=[PST[sb]])
                        out.append(f)

                    def g(tg=tg, sb=sb):
                        rstd_from_stats(sb, RSTD[:, tg * TG:(tg + 1) * TG], RSTDT[tg], TG, D_MODEL)
                    out.append(g)
                return out

            def pre_xn(k):
                l, i, hf = blocks[k]
                gpre = 0 if i == 0 else 32
                t0 = hf * 1024
                for tg in range(2):
                    T0 = t0 + tg * TG
                    gtg = T0 // TG
                    rs = RSTD[:, tg * TG:(tg + 1) * TG]
                    for c in range(KC):
                        xin = X[:, c, T0:T0 + TG]
                        xo = XN[:, c, tg * TG:(tg + 1) * TG]
                        gc = pcol(l, gpre + c)
                        S.op("dve", lambda e, xo=xo, xin=xin, gc=gc, rs=rs: e.scalar_tensor_tensor(
                            out=xo, in0=xin, scalar=gc, in1=rs, op0=ALU.mult, op1=ALU.mult),
                            reads=[XT[c][gtg], RSTDT[tg], PART], writes=[XNT[tg]])

            def post_apply(k):
                l, i, hf = blocks[k]
                gpost = 8 if i == 0 else 40
                out = []
                for tg in range(2):
                    T0 = hf * 1024 + tg * TG
                    gtg = T0 // TG

                    def g(tg=tg):
                        rstd_from_stats(6 + tg, R2[tg], R2T[tg], TG, D_MODEL)
                    out.append(g)
                    for m in range(KC):
                        def f(m=m, tg=tg, T0=T0, gtg=gtg):
                            dv = DSB[tg][:, m, :]
                            gc = pcol(l, gpost + m)
                            S.op("dve", lambda e: e.scalar_tensor_tensor(
                                out=dv, in0=dv, scalar=gc, in1=R2[tg], op0=ALU.mult, op1=ALU.mult),
                                reads=[R2T[tg], PART] + DSBT[tg], writes=DSBT[tg])
                            xv = X[:, m, T0:T0 + TG]
                            S.op("dve", lambda e: e.scalar_tensor_tensor(
                                out=xv, in0=dv, scalar=0.5, in1=xv, op0=ALU.mult, op1=ALU.add),
                                reads=DSBT[tg] + [XT[m][gtg]], writes=[XT[m][gtg]])
                        out.append(f)
                return out

            for f in pre_stats(0):
                f()
            pre_xn(0)
            for k in range(nb_):
                l, i, hf = blocks[k]
                if k + 1 < nb_:
                    bg.extend(pre_stats(k + 1))
                for j in range(NJ):
                    wv, wt = RG.get()
                    W = wv.rearrange("p (a k m) -> p a k m", a=2, k=KC, m=128)
                    for tg in range(2):
                        pair = cnt["gu"] % 2
                        cnt["gu"] += 1
                        bg_, bu = 2 * pair, 2 * pair + 1
                        for a_, b in ((0, bg_), (1, bu)):
                            for kc in range(KC):
                                S.op("pe", lambda e, a_=a_, b=b, kc=kc, W=W, tg=tg: e.matmul(
                                    bank(b), lhsT=W[:, a_, kc, :], rhs=XN[:, kc, tg * TG:(tg + 1) * TG],
                                    start=(kc == 0), stop=(kc == KC - 1)),
                                    reads=[wt, XNT[tg]], writes=[PST[b]])
                        sv, stt = ST[pair], STT[pair]
                        S.op("act", lambda e, sv=sv, bg_=bg_: e.activation(out=sv, in_=bank(bg_), func=AF.Silu),
                             reads=[PST[bg_]], writes=[stt])
                        ho = H[:, j, tg * TG:(tg + 1) * TG]
                        S.op("dve", lambda e, ho=ho, sv=sv, bu=bu: e.tensor_tensor(
                            out=ho, in0=bank(bu), in1=sv, op=ALU.mult),
                            reads=[PST[bu], stt], writes=[HT[j][tg]])
                        drain(2)
                    RG.done()
                drain(10 ** 6)
                if k + 1 < nb_:
                    pre_xn(k + 1)
                pend = []
                for m in range(KC):
                    wv, wt = RD.get()
                    Wd = wv.rearrange("p (j c) -> p j c", j=NJ, c=128)
                    for tg in range(2):
                        b = 4 + cnt["d"] % 2
                        cnt["d"] += 1
                        for j in range(NJ):
                            S.op("pe", lambda e, b=b, j=j, Wd=Wd, tg=tg: e.matmul(
                                bank(b), lhsT=Wd[:, j, :], rhs=H[:, j, tg * TG:(tg + 1) * TG],
                                start=(j == 0), stop=(j == NJ - 1)),
                                reads=[wt, HT[j][tg]], writes=[PST[b]])
                        dv = DSB[tg][:, m, :]
                        S.op("act", lambda e, dv=dv, b=b: e.activation(out=dv, in_=bank(b), func=AF.Copy),
                             reads=[PST[b]], writes=DSBT[tg])
                        q = state["sq"] % 4
                        state["sq"] += 1
                        sqv, sqt = SQ[q], SQT[q]
                        S.op("act", lambda e, sqv=sqv, b=b: e.activation(out=sqv, in_=bank(b), func=AF.Square),
                             reads=[PST[b]], writes=[sqt])
                        for f in pend:
                            f()
                        pend = []
                        sb = 6 + tg

                        def mk(sb=sb, sqv=sqv, sqt=sqt, m=m):
                            S.op("pe", lambda e: e.matmul(bank(sb), lhsT=ONES, rhs=sqv,
                                                          start=(m == 0), stop=(m == KC - 1)),
                                 reads=[sqt, ONEST], writes=[PST[sb]])
                        pend.append(mk)
                    RD.done()
                for f in pend:
                    f()
                bg.extend(post_apply(k))
            drain(10 ** 6)

        def load_mixw(l):
            base = sum(WIN_ITEMS) + 4 * WO_ITEM
            segs = [(0, NWUQ), (NWUQ, 512), (NWUQ + 512, 512), (NWUQ + 1024, 512)]
            for (o, n), t in zip(segs, MWT):
                S.op("pool", lambda e, o=o, n=n: e.dma_start(out=MIXW[:, o:o + n], in_=wmix[l][:, base + o:base + o + n]),
                     writes=[t], dma=True, dma_tile=t)
            S.op("pool", lambda e: e.dma_start(out=ROPE, in_=rope.rearrange("p (a t) -> p a t", a=2)),
                 writes=[ROPET], dma=True, dma_tile=ROPET)

        def mix_stage(l):
            S.barrier()
            load_mixw(l)
            A.off = phase_base
            o_xn = A.alloc(32768)
            o_yc = A.alloc(32768)
            o_tmp = A.off
            XN = A.view(o_xn, BF16, [KC, SEQ])
            XNT = [T(f"mxn{g}") for g in range(NTG)]
            YC = A.view(o_yc, BF16, [KC, SEQ])
            YCT = [[T(f"yc{c}_{g}") for g in range(NTG)] for c in range(KC)]
            o_rstd = A.alloc(SEQ * 4)
            o_sq = [A.alloc(TG * 2) for _ in range(4)]
            RSTD = A.view(o_rstd, F32, [SEQ])
            RSTDT = [T(f"mrstd{g}") for g in range(NTG)]
            SQ = [A.view(o, BF16, [TG]) for o in o_sq]
            SQT = [T(f"msq{q}") for q in range(4)]
            prenorm(l, 16, 0, NTG, XN, XNT, RSTD, RSTDT, SQ, SQT)
            o_xcs = [A.alloc(TG * 4) for _ in range(2)]
            o_zc = A.alloc((SEQ + 2) * 2)
            o_gb = A.alloc(SEQ * 2)
            o_dg = A.alloc(6 * 128 * 2)
            assert A.off <= MIX_LIMIT
            XCS = [A.view(o, F32, [TG]) for o in o_xcs]
            XCST = [T("xcs0"), T("xcs1")]
            ZC = A.view(o_zc, BF16, [SEQ + 2])
            ZCT = [T(f"zc{g}") for g in range(NTG)]
            ZPAD = T("zpad")
            GB = A.view(o_gb, BF16, [SEQ])
            GBT = [T(f"gb{g}") for g in range(NTG)]
            DG = A.view(o_dg, BF16, [6, 128])
            DGT = T("diag")
            for q in range(6):
                S.op("dve", lambda e, q=q: e.tensor_scalar(out=DG[:, q, :], in0=IDENT, scalar1=pcol(l, 48 + q),
                                                           scalar2=None, op0=ALU.mult),
                     reads=[CSTT, PART], writes=[DGT])
            S.op("dve", lambda e: e.memset(ZC[:, 0:1], 0.0), writes=[ZPAD])
            S.op("dve", lambda e: e.memset(ZC[:, SEQ + 1:SEQ + 2], 0.0), writes=[ZPAD])
            items = [RG.get() for _ in range(3)]
            Wi = [(v.rearrange("p (k c) -> p k c", k=KC, c=256), t) for (v, t) in items]
            sel = {0: ((0, 0), (0, 128), (1, 0)), 1: ((1, 128), (2, 0), (2, 128))}
            cnt = 0
            for cc in range(2):
                for tg in range(NTG):
                    bset = (cnt % 2) * 3
                    cnt += 1
                    for r_, (it, co) in enumerate(sel[cc]):
                        W, wt = Wi[it]
                        b = bset + r_
                        for kc in range(KC):
                            S.op("pe", lambda e, b=b, W=W, co=co, kc=kc, tg=tg: e.matmul(
                                bank(b), lhsT=W[:, kc, co:co + 128], rhs=XN[:, kc, tg * TG:(tg + 1) * TG],
                                start=(kc == 0), stop=(kc == KC - 1)),
                                reads=[wt, XNT[tg]], writes=[PST[b]])
                    xs, xst = XCS[tg % 2], XCST[tg % 2]
                    S.op("act", lambda e, xs=xs, b=bset: e.activation(out=xs, in_=bank(b), func=AF.Copy),
                         reads=[PST[bset]], writes=[xst])
                    zo = ZC[:, 1 + tg * TG:1 + (tg + 1) * TG]
                    S.op("dve", lambda e, zo=zo, xs=xs, b=bset + 1: e.tensor_tensor(out=zo, in0=bank(b), in1=xs, op=ALU.mult),
                         reads=[PST[bset + 1], xst], writes=[ZCT[tg]])
                    go = GB[:, tg * TG:(tg + 1) * TG]
                    S.op("act", lambda e, go=go, b=bset + 2: e.activation(out=go, in_=bank(b), func=AF.Copy),
                         reads=[PST[bset + 2]], writes=[GBT[tg]])
                for tg in range(NTG):
                    b = 6 + tg % 2
                    rd = [ZCT[g] for g in (tg - 1, tg, tg + 1) if 0 <= g < NTG] + [ZPAD, DGT]
                    for k in range(3):
                        S.op("pe", lambda e, b=b, k=k, cc=cc, tg=tg: e.matmul(
                            bank(b), lhsT=DG[:, cc * 3 + k, :], rhs=ZC[:, tg * TG + k:tg * TG + k + TG],
                            start=(k == 0), stop=(k == 2)),
                            reads=rd, writes=[PST[b]])
                    yo = YC[:, cc, tg * TG:(tg + 1) * TG]
                    go = GB[:, tg * TG:(tg + 1) * TG]
                    S.op("dve", lambda e, yo=yo, go=go, b=b, cc=cc: e.scalar_tensor_tensor(
                        out=yo, in0=bank(b), scalar=pcol(l, 54 + cc), in1=go, op0=ALU.add, op1=ALU.mult),
                        reads=[PST[b], GBT[tg], PART], writes=[YCT[cc][tg]])
            RG.done(3)
            S.barrier()
            A.off = o_tmp
            o_gv = A.alloc(16 * 256 * 2)
            o_vt = A.alloc(16 * 256 * 2)
            o_ut = [A.alloc(TG * 2) for _ in range(2)]
            o_ss = A.alloc(64)
            o_rv = A.alloc(64)
            o_jk = A.alloc(256 * 2)
            o_tm = [A.alloc(TG * 4) for _ in range(2)]
            assert A.off <= MIX_LIMIT
            GV = A.view(o_gv, BF16, [16, 256])
            GVT = [T(f"gv{i}") for i in range(16)]
            VT_ = A.view(o_vt, BF16, [16, 256])
            VTT = [T(f"vt{i}") for i in range(16)]
            UT = [A.view(o, BF16, [TG]) for o in o_ut]
            UTT = [T("ut0"), T("ut1")]
            SS = A.view(o_ss, F32, [16])
            SST = T("ss")
            RV = A.view(o_rv, F32, [16])
            RVT = T("rv")
            JK = A.view(o_jk, BF16, [256])
            JKT = T("jk")
            TM = [A.view(o, F32, [TG]) for o in o_tm]
            TMT = [T("tm0"), T("tm1")]
            (i3v, i3t) = RG.get()
            (i4v, i4t) = RG.get()
            W3 = i3v.rearrange("p (k c) -> p k c", k=KC, c=256)
            W4 = i4v.rearrange("p (k c) -> p k c", k=KC, c=256)
            for tt in range(16):
                b = 4 + tt % 2
                for kc in range(KC):
                    S.op("pe", lambda e, b=b, kc=kc, tt=tt: e.matmul(
                        bank(b)[:, 0:256], lhsT=XN[:, kc, tt * 128:(tt + 1) * 128], rhs=W4[:, kc, :],
                        start=(kc == 0), stop=(kc == KC - 1)),
                        reads=[i4t, XNT[tt // 4]], writes=[PST[b]])
                S.op("act", lambda e, b=b, tt=tt: e.activation(out=GV[:, tt, :], in_=bank(b)[:, 0:256], func=AF.Gelu),
                     reads=[PST[b]], writes=[GVT[tt]])
                S.op("act", lambda e, tt=tt: e.activation(out=JK, in_=GV[:, tt, :], func=AF.Square,
                                                          accum_out=SS[:, tt:tt + 1]),
                     reads=[GVT[tt]], writes=[JKT, SST])
            S.op("act", lambda e: e.activation(out=RV, in_=SS, func=AF.Ln, bias=EPSC, scale=1.0 / 256),
                 reads=[SST, CSTT2], writes=[RVT])
            S.op("act", lambda e: e.activation(out=RV, in_=RV, func=AF.Exp, scale=-0.5), reads=[RVT], writes=[RVT])
            GN = PAR[:, l, 64:320]
            for tt in range(16):
                S.op("dve", lambda e, tt=tt: e.scalar_tensor_tensor(
                    out=VT_[:, tt, :], in0=GV[:, tt, :], scalar=RV[:, tt:tt + 1], in1=GN, op0=ALU.mult, op1=ALU.mult),
                    reads=[GVT[tt], RVT, PART], writes=[VTT[tt]])
            cnt = 0
            for fc in range(2):
                for tg in range(NTG):
                    bm = 6 + cnt % 2
                    bz = cnt % 4
                    sl = cnt % 2
                    cnt += 1
                    for ch in range(4):
                        c16 = tg * 4 + ch
                        for gi in range(2):
                            g = 2 * fc + gi
                            S.op("pe", lambda e, bm=bm, gi=gi, ch=ch, c16=c16, g=g: e.matmul(
                                bank(bm)[gi * 64:(gi + 1) * 64, ch * 128:(ch + 1) * 128],
                                lhsT=VT_[:, c16, g * 64:(g + 1) * 64], rhs=WST[:, g, :], start=True, stop=True),
                                reads=[VTT[c16], MWT[3]], writes=[PST[bm]])
                    for kc in range(KC):
                        S.op("pe", lambda e, bz=bz, kc=kc, fc=fc, tg=tg: e.matmul(
                            bank(bz), lhsT=W3[:, kc, fc * 128:(fc + 1) * 128], rhs=XN[:, kc, tg * TG:(tg + 1) * TG],
                            start=(kc == 0), stop=(kc == KC - 1)),
                            reads=[i3t, XNT[tg]], writes=[PST[bz]])
                    S.op("act", lambda e, bz=bz, sl=sl: e.activation(out=UT[sl], in_=bank(bz), func=AF.Gelu),
                         reads=[PST[bz]], writes=[UTT[sl]])
                    for ch in range(4):
                        S.op("dve", lambda e, bm=bm, sl=sl, ch=ch, fc=fc: e.tensor_tensor(
                            out=TM[sl][:, ch * 128:(ch + 1) * 128], in0=bank(bm)[:, ch * 128:(ch + 1) * 128],
                            in1=PAR[:, l, 320 + fc * 128:320 + (fc + 1) * 128], op=ALU.add),
                            reads=[PST[bm], PART], writes=[TMT[sl]])
                    yo = YC[:, 6 + fc, tg * TG:(tg + 1) * TG]
                    S.op("dve", lambda e, yo=yo, sl=sl: e.tensor_tensor(out=yo, in0=TM[sl], in1=UT[sl], op=ALU.mult),
                         reads=[TMT[sl], UTT[sl]], writes=[YCT[6 + fc][tg]])
            RG.done(2)
            S.barrier()
            A.off = o_tmp
            o_cqn = A.alloc(2 * SEQ * 2)
            o_ckvn = A.alloc(SEQ * 2)
            o_kro = A.alloc(SEQ * 2)
            o_rq = A.alloc(TG * 4)
            o_rkv = A.alloc(TG * 4)
            o_t1 = A.alloc(TG * 4)
            o_t2 = A.alloc(TG * 4)
            o_sq2 = [A.alloc(TG * 2) for _ in range(3)]
            assert A.off <= MIX_LIMIT, A.off
            CQN = A.view(o_cqn, BF16, [2, SEQ])
            CQNT = [T(f"cqn{g}") for g in range(NTG)]
            CKVN = A.view(o_ckvn, BF16, [SEQ])
            CKVNT = [T(f"ckvn{g}") for g in range(NTG)]
            KRO = A.view(o_kro, BF16, [SEQ])
            KROT = [T(f"kro{g}") for g in range(NTG)]
            RQ = A.view(o_rq, F32, [TG])
            RQT = T("rq")
            RKV = A.view(o_rkv, F32, [TG])
            RKVT = T("rkv")
            T1 = A.view(o_t1, F32, [TG])
            T1T = T("t1")
            T2 = A.view(o_t2, F32, [TG])
            T2T = T("t2")
            SQ2 = [A.view(o, BF16, [TG]) for o in o_sq2]
            SQ2T = [T(f"sq2_{q}") for q in range(3)]
            (i5v, i5t) = RG.get()
            (i6v, i6t) = RG.get()
            (i7v, i7t) = RG.get()
            W5 = i5v.rearrange("p (k c) -> p k c", k=KC, c=256)
            W6 = i6v[:, 0:KC * 224].rearrange("p (k c) -> p k c", k=KC, c=224)
            W7 = i7v[:, 0:KC * 96].rearrange("p (k c) -> p k c", k=KC, c=96)
            for tg in range(NTG):
                tsl = slice(tg * TG, (tg + 1) * TG)
                projs = [(0, W5, i5t, 0, 128), (1, W5, i5t, 128, 128), (2, W6, i6t, 0, 128),
                         (3, W6, i6t, 128, 96), (4, W7, i7t, 0, 96)]
                for (b, W, wt, co, mw) in projs:
                    for kc in range(KC):
                        S.op("pe", lambda e, b=b, W=W, co=co, mw=mw, kc=kc, tsl=tsl: e.matmul(
                            bank(b)[0:mw, :], lhsT=W[:, kc, co:co + mw], rhs=XN[:, kc, tsl],
                            start=(kc == 0), stop=(kc == KC - 1)),
                            reads=[wt, XNT[tg]], writes=[PST[b]])
                for c in range(3):
                    S.op("act", lambda e, c=c: e.activation(out=SQ2[c], in_=bank(c), func=AF.Square),
                         reads=[PST[c]], writes=[SQ2T[c]])
                for c in range(2):
                    S.op("pe", lambda e, c=c: e.matmul(bank(6), lhsT=ONES, rhs=SQ2[c], start=(c == 0), stop=(c == 1)),
                         reads=[SQ2T[c], ONEST], writes=[PST[6]])
                S.op("pe", lambda e: e.matmul(bank(7), lhsT=ONES, rhs=SQ2[2], start=True, stop=True),
                     reads=[SQ2T[2], ONEST], writes=[PST[7]])
                rstd_from_stats(6, RQ, RQT, TG, 256)
                rstd_from_stats(7, RKV, RKVT, TG, 128)
                for c in range(2):
                    S.op("dve", lambda e, c=c, tsl=tsl: e.scalar_tensor_tensor(
                        out=CQN[:, c, tsl], in0=bank(c), scalar=pcol(l, 56 + c), in1=RQ, op0=ALU.mult, op1=ALU.mult),
                        reads=[PST[c], RQT, PART], writes=[CQNT[tg]])
                S.op("dve", lambda e, tsl=tsl: e.scalar_tensor_tensor(
                    out=CKVN[:, tsl], in0=bank(2), scalar=pcol(l, 58), in1=RKV, op0=ALU.mult, op1=ALU.mult),
                    reads=[PST[2], RKVT, PART], writes=[CKVNT[tg]])
                S.op("dve", lambda e, tsl=tsl: e.tensor_tensor(out=T1[64:96, :], in0=bank(3)[64:96, :], in1=COS[64:96, tsl],
                                                               op=ALU.mult),
                     reads=[PST[3], ROPET], writes=[T1T])
                S.op("dve", lambda e, tsl=tsl: e.tensor_tensor(out=T2[64:96, :], in0=bank(4)[64:96, :], in1=SIN[64:96, tsl],
                                                               op=ALU.mult),
                     reads=[PST[4], ROPET], writes=[T2T])
                S.op("dve", lambda e, tsl=tsl: e.tensor_tensor(out=KRO[64:96, tsl], in0=T1[64:96, :], in1=T2[64:96, :],
                                                               op=ALU.add),
                     reads=[T1T, T2T], writes=[KROT[tg]])
            RG.done(3)
            S.barrier()
            A.off = o_xn
            o_va = [A.alloc(16 * 128 * 2) for _ in range(2)]
            o_qh = [A.alloc(SEQ * 2) for _ in range(2)]
            o_kh = [A.alloc(SEQ * 2) for _ in range(2)]
            o_pt = [A.alloc(1024 * 2) for _ in range(3)]
            o_r = A.alloc(TG * 4)
            assert A.off <= o_yc, A.off
            VA = [A.view(o, BF16, [16, 128]) for o in o_va]
            VAT = [T("va0"), T("va1")]
            QH = [A.view(o, BF16, [SEQ]) for o in o_qh]
            QHT = [[T(f"qh{p}_{g}") for g in range(NTG)] for p in range(2)]
            KH = [A.view(o, BF16, [SEQ]) for o in o_kh]
            KHT = [[T(f"kh{p}_{g}") for g in range(NTG)] for p in range(2)]
            PTB = [A.view(o, BF16, [1024]) for o in o_pt]
            PTT = [T(f"pt{i}") for i in range(3)]
            R = A.view(o_r, F32, [TG])
            RT = T("r")
            S.op("dve", lambda e: e.memset(VA[0][:, :, 64:128], 1.0), writes=[VAT[0]])
            S.op("dve", lambda e: e.memset(VA[1][:, :, 0:64], 1.0), writes=[VAT[1]])
            pcnt = {"s": 0, "o": 0}

            def head_prep(h):
                par_ = h % 2
                out = []
                for tg in range(NTG):
                    tsl = slice(tg * TG, (tg + 1) * TG)

                    def f_a(tg=tg, tsl=tsl):
                        for kc in range(2):
                            S.op("pe", lambda e, kc=kc: e.matmul(
                                bank(4)[0:96, :], lhsT=WUQ[:, h, 0, kc, :], rhs=CQN[:, kc, tsl],
                                start=(kc == 0), stop=(kc == 1)),
                                reads=[MWT[0], CQNT[tg]], writes=[PST[4]])
                        S.op("dve", lambda e: e.tensor_copy(out=QH[par_][0:64, tsl], in_=bank(4)[0:64, :]),
                             reads=[PST[4]], writes=[QHT[par_][tg]])
                        S.op("dve", lambda e: e.tensor_tensor(out=T1[64:96, :], in0=bank(4)[64:96, :],
                                                              in1=COS[64:96, tsl], op=ALU.mult),
                             reads=[PST[4], ROPET], writes=[T1T])

                    def f_b(tg=tg, tsl=tsl):
                        for kc in range(2):
                            S.op("pe", lambda e, kc=kc: e.matmul(
                                bank(5)[0:96, :], lhsT=WUQ[:, h, 1, kc, :], rhs=CQN[:, kc, tsl],
                                start=(kc == 0), stop=(kc == 1)),
                                reads=[MWT[0], CQNT[tg]], writes=[PST[5]])
                        S.op("dve", lambda e: e.tensor_tensor(out=T2[64:96, :], in0=bank(5)[64:96, :],
                                                              in1=SIN[64:96, tsl], op=ALU.mult),
                             reads=[PST[5], ROPET], writes=[T2T])
                        S.op("dve", lambda e: e.tensor_tensor(out=QH[par_][64:96, tsl], in0=T1[64:96, :], in1=T2[64:96, :],
                                                              op=ALU.add),
                             reads=[T1T, T2T], writes=[QHT[par_][tg]])

                    def f_k(tg=tg, tsl=tsl):
                        S.op("pe", lambda e: e.matmul(
                            bank(4)[0:64, :], lhsT=WK[:, h * 64:(h + 1) * 64], rhs=CKVN[:, tsl], start=True, stop=True),
                            reads=[MWT[1], CKVNT[tg]], writes=[PST[4]])
                        S.op("act", lambda e: e.activation(out=KH[par_][0:64, tsl], in_=bank(4)[0:64, :], func=AF.Copy),
                             reads=[PST[4]], writes=[KHT[par_][tg]])
                        S.op("dve", lambda e: e.tensor_copy(out=KH[par_][64:96, tsl], in_=KRO[64:96, tsl]),
                             reads=[KROT[tg]], writes=[KHT[par_][tg]])
                    out += [f_a, f_b, f_k]
                vc0 = 0 if par_ == 0 else 64
                for half8 in range(2):
                    def f_v(half8=half8):
                        for t8 in range(8):
                            tt = half8 * 8 + t8
                            S.op("pe", lambda e, t8=t8, tt=tt: e.matmul(
                                bank(5)[:, t8 * 64:(t8 + 1) * 64], lhsT=CKVN[:, tt * 128:(tt + 1) * 128],
                                rhs=WV[:, h * 64:(h + 1) * 64], start=True, stop=True),
                                reads=[CKVNT[tt // 4], MWT[2]], writes=[PST[5]])
                        dst = VA[par_][:, half8 * 8:(half8 + 1) * 8, vc0:vc0 + 64]
                        srcv = bank(5).rearrange("p (a b) -> p a b", a=8, b=64)
                        S.op("dve", lambda e: e.tensor_copy(out=dst, in_=srcv), reads=[PST[5]], writes=[VAT[par_]])
                    out.append(f_v)
                return out

            nxt = head_prep(0)
            for f in nxt:
                f()
            for h in range(HEADS):
                par_ = h % 2
                nxt = head_prep(h + 1) if h + 1 < HEADS else []
                steps = [(qg, kp) for qg in range(NTG) for kp in range(8)]
                prev = None

                def do_pv(qg, kp, sl, ob, h=h):
                    for hf in range(2):
                        kc = 2 * kp + hf
                        S.op("pe", lambda e, kc=kc, hf=hf, sl=sl, ob=ob: e.matmul(
                            bank(ob), lhsT=VA[h % 2][:, kc, :], rhs=PTB[sl][:, hf * TG:(hf + 1) * TG],
                            start=(kp == 0 and hf == 0), stop=(kp == 7 and hf == 1)),
                            reads=[VAT[h % 2], PTT[sl]], writes=[PST[ob]])
                    if kp == 7:
                        tsl = slice(qg * TG, (qg + 1) * TG)
                        lo, hi = (slice(0, 64), slice(64, 128)) if h % 2 == 0 else (slice(64, 128), slice(0, 64))
                        S.op("dve", lambda e, ob=ob, lo=lo, hi=hi: e.reciprocal(out=R[lo, :], in_=bank(ob)[hi, :]),
                             reads=[PST[ob]], writes=[RT])
                        S.op("dve", lambda e, ob=ob, lo=lo, tsl=tsl: e.tensor_tensor(
                            out=YC[lo, 2 + h // 2, tsl], in0=bank(ob)[lo, :], in1=R[lo, :], op=ALU.mult),
                            reads=[PST[ob], RT], writes=[YCT[2 + h // 2][qg]])

                for si, (qg, kp) in enumerate(steps):
                    sl = pcnt["s"] % 2
                    ptsl = pcnt["s"] % 3
                    pcnt["s"] += 1
                    if kp == 0:
                        pcnt["o"] += 1
                    ob = 6 + pcnt["o"] % 2
                    for hf in range(2):
                        kc = 2 * kp + hf
                        S.op("pe", lambda e, sl=sl, hf=hf, kc=kc, qg=qg, par_=par_: e.matmul(
                            PS[sl][:, hf * TG:(hf + 1) * TG], lhsT=KH[par_][0:96, kc * 128:(kc + 1) * 128],
                            rhs=QH[par_][0:96, qg * TG:(qg + 1) * TG], start=True, stop=True),
                            reads=[KHT[par_][kc // 4], QHT[par_][qg]], writes=[PST[2 * sl + hf]])
                    S.op("act", lambda e, sl=sl, ptsl=ptsl: e.activation(out=PTB[ptsl], in_=PS[sl][:, 0:1024], func=AF.Exp,
                                                                       scale=SCALE),
                         reads=[PST[2 * sl], PST[2 * sl + 1]], writes=[PTT[ptsl]])
                    if prev is not None:
                        do_pv(*prev)
                    prev = (qg, kp, ptsl, ob)
                    if si % 2 == 1 and nxt:
                        nxt.pop(0)()
                do_pv(*prev)
                for f in nxt:
                    f()
            if dump_yc:
                S.barrier()
                for c in range(KC):
                    for tg in range(NTG):
                        tsl = slice(tg * TG, (tg + 1) * TG)
                        S.op("dve", lambda e, c=c, tsl=tsl: e.tensor_copy(out=X[:, c, tsl], in_=YC[:, c, tsl]),
                             reads=[YCT[c][tg]], writes=[XT[c][tg]])
                return
            S.barrier()
            A.off = o_xn
            o_dsb = [A.alloc(KC * TG * 4) for _ in range(2)]
            DSB = [A.view(o, F32, [KC, TG]) for o in o_dsb]
            DSBT = [[T("mdsb0")], [T("mdsb1")]]
            A.off = o_tmp
            o_r2 = [A.alloc(TG * 4) for _ in range(2)]
            o_sq3 = [A.alloc(TG * 2) for _ in range(4)]
            R2 = [A.view(o, F32, [TG]) for o in o_r2]
            R2T = [T("mr2_0"), T("mr2_1")]
            SQ3 = [A.view(o, BF16, [TG]) for o in o_sq3]
            SQ3T = [T(f"sq3_{q}") for q in range(4)]
            wo = [RG.get() for _ in range(4)]
            WO = [(v.rearrange("p (k c) -> p k c", k=KC, c=256), t) for (v, t) in wo]
            cnt = 0
            for tg in range(NTG):
                tsl = slice(tg * TG, (tg + 1) * TG)
                sb = 6 + tg % 2
                pend = []
                for m in range(KC):
                    W, wt = WO[m // 2]
                    co = (m % 2) * 128
                    b = cnt % 4
                    cnt += 1
                    for kc in range(KC):
                        S.op("pe", lambda e, b=b, W=W, co=co, kc=kc, tsl=tsl: e.matmul(
                            bank(b), lhsT=W[:, kc, co:co + 128], rhs=YC[:, kc, tsl], start=(kc == 0), stop=(kc == KC - 1)),
                            reads=[wt, YCT[kc][tg]], writes=[PST[b]])
                    dv = DSB[tg % 2][:, m, :]
                    S.op("act", lambda e, dv=dv, b=b: e.activation(out=dv, in_=bank(b), func=AF.Copy),
                         reads=[PST[b]], writes=DSBT[tg % 2])
                    q = cnt % 4
                    sqv, sqt = SQ3[q], SQ3T[q]
                    S.op("act", lambda e, sqv=sqv, b=b: e.activation(out=sqv, in_=bank(b), func=AF.Square),
                         reads=[PST[b]], writes=[sqt])
                    for f in pend:
                        f()
                    pend = []

                    def mk(sb=sb, sqv=sqv, sqt=sqt, m=m):
                        S.op("pe", lambda e: e.matmul(bank(sb), lhsT=ONES, rhs=sqv, start=(m == 0), stop=(m == KC - 1)),
                             reads=[sqt, ONEST], writes=[PST[sb]])
                    pend.append(mk)
                for f in pend:
                    f()
                postnorm_apply(l, 24, tg * TG, DSB[tg % 2], DSBT[tg % 2], sb, R2[tg % 2], R2T[tg % 2], 1.0)
            RG.done(4)

        for s in range(nseq):
            for c in range(KC):
                S.op("sp", lambda e, s=s, c=c: e.dma_start(out=X[:, c, :], in_=xT[s][:, c * SEQ:(c + 1) * SEQ]),
                     writes=XT[c], dma=True, dma_tile=XLD[c], nobarrier=True)
            pi = 0
            while pi < len(plan):
                l, stg = plan[pi]
                if stg == "mix":
                    mix_stage(l)
                    pi += 1
                else:
                    chain = []
                    while pi < len(plan) and plan[pi][1] != "mix":
                        chain.append((plan[pi][0], 0 if plan[pi][1] == "ffn1" else 1))
                        pi += 1
                    ffn_chain(chain)
            S.barrier()
            stores = []
            for c in range(KC):
                stores.append(S.op("sp", lambda e, s=s, c=c: e.dma_start(
                    out=yT[s][:, c * SEQ:(c + 1) * SEQ], in_=X[:, c, :]),
                    reads=XT[c], dma=True, dma_tile=XST[c]))
        S.op("sp", lambda e: e.nop(), after=stores)
        S.run()
    return nc


def _prep_shared(inp):
    L = DEPTH
    f = lambda a: np.ascontiguousarray(np.asarray(a, dtype=np.float32))
    wgu = np.empty((L, 2, NJ, 128, 2, KC, 128), np.float32)
    wdn = np.empty((L, 2, KC, 128, NJ, 128), np.float32)
    for i, (ngu, ndn) in enumerate((("ffn1_w_gu", "ffn1_w_down"), ("ffn2_w_gu", "ffn2_w_down"))):
        g = f(inp[ngu]).reshape(L, KC, 128, 2, NJ, 128)
        wgu[:, i] = g.transpose(0, 4, 2, 3, 1, 5)
        d = f(inp[ndn]).reshape(L, NJ, 128, KC, 128)
        wdn[:, i] = d.transpose(0, 3, 2, 1, 4)
    wgu = wgu.reshape(L * 2 * NJ, 128, 2048)
    wdn = wdn.reshape(L * 2 * KC, 128, NJ * 128)
    cst = np.eye(128, dtype=np.float32)

    w_in = f(inp["w_in"])
    r = np.arange
    kr = 1152
    col_sets = [
        np.concatenate([r(0, 128), r(512, 640)]),
        np.concatenate([r(256, 384), r(128, 256)]),
        np.concatenate([r(640, 768), r(384, 512)]),
        r(1184, 1440),
        r(1440, 1696),
        r(768, 1024),
        np.concatenate([r(1024, 1152)] + [r(kr, kr + 32)] * 3),
        np.concatenate([np.concatenate([r(kr + 16, kr + 32), r(kr, kr + 16)])] * 3),
    ]
    wmix = np.zeros((L, 128, NMIX), np.float32)
    off = 0
    for cs in col_sets:
        blk = w_in[:, :, cs].reshape(L, KC, 128, len(cs)).transpose(0, 2, 1, 3)
        n = KC * len(cs)
        wmix[:, :, off:off + n] = blk.reshape(L, 128, n)
        off += n
    w_out = f(inp["w_out"])
    for mp in range(4):
        blk = w_out[:, :, mp * 256:(mp + 1) * 256].reshape(L, KC, 128, 256).transpose(0, 2, 1, 3)
        wmix[:, :, off:off + WO_ITEM] = blk.reshape(L, 128, WO_ITEM)
        off += WO_ITEM
    w_uq = f(inp["w_uq"]).reshape(L, 2, 128, HEADS, QK_DIM)
    a0 = w_uq
    rot = np.concatenate([r(0, 64), r(80, 96), r(64, 80)])
    a1 = w_uq[..., rot]
    wuq = np.stack([a0, a1], 0).transpose(1, 3, 4, 0, 2, 5)
    wmix[:, :, off:off + NWUQ] = wuq.reshape(L, 128, NWUQ)
    off += NWUQ
    w_ukv = f(inp["w_ukv"]).reshape(L, 128, HEADS, 128)
    wmix[:, :, off:off + 512] = w_ukv[..., :64].reshape(L, 128, 512)
    off += 512
    wmix[:, :, off:off + 512] = w_ukv[..., 64:].reshape(L, 128, 512)
    off += 512
    ws = f(inp["gmlp_ws"])
    wmix[:, :, off:off + 512] = ws.transpose(0, 3, 1, 2).reshape(L, 128, 512)
    off += 512
    assert off == NMIX

    par = np.zeros((L, 128, NPAR), np.float32)
    for ci, nm in enumerate(("ffn1_pre_g", "ffn1_post_g", "mix_pre_g", "mix_post_g", "ffn2_pre_g", "ffn2_post_g")):
        par[:, :, ci * 8:(ci + 1) * 8] = f(inp[nm]).reshape(L, KC, 128).transpose(0, 2, 1)
    cw = f(inp["conv_w"]).reshape(L, 3, 2, 128)
    par[:, :, 48:54] = cw.transpose(0, 3, 2, 1).reshape(L, 128, 6)
    par[:, :, 54:56] = f(inp["conv_b"]).reshape(L, 2, 128).transpose(0, 2, 1)
    par[:, :, 56:58] = f(inp["q_norm_g"]).reshape(L, 2, 128).transpose(0, 2, 1)
    par[:, :, 58] = f(inp["kv_norm_g"])
    par[:, :, 64:320] = f(inp["gmlp_norm_g"])[:, None, :]
    gb = f(inp["gmlp_b"])
    for fc in range(2):
        par[:, 0:64, 320 + fc * 128:320 + (fc + 1) * 128] = gb[:, 2 * fc, None, :]
        par[:, 64:128, 320 + fc * 128:320 + (fc + 1) * 128] = gb[:, 2 * fc + 1, None, :]

    pos = np.arange(SEQ, dtype=np.float32)
    inv = (1.0 / (np.float32(10000.0) ** (np.arange(0, 32, 2, dtype=np.float32) / np.float32(32)))).astype(np.float32)
    ang = (pos[:, None] * inv[None, :]).astype(np.float32)
    cos = np.cos(ang).astype(np.float32).T
    sin = np.sin(ang).astype(np.float32).T
    rope = np.zeros((128, 2, SEQ), np.float32)
    rope[64:80, 0] = cos
    rope[80:96, 0] = cos
    rope[64:80, 1] = -sin
    rope[80:96, 1] = sin
    rope = rope.reshape(128, 2 * SEQ)
    return {"wgu": wgu, "wdn": wdn, "cst": cst, "wmix": wmix, "par": par, "rope": rope}


def _prep_x(x):
    xt = np.asarray(x, np.float32).reshape(BATCH, SEQ, KC, 128).transpose(0, 3, 2, 1)
    return np.ascontiguousarray(xt).reshape(NCORES, NSEQ, 128, KC * SEQ)


def _unprep_y(ys):
    y = np.stack(ys, 0).reshape(BATCH, 128, KC, SEQ).transpose(0, 3, 2, 1)
    return np.ascontiguousarray(y).reshape(BATCH, SEQ, D_MODEL)


def kernel(**inputs):
    shared = _prep_shared(inputs)
    xs = _prep_x(inputs["x"])
    nc = build_program()
    in_maps = [dict(shared, xT=xs[c]) for c in range(NCORES)]
    res = run_bass_kernel_spmd(nc, in_maps, core_ids=list(range(NCORES)))
    return _unprep_y([res.results[c]["yT"] for c in range(NCORES)])
```

```python
import contextlib
import math
import numpy as np
import concourse.bass as bass
import concourse.mybir as mybir
from concourse.bass_utils import run_bass_kernel_spmd

F32 = mybir.dt.float32
BF16 = mybir.dt.bfloat16
AF = mybir.ActivationFunctionType
ALU = mybir.AluOpType

D_MODEL = 1024
BATCH = 32
SEQ = 2048
DEPTH = 2
D_FF = 2816
EPS = 1e-6
NCORES = 8
NSEQ = BATCH // NCORES
KC = D_MODEL // 128
NJ = D_FF // 128
HEADS = 8
QK_DIM = 96
SCALE = 1.0 / math.sqrt(QK_DIM)
TG = 512
NTG = SEQ // TG

WIN_ITEMS = [8 * 256, 8 * 256, 8 * 256, 8 * 256, 8 * 256, 8 * 256, 8 * 224, 8 * 96]
WO_ITEM = 8 * 256
NWUQ = HEADS * 2 * 2 * 96
NMIXW = NWUQ + 512 + 512 + 512
NMIX = sum(WIN_ITEMS) + 4 * WO_ITEM + NMIXW
NPAR = 576
SEM_CAP = 16000
ENGINES = ("pe", "act", "dve", "pool", "sp")


class T:
    __slots__ = ("name", "w", "rs", "rdma", "sem", "cnt")

    def __init__(self, name):
        self.name = name
        self.w = None
        self.rs = {}
        self.rdma = []
        self.sem = None
        self.cnt = 0


class Op:
    __slots__ = ("eng", "fn", "deps", "signal", "sem", "val", "dma")

    def __init__(self, eng, fn, dma):
        self.eng = eng
        self.fn = fn
        self.deps = []
        self.signal = False
        self.sem = None
        self.val = None
        self.dma = dma


class Sched:
    def __init__(self, nc, stack):
        self.nc = nc
        self.stack = stack
        self.ops = {e: [] for e in ENGINES}
        self.nsem = 0
        self.eng_sems = {e: [] for e in ENGINES}
        self.last = {e: None for e in ENGINES}
        self.bar = []

    def new_sem(self, name):
        self.nsem += 1
        return self.stack.enter_context(self.nc.semaphore(f"s{self.nsem}_{name}"))

    def _dep(self, op, other):
        if other is None or other is op:
            return
        if other.eng == "pe" and op.eng == "pe" and not other.dma and not op.dma:
            return
        for d in op.deps:
            if d is other:
                return
        op.deps.append(other)
        other.signal = True

    def op(self, eng, fn, reads=(), writes=(), dma=False, dma_tile=None, nobarrier=False, after=()):
        o = Op(eng, fn, dma)
        if not nobarrier:
            for b in self.bar:
                self._dep(o, b)
        for a in after:
            self._dep(o, a)
        for t in reads:
            self._dep(o, t.w)
        for t in writes:
            self._dep(o, t.w)
            for r in t.rs.values():
                self._dep(o, r)
            for r in t.rdma:
                self._dep(o, r)
        for t in reads:
            if dma:
                t.rdma.append(o)
            else:
                t.rs[eng] = o
        for t in writes:
            t.w = o
            t.rs = {}
            t.rdma = []
        if dma:
            t = dma_tile
            if t.sem is None:
                t.sem = self.new_sem("d_" + t.name)
            t.cnt += 16
            o.sem = t.sem
            o.val = t.cnt
            o.signal = True
        else:
            self.last[eng] = o
        self.ops[eng].append(o)
        return o

    def barrier(self):
        self.bar = [self.last[e] for e in ("pe", "act", "dve") if self.last[e] is not None]

    def finalize(self):
        for e in ENGINES:
            n = 0
            for o in self.ops[e]:
                if o.dma or not o.signal:
                    continue
                k = n // SEM_CAP
                while len(self.eng_sems[e]) <= k:
                    self.eng_sems[e].append(self.new_sem(f"{e}{len(self.eng_sems[e])}"))
                o.sem = self.eng_sems[e][k]
                o.val = n % SEM_CAP + 1
                n += 1

    def emit(self, eng_name, eng):
        waited = {}
        for o in self.ops[eng_name]:
            for d in o.deps:
                key = d.sem.num
                if waited.get(key, 0) >= d.val:
                    continue
                eng.wait_ge(d.sem, d.val)
                waited[key] = d.val
            ins = o.fn(eng)
            if o.signal:
                ins.then_inc(o.sem, 16 if o.dma else 1)

    def run(self):
        self.finalize()
        with self.nc.Block() as block:
            @block.tensor
            def _(e):
                self.emit("pe", e)

            @block.scalar
            def _(e):
                self.emit("act", e)

            @block.vector
            def _(e):
                self.emit("dve", e)

            @block.gpsimd
            def _(e):
                self.emit("pool", e)

            @block.sync
            def _(e):
                self.emit("sp", e)


class Arena:
    def __init__(self, base_f32):
        self.base = base_f32
        self.off = 0

    def alloc(self, nbytes):
        o = self.off
        self.off += (nbytes + 63) // 64 * 64
        return o

    def view(self, off, dtype, shape):
        n = 1
        for s in shape:
            n *= s
        esz = 2 if dtype == BF16 else 4
        assert off % 4 == 0
        w0 = off // 4
        w1 = (off + n * esz + 3) // 4
        ap = self.base[:, w0:w1]
        if dtype == BF16:
            ap = ap.bitcast(BF16)
        ap = ap[:, 0:n]
        if len(shape) == 2:
            ap = ap.rearrange("p (a b) -> p a b", a=shape[0], b=shape[1])
        elif len(shape) == 3:
            ap = ap.rearrange("p (a b c) -> p a b c", a=shape[0], b=shape[1], c=shape[2])
        elif len(shape) == 4:
            ap = ap.rearrange("p (a b c d) -> p a b c d", a=shape[0], b=shape[1], c=shape[2], d=shape[3])
        return ap


class Ring:
    def __init__(self, S, name, views, items):
        self.S = S
        self.views = views
        self.items = items
        self.n = len(views)
        self.ts = [T(f"{name}{i}") for i in range(self.n)]
        self.k = 0
        self.issued = 0
        self.rel = 0

    def _issue(self, i):
        slot = i % self.n
        src, n = self.items[i]
        dst = self.views[slot][:, 0:n]
        self.S.op("pool", lambda e: e.dma_start(out=dst, in_=src), writes=[self.ts[slot]],
                  dma=True, dma_tile=self.ts[slot], nobarrier=True)

    def _fill(self):
        lim = min(len(self.items), self.rel + self.n)
        while self.issued < lim:
            self._issue(self.issued)
            self.issued += 1

    def get(self):
        self._fill()
        i = self.k
        assert i < self.issued, "ring over-subscribed"
        self.k += 1
        slot = i % self.n
        return self.views[slot], self.ts[slot]

    def done(self, n=1):
        self.rel += n
        assert self.rel <= self.k
        self._fill()


FULL_PLAN = [(0, "ffn1"), (0, "mix"), (0, "ffn2"), (1, "ffn1"), (1, "mix"), (1, "ffn2")]


def build_program(nseq=NSEQ, plan=FULL_PLAN, NG=4, ND=2, dump_yc=False):
    nc = bass.Bass("TRN2", target_bir_lowering=False)
    L = DEPTH
    xT = nc.dram_tensor("xT", [nseq, 128, KC * SEQ], F32, kind="ExternalInput").ap()
    wgu = nc.dram_tensor("wgu", [L * 2 * NJ, 128, 2048], F32, kind="ExternalInput").ap()
    wdn = nc.dram_tensor("wdn", [L * 2 * KC, 128, NJ * 128], F32, kind="ExternalInput").ap()
    wmix = nc.dram_tensor("wmix", [L, 128, NMIX], F32, kind="ExternalInput").ap()
    par = nc.dram_tensor("par", [L, 128, NPAR], F32, kind="ExternalInput").ap()
    cst = nc.dram_tensor("cst", [128, 128], F32, kind="ExternalInput").ap()
    rope = nc.dram_tensor("rope", [128, 2 * SEQ], F32, kind="ExternalInput").ap()
    yT = nc.dram_tensor("yT", [nseq, 128, KC * SEQ], F32, kind="ExternalOutput").ap()

    with contextlib.ExitStack() as st:
        S = Sched(nc, st)
        ARENA_BYTES = 211968
        arena_t = st.enter_context(nc.sbuf_tensor("arena", [128, ARENA_BYTES // 4], F32))
        A = Arena(arena_t[:])
        PS = [st.enter_context(nc.psum_tensor(f"ps{i}", [128, 1024], F32)) for i in range(4)]
        PST = [T(f"bank{b}") for b in range(8)]

        def bank(b):
            return PS[b // 2][:, (b % 2) * 512:(b % 2) * 512 + 512]

        o_x = A.alloc(KC * SEQ * 4)
        X = A.view(o_x, F32, [KC, SEQ])
        XT = [[T(f"x{c}_{g}") for g in range(NTG)] for c in range(KC)]
        XLD = [T(f"xld{c}") for c in range(KC)]
        XST = [T(f"xst{c}") for c in range(KC)]
        o_par = A.alloc(L * NPAR * 4)
        PAR = A.view(o_par, F32, [L, NPAR])
        PART = T("par")
        o_id = A.alloc(128 * 4)
        IDENT = A.view(o_id, F32, [128])
        o_ones = A.alloc(128 * 2)
        ONES = A.view(o_ones, BF16, [128])
        o_mw = ARENA_BYTES - NMIXW * 2
        o_rope = o_mw - 2 * SEQ * 2
        MIX_LIMIT = o_rope
        ROPE = A.view(o_rope, BF16, [2, SEQ])
        CSTT = T("cst")
        ROPET = T("rope")
        o_rg = [A.alloc(4096) for _ in range(NG)]
        o_rd = [A.alloc(NJ * 128 * 2) for _ in range(ND)]
        MIXW = A.view(o_mw, BF16, [NMIXW])
        MWT = [T("wuq"), T("wk"), T("wv"), T("wst")]
        WUQ = MIXW[:, 0:NWUQ].rearrange("p (h a k c) -> p h a k c", h=HEADS, a=2, k=2, c=96)
        WK = MIXW[:, NWUQ:NWUQ + 512]
        WV = MIXW[:, NWUQ + 512:NWUQ + 1024]
        WST = MIXW[:, NWUQ + 1024:NWUQ + 1536].rearrange("p (g q) -> p g q", g=4, q=128)
        phase_base = A.off
        PHASE_BYTES = ARENA_BYTES - phase_base

        g_items, d_items = [], []
        for s in range(nseq):
            for (l, stg) in plan:
                if stg in ("ffn1", "ffn2"):
                    i = 0 if stg == "ffn1" else 1
                    for hf in range(2):
                        for j in range(NJ):
                            g_items.append((wgu[(l * 2 + i) * NJ + j], 2048))
                        for m in range(KC):
                            d_items.append((wdn[(l * 2 + i) * KC + m], NJ * 128))
                else:
                    off = 0
                    for n in WIN_ITEMS + [WO_ITEM] * 4:
                        g_items.append((wmix[l][:, off:off + n], n))
                        off += n
        RG = Ring(S, "rg", [A.view(o, BF16, [2048]) for o in o_rg], g_items)
        RD = Ring(S, "rd", [A.view(o, BF16, [NJ * 128]) for o in o_rd], d_items)

        S.op("sp", lambda e: e.dma_start(out=PAR, in_=par.rearrange("l p n -> p l n")), writes=[PART],
             dma=True, dma_tile=PART)
        S.op("sp", lambda e: e.dma_start(out=IDENT, in_=cst), writes=[CSTT], dma=True, dma_tile=CSTT)
        ONEST = T("ones")
        S.op("dve", lambda e: e.memset(ONES, 1.0), writes=[ONEST])
        COS = ROPE[:, 0, :]
        SIN = ROPE[:, 1, :]

        def pcol(l, c):
            return PAR[:, l, c:c + 1]

        state = {"sq": 0}

        def rstd_from_stats(stat_b, dst, dst_t, n, dim):
            src = bank(stat_b)[:, 0:n]
            S.op("act", lambda e: e.activation(out=dst, in_=src, func=AF.Ln, bias=EPSC, scale=1.0 / dim),
                 reads=[PST[stat_b], CSTT2], writes=[dst_t])
            S.op("act", lambda e: e.activation(out=dst, in_=dst, func=AF.Exp, scale=-0.5),
                 reads=[dst_t], writes=[dst_t])

        o_eps = A.alloc(64)
        phase_base = A.off
        PHASE_BYTES = ARENA_BYTES - phase_base
        EPSC = A.view(o_eps, F32, [1])
        CSTT2 = T("eps")
        S.op("dve", lambda e: e.memset(EPSC, EPS), writes=[CSTT2])

        def prenorm(l, gcol, t0, ntg, XN, XNT, RSTD, RSTDT, SQ, SQT):
            for tg in range(ntg):
                T0 = t0 + tg * TG
                gtg = T0 // TG
                sb = 6 + (tg % 2)
                for c in range(KC):
                    q = state["sq"] % len(SQ)
                    state["sq"] += 1
                    sqv, sqt = SQ[q], SQT[q]
                    xin = X[:, c, T0:T0 + TG]
                    S.op("act", lambda e, sqv=sqv, xin=xin: e.activation(out=sqv, in_=xin, func=AF.Square),
                         reads=[XT[c][gtg]], writes=[sqt])
                    S.op("pe", lambda e, sb=sb, sqv=sqv, c=c: e.matmul(bank(sb), lhsT=ONES, rhs=sqv,
                                                                         start=(c == 0), stop=(c == KC - 1)),
                         reads=[sqt, ONEST], writes=[PST[sb]])
                rs = RSTD[:, tg * TG:(tg + 1) * TG]
                rstd_from_stats(sb, rs, RSTDT[tg], TG, D_MODEL)
                for c in range(KC):
                    xin = X[:, c, T0:T0 + TG]
                    xo = XN[:, c, tg * TG:(tg + 1) * TG]
                    gc = pcol(l, gcol + c)
                    S.op("dve", lambda e, xo=xo, xin=xin, gc=gc, rs=rs: e.scalar_tensor_tensor(
                        out=xo, in0=xin, scalar=gc, in1=rs, op0=ALU.mult, op1=ALU.mult),
                        reads=[XT[c][gtg], RSTDT[tg], PART], writes=[XNT[tg]])

        def postnorm_apply(l, gcol, T0, DSB, DSBT, stat_b, R2, R2T, half):
            gtg = T0 // TG
            rstd_from_stats(stat_b, R2, R2T, TG, D_MODEL)
            for m in range(KC):
                dv = DSB[:, m, :]
                gc = pcol(l, gcol + m)
                S.op("dve", lambda e, dv=dv, gc=gc: e.scalar_tensor_tensor(
                    out=dv, in0=dv, scalar=gc, in1=R2, op0=ALU.mult, op1=ALU.mult),
                    reads=[R2T, PART] + DSBT, writes=DSBT)
                xv = X[:, m, T0:T0 + TG]
                S.op("dve", lambda e, dv=dv, xv=xv: e.scalar_tensor_tensor(
                    out=xv, in0=dv, scalar=half, in1=xv, op0=ALU.mult, op1=ALU.add),
                    reads=DSBT + [XT[m][gtg]], writes=[XT[m][gtg]])

        def ffn_chain(stages):
            from collections import deque
            S.barrier()
            A.off = phase_base
            o_xn = A.alloc(KC * 1024 * 2)
            o_h = A.alloc(NJ * 1024 * 2)
            o_sq = [A.alloc(TG * 2) for _ in range(4)]
            o_rstd = A.alloc(1024 * 4)
            o_r2 = [A.alloc(TG * 4) for _ in range(2)]
            o_st = [A.alloc(TG * 2) for _ in range(2)]
            o_dsb = [A.alloc(KC * TG * 4) for _ in range(2)]
            assert A.off <= ARENA_BYTES, A.off
            XN = A.view(o_xn, BF16, [KC, 1024])
            XNT = [T("xn0"), T("xn1")]
            H = A.view(o_h, BF16, [NJ, 1024])
            HT = [[T(f"h{j}_{g}") for g in range(2)] for j in range(NJ)]
            SQ = [A.view(o, BF16, [TG]) for o in o_sq]
            SQT = [T(f"sq{q}") for q in range(4)]
            RSTD = A.view(o_rstd, F32, [1024])
            RSTDT = [T("rstd0"), T("rstd1")]
            R2 = [A.view(o, F32, [TG]) for o in o_r2]
            R2T = [T("r2_0"), T("r2_1")]
            ST = [A.view(o, BF16, [TG]) for o in o_st]
            STT = [T("st0"), T("st1")]
            DSB = [A.view(o, F32, [KC, TG]) for o in o_dsb]
            DSBT = [[T("dsb0")], [T("dsb1")]]
            blocks = [(l, i, hf) for (l, i) in stages for hf in range(2)]
            nb_ = len(blocks)
            bg = deque()
            cnt = {"gu": 0, "d": 0}

            def drain(n):
                for _ in range(n):
                    if not bg:
                        return
                    bg.popleft()()

            def pre_stats(k):
                l, i, hf = blocks[k]
                t0 = hf * 1024
                out = []
                for tg in range(2):
                    T0 = t0 + tg * TG
                    gtg = T0 // TG
                    sb = 6 + tg
                    for c in range(KC):
                        def f(c=c, T0=T0, gtg=gtg, sb=sb):
                            q = state["sq"] % 4
                            state["sq"] += 1
                            sqv, sqt = SQ[q], SQT[q]
                            xin = X[:, c, T0:T0 + TG]
                            S.op("act", lambda e: e.activation(out=sqv, in_=xin, func=AF.Square),
                                 reads=[XT[c][gtg]], writes=[sqt])
                            S.op("pe", lambda e: e.matmul(bank(sb), lhsT=ONES, rhs=sqv, start=(c == 0), stop=(c == KC - 1)),
                                 reads=[sqt, ONEST], writes=[PST[sb]])
                        out.append(f)

                    def g(tg=tg, sb=sb):
                        rstd_from_stats(sb, RSTD[:, tg * TG:(tg + 1) * TG], RSTDT[tg], TG, D_MODEL)
                    out.append(g)
                return out

            def pre_xn(k):
                l, i, hf = blocks[k]
                gpre = 0 if i == 0 else 32
                t0 = hf * 1024
                for tg in range(2):
                    T0 = t0 + tg * TG
                    gtg = T0 // TG
                    rs = RSTD[:, tg * TG:(tg + 1) * TG]
                    for c in range(KC):
                        xin = X[:, c, T0:T0 + TG]
                        xo = XN[:, c, tg * TG:(tg + 1) * TG]
                        gc = pcol(l, gpre + c)
                        S.op("dve", lambda e, xo=xo, xin=xin, gc=gc, rs=rs: e.scalar_tensor_tensor(
                            out=xo, in0=xin, scalar=gc, in1=rs, op0=ALU.mult, op1=ALU.mult),
                            reads=[XT[c][gtg], RSTDT[tg], PART], writes=[XNT[tg]])

            def post_apply(k):
                l, i, hf = blocks[k]
                gpost = 8 if i == 0 else 40
                out = []
                for tg in range(2):
                    T0 = hf * 1024 + tg * TG
                    gtg = T0 // TG

                    def g(tg=tg):
                        rstd_from_stats(6 + tg, R2[tg], R2T[tg], TG, D_MODEL)
                    out.append(g)
                    for m in range(KC):
                        def f(m=m, tg=tg, T0=T0, gtg=gtg):
                            dv = DSB[tg][:, m, :]
                            gc = pcol(l, gpost + m)
                            S.op("dve", lambda e: e.scalar_tensor_tensor(
                                out=dv, in0=dv, scalar=gc, in1=R2[tg], op0=ALU.mult, op1=ALU.mult),
                                reads=[R2T[tg], PART] + DSBT[tg], writes=DSBT[tg])
                            xv = X[:, m, T0:T0 + TG]
                            S.op("dve", lambda e: e.scalar_tensor_tensor(
                                out=xv, in0=dv, scalar=0.5, in1=xv, op0=ALU.mult, op1=ALU.add),
                                reads=DSBT[tg] + [XT[m][gtg]], writes=[XT[m][gtg]])
                        out.append(f)
                return out

            for f in pre_stats(0):
                f()
            pre_xn(0)
            for k in range(nb_):
                l, i, hf = blocks[k]
                if k + 1 < nb_:
                    bg.extend(pre_stats(k + 1))
                for j in range(NJ):
                    wv, wt = RG.get()
                    W = wv.rearrange("p (a k m) -> p a k m", a=2, k=KC, m=128)
                    for tg in range(2):
                        pair = cnt["gu"] % 2
                        cnt["gu"] += 1
                        bg_, bu = 2 * pair, 2 * pair + 1
                        for a_, b in ((0, bg_), (1, bu)):
                            for kc in range(KC):
                                S.op("pe", lambda e, a_=a_, b=b, kc=kc, W=W, tg=tg: e.matmul(
                                    bank(b), lhsT=W[:, a_, kc, :], rhs=XN[:, kc, tg * TG:(tg + 1) * TG],
                                    start=(kc == 0), stop=(kc == KC - 1)),
                                    reads=[wt, XNT[tg]], writes=[PST[b]])
                        sv, stt = ST[pair], STT[pair]
                        S.op("act", lambda e, sv=sv, bg_=bg_: e.activation(out=sv, in_=bank(bg_), func=AF.Silu),
                             reads=[PST[bg_]], writes=[stt])
                        ho = H[:, j, tg * TG:(tg + 1) * TG]
                        S.op("dve", lambda e, ho=ho, sv=sv, bu=bu: e.tensor_tensor(
                            out=ho, in0=bank(bu), in1=sv, op=ALU.mult),
                            reads=[PST[bu], stt], writes=[HT[j][tg]])
                        drain(2)
                    RG.done()
                drain(10 ** 6)
                if k + 1 < nb_:
                    pre_xn(k + 1)
                pend = []
                for m in range(KC):
                    wv, wt = RD.get()
                    Wd = wv.rearrange("p (j c) -> p j c", j=NJ, c=128)
                    for tg in range(2):
                        b = 4 + cnt["d"] % 2
                        cnt["d"] += 1
                        for j in range(NJ):
                            S.op("pe", lambda e, b=b, j=j, Wd=Wd, tg=tg: e.matmul(
                                bank(b), lhsT=Wd[:, j, :], rhs=H[:, j, tg * TG:(tg + 1) * TG],
                                start=(j == 0), stop=(j == NJ - 1)),
                                reads=[wt, HT[j][tg]], writes=[PST[b]])
                        dv = DSB[tg][:, m, :]
                        S.op("act", lambda e, dv=dv, b=b: e.activation(out=dv, in_=bank(b), func=AF.Copy),
                             reads=[PST[b]], writes=DSBT[tg])
                        q = state["sq"] % 4
                        state["sq"] += 1
                        sqv, sqt = SQ[q], SQT[q]
                        S.op("act", lambda e, sqv=sqv, b=b: e.activation(out=sqv, in_=bank(b), func=AF.Square),
                             reads=[PST[b]], writes=[sqt])
                        for f in pend:
                            f()
                        pend = []
                        sb = 6 + tg

                        def mk(sb=sb, sqv=sqv, sqt=sqt, m=m):
                            S.op("pe", lambda e: e.matmul(bank(sb), lhsT=ONES, rhs=sqv,
                                                          start=(m == 0), stop=(m == KC - 1)),
                                 reads=[sqt, ONEST], writes=[PST[sb]])
                        pend.append(mk)
                    RD.done()
                for f in pend:
                    f()
                bg.extend(post_apply(k))
            drain(10 ** 6)

        def load_mixw(l):
            base = sum(WIN_ITEMS) + 4 * WO_ITEM
            segs = [(0, NWUQ), (NWUQ, 512), (NWUQ + 512, 512), (NWUQ + 1024, 512)]
            for (o, n), t in zip(segs, MWT):
                S.op("pool", lambda e, o=o, n=n: e.dma_start(out=MIXW[:, o:o + n], in_=wmix[l][:, base + o:base + o + n]),
                     writes=[t], dma=True, dma_tile=t)
            S.op("pool", lambda e: e.dma_start(out=ROPE, in_=rope.rearrange("p (a t) -> p a t", a=2)),
                 writes=[ROPET], dma=True, dma_tile=ROPET)

        def mix_stage(l):
            S.barrier()
            load_mixw(l)
            A.off = phase_base
            o_xn = A.alloc(32768)
            o_yc = A.alloc(32768)
            o_tmp = A.off
            XN = A.view(o_xn, BF16, [KC, SEQ])
            XNT = [T(f"mxn{g}") for g in range(NTG)]
            YC = A.view(o_yc, BF16, [KC, SEQ])
            YCT = [[T(f"yc{c}_{g}") for g in range(NTG)] for c in range(KC)]
            o_rstd = A.alloc(SEQ * 4)
            o_sq = [A.alloc(TG * 2) for _ in range(4)]
            RSTD = A.view(o_rstd, F32, [SEQ])
            RSTDT = [T(f"mrstd{g}") for g in range(NTG)]
            SQ = [A.view(o, BF16, [TG]) for o in o_sq]
            SQT = [T(f"msq{q}") for q in range(4)]
            prenorm(l, 16, 0, NTG, XN, XNT, RSTD, RSTDT, SQ, SQT)
            o_xcs = [A.alloc(TG * 4) for _ in range(2)]
            o_zc = A.alloc((SEQ + 2) * 2)
            o_gb = A.alloc(SEQ * 2)
            o_dg = A.alloc(6 * 128 * 2)
            assert A.off <= MIX_LIMIT
            XCS = [A.view(o, F32, [TG]) for o in o_xcs]
            XCST = [T("xcs0"), T("xcs1")]
            ZC = A.view(o_zc, BF16, [SEQ + 2])
            ZCT = [T(f"zc{g}") for g in range(NTG)]
            ZPAD = T("zpad")
            GB = A.view(o_gb, BF16, [SEQ])
            GBT = [T(f"gb{g}") for g in range(NTG)]
            DG = A.view(o_dg, BF16, [6, 128])
            DGT = T("diag")
            for q in range(6):
                S.op("dve", lambda e, q=q: e.tensor_scalar(out=DG[:, q, :], in0=IDENT, scalar1=pcol(l, 48 + q),
                                                           scalar2=None, op0=ALU.mult),
                     reads=[CSTT, PART], writes=[DGT])
            S.op("dve", lambda e: e.memset(ZC[:, 0:1], 0.0), writes=[ZPAD])
            S.op("dve", lambda e: e.memset(ZC[:, SEQ + 1:SEQ + 2], 0.0), writes=[ZPAD])
            items = [RG.get() for _ in range(3)]
            Wi = [(v.rearrange("p (k c) -> p k c", k=KC, c=256), t) for (v, t) in items]
            sel = {0: ((0, 0), (0, 128), (1, 0)), 1: ((1, 128), (2, 0), (2, 128))}
            cnt = 0
            for cc in range(2):
                for tg in range(NTG):
                    bset = (cnt % 2) * 3
                    cnt += 1
                    for r_, (it, co) in enumerate(sel[cc]):
                        W, wt = Wi[it]
                        b = bset + r_
                        for kc in range(KC):
                            S.op("pe", lambda e, b=b, W=W, co=co, kc=kc, tg=tg: e.matmul(
                                bank(b), lhsT=W[:, kc, co:co + 128], rhs=XN[:, kc, tg * TG:(tg + 1) * TG],
                                start=(kc == 0), stop=(kc == KC - 1)),
                                reads=[wt, XNT[tg]], writes=[PST[b]])
                    xs, xst = XCS[tg % 2], XCST[tg % 2]
                    S.op("act", lambda e, xs=xs, b=bset: e.activation(out=xs, in_=bank(b), func=AF.Copy),
                         reads=[PST[bset]], writes=[xst])
                    zo = ZC[:, 1 + tg * TG:1 + (tg + 1) * TG]
                    S.op("dve", lambda e, zo=zo, xs=xs, b=bset + 1: e.tensor_tensor(out=zo, in0=bank(b), in1=xs, op=ALU.mult),
                         reads=[PST[bset + 1], xst], writes=[ZCT[tg]])
                    go = GB[:, tg * TG:(tg + 1) * TG]
                    S.op("act", lambda e, go=go, b=bset + 2: e.activation(out=go, in_=bank(b), func=AF.Copy),
                         reads=[PST[bset + 2]], writes=[GBT[tg]])
                for tg in range(NTG):
                    b = 6 + tg % 2
                    rd = [ZCT[g] for g in (tg - 1, tg, tg + 1) if 0 <= g < NTG] + [ZPAD, DGT]
                    for k in range(3):
                        S.op("pe", lambda e, b=b, k=k, cc=cc, tg=tg: e.matmul(
                            bank(b), lhsT=DG[:, cc * 3 + k, :], rhs=ZC[:, tg * TG + k:tg * TG + k + TG],
                            start=(k == 0), stop=(k == 2)),
                            reads=rd, writes=[PST[b]])
                    yo = YC[:, cc, tg * TG:(tg + 1) * TG]
                    go = GB[:, tg * TG:(tg + 1) * TG]
                    S.op("dve", lambda e, yo=yo, go=go, b=b, cc=cc: e.scalar_tensor_tensor(
                        out=yo, in0=bank(b), scalar=pcol(l, 54 + cc), in1=go, op0=ALU.add, op1=ALU.mult),
                        reads=[PST[b], GBT[tg], PART], writes=[YCT[cc][tg]])
            RG.done(3)
            S.barrier()
            A.off = o_tmp
            o_gv = A.alloc(16 * 256 * 2)
            o_vt = A.alloc(16 * 256 * 2)
            o_ut = [A.alloc(TG * 2) for _ in range(2)]
            o_ss = A.alloc(64)
            o_rv = A.alloc(64)
            o_jk = A.alloc(256 * 2)
            o_tm = [A.alloc(TG * 4) for _ in range(2)]
            assert A.off <= MIX_LIMIT
            GV = A.view(o_gv, BF16, [16, 256])
            GVT = [T(f"gv{i}") for i in range(16)]
            VT_ = A.view(o_vt, BF16, [16, 256])
            VTT = [T(f"vt{i}") for i in range(16)]
            UT = [A.view(o, BF16, [TG]) for o in o_ut]
            UTT = [T("ut0"), T("ut1")]
            SS = A.view(o_ss, F32, [16])
            SST = T("ss")
            RV = A.view(o_rv, F32, [16])
            RVT = T("rv")
            JK = A.view(o_jk, BF16, [256])
            JKT = T("jk")
            TM = [A.view(o, F32, [TG]) for o in o_tm]
            TMT = [T("tm0"), T("tm1")]
            (i3v, i3t) = RG.get()
            (i4v, i4t) = RG.get()
            W3 = i3v.rearrange("p (k c) -> p k c", k=KC, c=256)
            W4 = i4v.rearrange("p (k c) -> p k c", k=KC, c=256)
            for tt in range(16):
                b = 4 + tt % 2
                for kc in range(KC):
                    S.op("pe", lambda e, b=b, kc=kc, tt=tt: e.matmul(
                        bank(b)[:, 0:256], lhsT=XN[:, kc, tt * 128:(tt + 1) * 128], rhs=W4[:, kc, :],
                        start=(kc == 0), stop=(kc == KC - 1)),
                        reads=[i4t, XNT[tt // 4]], writes=[PST[b]])
                S.op("act", lambda e, b=b, tt=tt: e.activation(out=GV[:, tt, :], in_=bank(b)[:, 0:256], func=AF.Gelu),
                     reads=[PST[b]], writes=[GVT[tt]])
                S.op("act", lambda e, tt=tt: e.activation(out=JK, in_=GV[:, tt, :], func=AF.Square,
                                                          accum_out=SS[:, tt:tt + 1]),
                     reads=[GVT[tt]], writes=[JKT, SST])
            S.op("act", lambda e: e.activation(out=RV, in_=SS, func=AF.Ln, bias=EPSC, scale=1.0 / 256),
                 reads=[SST, CSTT2], writes=[RVT])
            S.op("act", lambda e: e.activation(out=RV, in_=RV, func=AF.Exp, scale=-0.5), reads=[RVT], writes=[RVT])
            GN = PAR[:, l, 64:320]
            for tt in range(16):
                S.op("dve", lambda e, tt=tt: e.scalar_tensor_tensor(
                    out=VT_[:, tt, :], in0=GV[:, tt, :], scalar=RV[:, tt:tt + 1], in1=GN, op0=ALU.mult, op1=ALU.mult),
                    reads=[GVT[tt], RVT, PART], writes=[VTT[tt]])
            cnt = 0
            for fc in range(2):
                for tg in range(NTG):
                    bm = 6 + cnt % 2
                    bz = cnt % 4
                    sl = cnt % 2
                    cnt += 1
                    for ch in range(4):
                        c16 = tg * 4 + ch
                        for gi in range(2):
                            g = 2 * fc + gi
                            S.op("pe", lambda e, bm=bm, gi=gi, ch=ch, c16=c16, g=g: e.matmul(
                                bank(bm)[gi * 64:(gi + 1) * 64, ch * 128:(ch + 1) * 128],
                                lhsT=VT_[:, c16, g * 64:(g + 1) * 64], rhs=WST[:, g, :], start=True, stop=True),
                                reads=[VTT[c16], MWT[3]], writes=[PST[bm]])
                    for kc in range(KC):
                        S.op("pe", lambda e, bz=bz, kc=kc, fc=fc, tg=tg: e.matmul(
                            bank(bz), lhsT=W3[:, kc, fc * 128:(fc + 1) * 128], rhs=XN[:, kc, tg * TG:(tg + 1) * TG],
                            start=(kc == 0), stop=(kc == KC - 1)),
                            reads=[i3t, XNT[tg]], writes=[PST[bz]])
                    S.op("act", lambda e, bz=bz, sl=sl: e.activation(out=UT[sl], in_=bank(bz), func=AF.Gelu),
                         reads=[PST[bz]], writes=[UTT[sl]])
                    for ch in range(4):
                        S.op("dve", lambda e, bm=bm, sl=sl, ch=ch, fc=fc: e.tensor_tensor(
                            out=TM[sl][:, ch * 128:(ch + 1) * 128], in0=bank(bm)[:, ch * 128:(ch + 1) * 128],
                            in1=PAR[:, l, 320 + fc * 128:320 + (fc + 1) * 128], op=ALU.add),
                            reads=[PST[bm], PART], writes=[TMT[sl]])
                    yo = YC[:, 6 + fc, tg * TG:(tg + 1) * TG]
                    S.op("dve", lambda e, yo=yo, sl=sl: e.tensor_tensor(out=yo, in0=TM[sl], in1=UT[sl], op=ALU.mult),
                         reads=[TMT[sl], UTT[sl]], writes=[YCT[6 + fc][tg]])
            RG.done(2)
            S.barrier()
            A.off = o_tmp
            o_cqn = A.alloc(2 * SEQ * 2)
            o_ckvn = A.alloc(SEQ * 2)
            o_kro = A.alloc(SEQ * 2)
            o_rq = A.alloc(TG * 4)
            o_rkv = A.alloc(TG * 4)
            o_t1 = A.alloc(TG * 4)
            o_t2 = A.alloc(TG * 4)
            o_sq2 = [A.alloc(TG * 2) for _ in range(3)]
            assert A.off <= MIX_LIMIT, A.off
            CQN = A.view(o_cqn, BF16, [2, SEQ])
            CQNT = [T(f"cqn{g}") for g in range(NTG)]
            CKVN = A.view(o_ckvn, BF16, [SEQ])
            CKVNT = [T(f"ckvn{g}") for g in range(NTG)]
            KRO = A.view(o_kro, BF16, [SEQ])
            KROT = [T(f"kro{g}") for g in range(NTG)]
            RQ = A.view(o_rq, F32, [TG])
            RQT = T("rq")
            RKV = A.view(o_rkv, F32, [TG])
            RKVT = T("rkv")
            T1 = A.view(o_t1, F32, [TG])
            T1T = T("t1")
            T2 = A.view(o_t2, F32, [TG])
            T2T = T("t2")
            SQ2 = [A.view(o, BF16, [TG]) for o in o_sq2]
            SQ2T = [T(f"sq2_{q}") for q in range(3)]
            (i5v, i5t) = RG.get()
            (i6v, i6t) = RG.get()
            (i7v, i7t) = RG.get()
            W5 = i5v.rearrange("p (k c) -> p k c", k=KC, c=256)
            W6 = i6v[:, 0:KC * 224].rearrange("p (k c) -> p k c", k=KC, c=224)
            W7 = i7v[:, 0:KC * 96].rearrange("p (k c) -> p k c", k=KC, c=96)
            for tg in range(NTG):
                tsl = slice(tg * TG, (tg + 1) * TG)
                projs = [(0, W5, i5t, 0, 128), (1, W5, i5t, 128, 128), (2, W6, i6t, 0, 128),
                         (3, W6, i6t, 128, 96), (4, W7, i7t, 0, 96)]
                for (b, W, wt, co, mw) in projs:
                    for kc in range(KC):
                        S.op("pe", lambda e, b=b, W=W, co=co, mw=mw, kc=kc, tsl=tsl: e.matmul(
                            bank(b)[0:mw, :], lhsT=W[:, kc, co:co + mw], rhs=XN[:, kc, tsl],
                            start=(kc == 0), stop=(kc == KC - 1)),
                            reads=[wt, XNT[tg]], writes=[PST[b]])
                for c in range(3):
                    S.op("act", lambda e, c=c: e.activation(out=SQ2[c], in_=bank(c), func=AF.Square),
                         reads=[PST[c]], writes=[SQ2T[c]])
                for c in range(2):
                    S.op("pe", lambda e, c=c: e.matmul(bank(6), lhsT=ONES, rhs=SQ2[c], start=(c == 0), stop=(c == 1)),
                         reads=[SQ2T[c], ONEST], writes=[PST[6]])
                S.op("pe", lambda e: e.matmul(bank(7), lhsT=ONES, rhs=SQ2[2], start=True, stop=True),
                     reads=[SQ2T[2], ONEST], writes=[PST[7]])
                rstd_from_stats(6, RQ, RQT, TG, 256)
                rstd_from_stats(7, RKV, RKVT, TG, 128)
                for c in range(2):
                    S.op("dve", lambda e, c=c, tsl=tsl: e.scalar_tensor_tensor(
                        out=CQN[:, c, tsl], in0=bank(c), scalar=pcol(l, 56 + c), in1=RQ, op0=ALU.mult, op1=ALU.mult),
                        reads=[PST[c], RQT, PART], writes=[CQNT[tg]])
                S.op("dve", lambda e, tsl=tsl: e.scalar_tensor_tensor(
                    out=CKVN[:, tsl], in0=bank(2), scalar=pcol(l, 58), in1=RKV, op0=ALU.mult, op1=ALU.mult),
                    reads=[PST[2], RKVT, PART], writes=[CKVNT[tg]])
                S.op("dve", lambda e, tsl=tsl: e.tensor_tensor(out=T1[64:96, :], in0=bank(3)[64:96, :], in1=COS[64:96, tsl],
                                                               op=ALU.mult),
                     reads=[PST[3], ROPET], writes=[T1T])
                S.op("dve", lambda e, tsl=tsl: e.tensor_tensor(out=T2[64:96, :], in0=bank(4)[64:96, :], in1=SIN[64:96, tsl],
                                                               op=ALU.mult),
                     reads=[PST[4], ROPET], writes=[T2T])
                S.op("dve", lambda e, tsl=tsl: e.tensor_tensor(out=KRO[64:96, tsl], in0=T1[64:96, :], in1=T2[64:96, :],
                                                               op=ALU.add),
                     reads=[T1T, T2T], writes=[KROT[tg]])
            RG.done(3)
            S.barrier()
            A.off = o_xn
            o_va = [A.alloc(16 * 128 * 2) for _ in range(2)]
            o_qh = [A.alloc(SEQ * 2) for _ in range(2)]
            o_kh = [A.alloc(SEQ * 2) for _ in range(2)]
            o_pt = [A.alloc(1024 * 2) for _ in range(3)]
            o_r = A.alloc(TG * 4)
            assert A.off <= o_yc, A.off
            VA = [A.view(o, BF16, [16, 128]) for o in o_va]
            VAT = [T("va0"), T("va1")]
            QH = [A.view(o, BF16, [SEQ]) for o in o_qh]
            QHT = [[T(f"qh{p}_{g}") for g in range(NTG)] for p in range(2)]
            KH = [A.view(o, BF16, [SEQ]) for o in o_kh]
            KHT = [[T(f"kh{p}_{g}") for g in range(NTG)] for p in range(2)]
            PTB = [A.view(o, BF16, [1024]) for o in o_pt]
            PTT = [T(f"pt{i}") for i in range(3)]
            R = A.view(o_r, F32, [TG])
            RT = T("r")
            S.op("dve", lambda e: e.memset(VA[0][:, :, 64:128], 1.0), writes=[VAT[0]])
            S.op("dve", lambda e: e.memset(VA[1][:, :, 0:64], 1.0), writes=[VAT[1]])
            pcnt = {"s": 0, "o": 0}

            def head_prep(h):
                par_ = h % 2
                out = []
                for tg in range(NTG):
                    tsl = slice(tg * TG, (tg + 1) * TG)

                    def f_a(tg=tg, tsl=tsl):
                        for kc in range(2):
                            S.op("pe", lambda e, kc=kc: e.matmul(
                                bank(4)[0:96, :], lhsT=WUQ[:, h, 0, kc, :], rhs=CQN[:, kc, tsl],
                                start=(kc == 0), stop=(kc == 1)),
                                reads=[MWT[0], CQNT[tg]], writes=[PST[4]])
                        S.op("dve", lambda e: e.tensor_copy(out=QH[par_][0:64, tsl], in_=bank(4)[0:64, :]),
                             reads=[PST[4]], writes=[QHT[par_][tg]])
                        S.op("dve", lambda e: e.tensor_tensor(out=T1[64:96, :], in0=bank(4)[64:96, :],
                                                              in1=COS[64:96, tsl], op=ALU.mult),
                             reads=[PST[4], ROPET], writes=[T1T])

                    def f_b(tg=tg, tsl=tsl):
                        for kc in range(2):
                            S.op("pe", lambda e, kc=kc: e.matmul(
                                bank(5)[0:96, :], lhsT=WUQ[:, h, 1, kc, :], rhs=CQN[:, kc, tsl],
                                start=(kc == 0), stop=(kc == 1)),
                                reads=[MWT[0], CQNT[tg]], writes=[PST[5]])
                        S.op("dve", lambda e: e.tensor_tensor(out=T2[64:96, :], in0=bank(5)[64:96, :],
                                                              in1=SIN[64:96, tsl], op=ALU.mult),
                             reads=[PST[5], ROPET], writes=[T2T])
                        S.op("dve", lambda e: e.tensor_tensor(out=QH[par_][64:96, tsl], in0=T1[64:96, :], in1=T2[64:96, :],
                                                              op=ALU.add),
                             reads=[T1T, T2T], writes=[QHT[par_][tg]])

                    def f_k(tg=tg, tsl=tsl):
                        S.op("pe", lambda e: e.matmul(
                            bank(4)[0:64, :], lhsT=WK[:, h * 64:(h + 1) * 64], rhs=CKVN[:, tsl], start=True, stop=True),
                            reads=[MWT[1], CKVNT[tg]], writes=[PST[4]])
                        S.op("dve", lambda e: e.tensor_copy(out=KH[par_][0:64, tsl], in_=bank(4)[0:64, :]),
                             reads=[PST[4]], writes=[KHT[par_][tg]])
                        S.op("dve", lambda e: e.tensor_copy(out=KH[par_][64:96, tsl], in_=KRO[64:96, tsl]),
                             reads=[KROT[tg]], writes=[KHT[par_][tg]])
                    out += [f_a, f_b, f_k]
                vc0 = 0 if par_ == 0 else 64
                for half8 in range(2):
                    def f_v(half8=half8):
                        for t8 in range(8):
                            tt = half8 * 8 + t8
                            S.op("pe", lambda e, t8=t8, tt=tt: e.matmul(
                                bank(5)[:, t8 * 64:(t8 + 1) * 64], lhsT=CKVN[:, tt * 128:(tt + 1) * 128],
                                rhs=WV[:, h * 64:(h + 1) * 64], start=True, stop=True),
                                reads=[CKVNT[tt // 4], MWT[2]], writes=[PST[5]])
                        dst = VA[par_][:, half8 * 8:(half8 + 1) * 8, vc0:vc0 + 64]
                        srcv = bank(5).rearrange("p (a b) -> p a b", a=8, b=64)
                        S.op("dve", lambda e: e.tensor_copy(out=dst, in_=srcv), reads=[PST[5]], writes=[VAT[par_]])
                    out.append(f_v)
                return out

            for f in head_prep(0):
                f()
            gsteps = [(h, qg, kp) for h in range(HEADS) for qg in range(NTG) for kp in range(8)]
            ng = len(gsteps)

            def rec_S(g):
                h, qg, kp = gsteps[g]
                par_ = h % 2
                sl = g % 2
                for hf in range(2):
                    kc = 2 * kp + hf
                    S.op("pe", lambda e, sl=sl, hf=hf, kc=kc, qg=qg, par_=par_: e.matmul(
                        PS[sl][:, hf * TG:(hf + 1) * TG], lhsT=KH[par_][0:96, kc * 128:(kc + 1) * 128],
                        rhs=QH[par_][0:96, qg * TG:(qg + 1) * TG], start=True, stop=True),
                        reads=[KHT[par_][kc // 4], QHT[par_][qg]], writes=[PST[2 * sl + hf]])

            def rec_exp(g):
                sl = g % 2
                ptsl = g % 3
                S.op("act", lambda e, sl=sl, ptsl=ptsl: e.activation(out=PTB[ptsl], in_=PS[sl][:, 0:1024], func=AF.Exp,
                                                                   scale=SCALE),
                     reads=[PST[2 * sl], PST[2 * sl + 1]], writes=[PTT[ptsl]])

            def rec_pv(g):
                h, qg, kp = gsteps[g]
                ptsl = g % 3
                ob = 6 + (g // 8) % 2
                for hf in range(2):
                    kc = 2 * kp + hf
                    S.op("pe", lambda e, kc=kc, hf=hf, ptsl=ptsl, ob=ob, h=h, kp=kp: e.matmul(
                        bank(ob), lhsT=VA[h % 2][:, kc, :], rhs=PTB[ptsl][:, hf * TG:(hf + 1) * TG],
                        start=(kp == 0 and hf == 0), stop=(kp == 7 and hf == 1)),
                        reads=[VAT[h % 2], PTT[ptsl]], writes=[PST[ob]])
                if kp == 7:
                    tsl = slice(qg * TG, (qg + 1) * TG)
                    lo, hi = (slice(0, 64), slice(64, 128)) if h % 2 == 0 else (slice(64, 128), slice(0, 64))
                    S.op("act", lambda e, ob=ob, lo=lo, hi=hi: e.activation(out=R[lo, :], in_=bank(ob)[hi, :], func=AF.Ln),
                         reads=[PST[ob]], writes=[RT])
                    S.op("act", lambda e, lo=lo: e.activation(out=R[lo, :], in_=R[lo, :], func=AF.Exp, scale=-1.0),
                         reads=[RT], writes=[RT])
                    S.op("dve", lambda e, ob=ob, lo=lo, tsl=tsl, h=h: e.tensor_tensor(
                        out=YC[lo, 2 + h // 2, tsl], in0=bank(ob)[lo, :], in1=R[lo, :], op=ALU.mult),
                        reads=[PST[ob], RT], writes=[YCT[2 + h // 2][qg]])

            nxt = []
            rec_S(0)
            for g in range(ng):
                h, qg, kp = gsteps[g]
                if qg == 0 and kp == 0:
                    for f in nxt:
                        f()
                    nxt = head_prep(h + 1) if h + 1 < HEADS else []
                if g + 1 < ng:
                    if gsteps[g + 1][0] != h:
                        for f in nxt:
                            f()
                        nxt = []
                    rec_S(g + 1)
                rec_exp(g)
                if g >= 1:
                    rec_pv(g - 1)
                if g % 2 == 1 and nxt:
                    nxt.pop(0)()
            rec_pv(ng - 1)
            if dump_yc:
                S.barrier()
                for c in range(KC):
                    for tg in range(NTG):
                        tsl = slice(tg * TG, (tg + 1) * TG)
                        S.op("dve", lambda e, c=c, tsl=tsl: e.tensor_copy(out=X[:, c, tsl], in_=YC[:, c, tsl]),
                             reads=[YCT[c][tg]], writes=[XT[c][tg]])
                return
            S.barrier()
            A.off = o_xn
            o_dsb = [A.alloc(KC * TG * 4) for _ in range(2)]
            DSB = [A.view(o, F32, [KC, TG]) for o in o_dsb]
            DSBT = [[T("mdsb0")], [T("mdsb1")]]
            A.off = o_tmp
            o_r2 = [A.alloc(TG * 4) for _ in range(2)]
            o_sq3 = [A.alloc(TG * 2) for _ in range(4)]
            R2 = [A.view(o, F32, [TG]) for o in o_r2]
            R2T = [T("mr2_0"), T("mr2_1")]
            SQ3 = [A.view(o, BF16, [TG]) for o in o_sq3]
            SQ3T = [T(f"sq3_{q}") for q in range(4)]
            wo = [RG.get() for _ in range(4)]
            WO = [(v.rearrange("p (k c) -> p k c", k=KC, c=256), t) for (v, t) in wo]
            cnt = 0
            for tg in range(NTG):
                tsl = slice(tg * TG, (tg + 1) * TG)
                sb = 6 + tg % 2
                pend = []
                for m in range(KC):
                    W, wt = WO[m // 2]
                    co = (m % 2) * 128
                    b = cnt % 4
                    cnt += 1
                    for kc in range(KC):
                        S.op("pe", lambda e, b=b, W=W, co=co, kc=kc, tsl=tsl: e.matmul(
                            bank(b), lhsT=W[:, kc, co:co + 128], rhs=YC[:, kc, tsl], start=(kc == 0), stop=(kc == KC - 1)),
                            reads=[wt, YCT[kc][tg]], writes=[PST[b]])
                    dv = DSB[tg % 2][:, m, :]
                    S.op("act", lambda e, dv=dv, b=b: e.activation(out=dv, in_=bank(b), func=AF.Copy),
                         reads=[PST[b]], writes=DSBT[tg % 2])
                    q = cnt % 4
                    sqv, sqt = SQ3[q], SQ3T[q]
                    S.op("act", lambda e, sqv=sqv, b=b: e.activation(out=sqv, in_=bank(b), func=AF.Square),
                         reads=[PST[b]], writes=[sqt])
                    for f in pend:
                        f()
                    pend = []

                    def mk(sb=sb, sqv=sqv, sqt=sqt, m=m):
                        S.op("pe", lambda e: e.matmul(bank(sb), lhsT=ONES, rhs=sqv, start=(m == 0), stop=(m == KC - 1)),
                             reads=[sqt, ONEST], writes=[PST[sb]])
                    pend.append(mk)
                for f in pend:
                    f()
                postnorm_apply(l, 24, tg * TG, DSB[tg % 2], DSBT[tg % 2], sb, R2[tg % 2], R2T[tg % 2], 1.0)
            RG.done(4)

        for s in range(nseq):
            for c in range(KC):
                S.op("sp", lambda e, s=s, c=c: e.dma_start(out=X[:, c, :], in_=xT[s][:, c * SEQ:(c + 1) * SEQ]),
                     writes=XT[c], dma=True, dma_tile=XLD[c], nobarrier=True)
            pi = 0
            while pi < len(plan):
                l, stg = plan[pi]
                if stg == "mix":
                    mix_stage(l)
                    pi += 1
                else:
                    chain = []
                    while pi < len(plan) and plan[pi][1] != "mix":
                        chain.append((plan[pi][0], 0 if plan[pi][1] == "ffn1" else 1))
                        pi += 1
                    ffn_chain(chain)
            S.barrier()
            stores = []
            for c in range(KC):
                stores.append(S.op("sp", lambda e, s=s, c=c: e.dma_start(
                    out=yT[s][:, c * SEQ:(c + 1) * SEQ], in_=X[:, c, :]),
                    reads=XT[c], dma=True, dma_tile=XST[c]))
        S.op("sp", lambda e: e.nop(), after=stores)
        S.run()
    return nc


def _prep_shared(inp):
    L = DEPTH
    f = lambda a: np.ascontiguousarray(np.asarray(a, dtype=np.float32))
    wgu = np.empty((L, 2, NJ, 128, 2, KC, 128), np.float32)
    wdn = np.empty((L, 2, KC, 128, NJ, 128), np.float32)
    for i, (ngu, ndn) in enumerate((("ffn1_w_gu", "ffn1_w_down"), ("ffn2_w_gu", "ffn2_w_down"))):
        g = f(inp[ngu]).reshape(L, KC, 128, 2, NJ, 128)
        wgu[:, i] = g.transpose(0, 4, 2, 3, 1, 5)
        d = f(inp[ndn]).reshape(L, NJ, 128, KC, 128)
        wdn[:, i] = d.transpose(0, 3, 2, 1, 4)
    wgu = wgu.reshape(L * 2 * NJ, 128, 2048)
    wdn = wdn.reshape(L * 2 * KC, 128, NJ * 128)
    cst = np.eye(128, dtype=np.float32)

    w_in = f(inp["w_in"])
    r = np.arange
    kr = 1152
    col_sets = [
        np.concatenate([r(0, 128), r(512, 640)]),
        np.concatenate([r(256, 384), r(128, 256)]),
        np.concatenate([r(640, 768), r(384, 512)]),
        r(1184, 1440),
        r(1440, 1696),
        r(768, 1024),
        np.concatenate([r(1024, 1152)] + [r(kr, kr + 32)] * 3),
        np.concatenate([np.concatenate([r(kr + 16, kr + 32), r(kr, kr + 16)])] * 3),
    ]
    wmix = np.zeros((L, 128, NMIX), np.float32)
    off = 0
    for cs in col_sets:
        blk = w_in[:, :, cs].reshape(L, KC, 128, len(cs)).transpose(0, 2, 1, 3)
        n = KC * len(cs)
        wmix[:, :, off:off + n] = blk.reshape(L, 128, n)
        off += n
    w_out = f(inp["w_out"])
    for mp in range(4):
        blk = w_out[:, :, mp * 256:(mp + 1) * 256].reshape(L, KC, 128, 256).transpose(0, 2, 1, 3)
        wmix[:, :, off:off + WO_ITEM] = blk.reshape(L, 128, WO_ITEM)
        off += WO_ITEM
    w_uq = f(inp["w_uq"]).reshape(L, 2, 128, HEADS, QK_DIM)
    a0 = w_uq
    rot = np.concatenate([r(0, 64), r(80, 96), r(64, 80)])
    a1 = w_uq[..., rot]
    wuq = np.stack([a0, a1], 0).transpose(1, 3, 4, 0, 2, 5)
    wmix[:, :, off:off + NWUQ] = wuq.reshape(L, 128, NWUQ)
    off += NWUQ
    w_ukv = f(inp["w_ukv"]).reshape(L, 128, HEADS, 128)
    wmix[:, :, off:off + 512] = w_ukv[..., :64].reshape(L, 128, 512)
    off += 512
    wmix[:, :, off:off + 512] = w_ukv[..., 64:].reshape(L, 128, 512)
    off += 512
    ws = f(inp["gmlp_ws"])
    wmix[:, :, off:off + 512] = ws.transpose(0, 3, 1, 2).reshape(L, 128, 512)
    off += 512
    assert off == NMIX

    par = np.zeros((L, 128, NPAR), np.float32)
    for ci, nm in enumerate(("ffn1_pre_g", "ffn1_post_g", "mix_pre_g", "mix_post_g", "ffn2_pre_g", "ffn2_post_g")):
        par[:, :, ci * 8:(ci + 1) * 8] = f(inp[nm]).reshape(L, KC, 128).transpose(0, 2, 1)
    cw = f(inp["conv_w"]).reshape(L, 3, 2, 128)
    par[:, :, 48:54] = cw.transpose(0, 3, 2, 1).reshape(L, 128, 6)
    par[:, :, 54:56] = f(inp["conv_b"]).reshape(L, 2, 128).transpose(0, 2, 1)
    par[:, :, 56:58] = f(inp["q_norm_g"]).reshape(L, 2, 128).transpose(0, 2, 1)
    par[:, :, 58] = f(inp["kv_norm_g"])
    par[:, :, 64:320] = f(inp["gmlp_norm_g"])[:, None, :]
    gb = f(inp["gmlp_b"])
    for fc in range(2):
        par[:, 0:64, 320 + fc * 128:320 + (fc + 1) * 128] = gb[:, 2 * fc, None, :]
        par[:, 64:128, 320 + fc * 128:320 + (fc + 1) * 128] = gb[:, 2 * fc + 1, None, :]

    pos = np.arange(SEQ, dtype=np.float32)
    inv = (1.0 / (np.float32(10000.0) ** (np.arange(0, 32, 2, dtype=np.float32) / np.float32(32)))).astype(np.float32)
    ang = (pos[:, None] * inv[None, :]).astype(np.float32)
    cos = np.cos(ang).astype(np.float32).T
    sin = np.sin(ang).astype(np.float32).T
    rope = np.zeros((128, 2, SEQ), np.float32)
    rope[64:80, 0] = cos
    rope[80:96, 0] = cos
    rope[64:80, 1] = -sin
    rope[80:96, 1] = sin
    rope = rope.reshape(128, 2 * SEQ)
    return {"wgu": wgu, "wdn": wdn, "cst": cst, "wmix": wmix, "par": par, "rope": rope}


def _prep_x(x):
    xt = np.asarray(x, np.float32).reshape(BATCH, SEQ, KC, 128).transpose(0, 3, 2, 1)
    return np.ascontiguousarray(xt).reshape(NCORES, NSEQ, 128, KC * SEQ)


def _unprep_y(ys):
    y = np.stack(ys, 0).reshape(BATCH, 128, KC, SEQ).transpose(0, 3, 2, 1)
    return np.ascontiguousarray(y).reshape(BATCH, SEQ, D_MODEL)


def kernel(**inputs):
    shared = _prep_shared(inputs)
    xs = _prep_x(inputs["x"])
    nc = build_program()
    in_maps = [dict(shared, xT=xs[c]) for c in range(NCORES)]
    res = run_bass_kernel_spmd(nc, in_maps, core_ids=list(range(NCORES)))
    return _unprep_y([res.results[c]["yT"] for c in range(NCORES)])
```

```python
import contextlib
import math
import numpy as np
import concourse.bass as bass
import concourse.mybir as mybir
from concourse.bass_utils import run_bass_kernel_spmd

F32 = mybir.dt.float32
BF16 = mybir.dt.bfloat16
AF = mybir.ActivationFunctionType
ALU = mybir.AluOpType

D_MODEL = 1024
BATCH = 32
SEQ = 2048
DEPTH = 2
D_FF = 2816
EPS = 1e-6
NCORES = 8
NSEQ = BATCH // NCORES
KC = D_MODEL // 128
NJ = D_FF // 128
HEADS = 8
QK_DIM = 96
SCALE = 1.0 / math.sqrt(QK_DIM)
TG = 512
NTG = SEQ // TG

WIN_ITEMS = [8 * 256, 8 * 256, 8 * 256, 8 * 256, 8 * 256, 8 * 256, 8 * 224, 8 * 96]
WO_ITEM = 8 * 256
NWUQ = HEADS * 2 * 2 * 96
NMIXW = NWUQ + 512 + 512 + 512
NMIX = sum(WIN_ITEMS) + 4 * WO_ITEM + NMIXW
NPAR = 576
SEM_CAP = 16000
ENGINES = ("pe", "act", "dve", "pool", "sp")


class T:
    __slots__ = ("name", "w", "rs", "rdma", "sem", "cnt")

    def __init__(self, name):
        self.name = name
        self.w = None
        self.rs = {}
        self.rdma = []
        self.sem = None
        self.cnt = 0


class Op:
    __slots__ = ("eng", "fn", "deps", "signal", "sem", "val", "dma")

    def __init__(self, eng, fn, dma):
        self.eng = eng
        self.fn = fn
        self.deps = []
        self.signal = False
        self.sem = None
        self.val = None
        self.dma = dma


class Sched:
    def __init__(self, nc, stack):
        self.nc = nc
        self.stack = stack
        self.ops = {e: [] for e in ENGINES}
        self.nsem = 0
        self.eng_sems = {e: [] for e in ENGINES}
        self.last = {e: None for e in ENGINES}
        self.bar = []

    def new_sem(self, name):
        self.nsem += 1
        return self.stack.enter_context(self.nc.semaphore(f"s{self.nsem}_{name}"))

    def _dep(self, op, other):
        if other is None or other is op:
            return
        if other.eng == "pe" and op.eng == "pe" and not other.dma and not op.dma:
            return
        for d in op.deps:
            if d is other:
                return
        op.deps.append(other)
        other.signal = True

    def op(self, eng, fn, reads=(), writes=(), dma=False, dma_tile=None, nobarrier=False, after=()):
        o = Op(eng, fn, dma)
        if not nobarrier:
            for b in self.bar:
                self._dep(o, b)
        for a in after:
            self._dep(o, a)
        for t in reads:
            self._dep(o, t.w)
        for t in writes:
            self._dep(o, t.w)
            for r in t.rs.values():
                self._dep(o, r)
            for r in t.rdma:
                self._dep(o, r)
        for t in reads:
            if dma:
                t.rdma.append(o)
            else:
                t.rs[eng] = o
        for t in writes:
            t.w = o
            t.rs = {}
            t.rdma = []
        if dma:
            t = dma_tile
            if t.sem is None:
                t.sem = self.new_sem("d_" + t.name)
            t.cnt += 16
            o.sem = t.sem
            o.val = t.cnt
            o.signal = True
        else:
            self.last[eng] = o
        self.ops[eng].append(o)
        return o

    def barrier(self):
        self.bar = [self.last[e] for e in ("pe", "act", "dve") if self.last[e] is not None]

    def finalize(self):
        for e in ENGINES:
            n = 0
            for o in self.ops[e]:
                if o.dma or not o.signal:
                    continue
                k = n // SEM_CAP
                while len(self.eng_sems[e]) <= k:
                    self.eng_sems[e].append(self.new_sem(f"{e}{len(self.eng_sems[e])}"))
                o.sem = self.eng_sems[e][k]
                o.val = n % SEM_CAP + 1
                n += 1

    def emit(self, eng_name, eng):
        waited = {}
        for o in self.ops[eng_name]:
            for d in o.deps:
                key = d.sem.num
                if waited.get(key, 0) >= d.val:
                    continue
                eng.wait_ge(d.sem, d.val)
                waited[key] = d.val
            ins = o.fn(eng)
            if o.signal:
                ins.then_inc(o.sem, 16 if o.dma else 1)

    def run(self):
        self.finalize()
        with self.nc.Block() as block:
            @block.tensor
            def _(e):
                self.emit("pe", e)

            @block.scalar
            def _(e):
                self.emit("act", e)

            @block.vector
            def _(e):
                self.emit("dve", e)

            @block.gpsimd
            def _(e):
                self.emit("pool", e)

            @block.sync
            def _(e):
                self.emit("sp", e)


class Arena:
    def __init__(self, base_f32):
        self.base = base_f32
        self.off = 0

    def alloc(self, nbytes):
        o = self.off
        self.off += (nbytes + 63) // 64 * 64
        return o

    def view(self, off, dtype, shape):
        n = 1
        for s in shape:
            n *= s
        esz = 2 if dtype == BF16 else 4
        assert off % 4 == 0
        w0 = off // 4
        w1 = (off + n * esz + 3) // 4
        ap = self.base[:, w0:w1]
        if dtype == BF16:
            ap = ap.bitcast(BF16)
        ap = ap[:, 0:n]
        if len(shape) == 2:
            ap = ap.rearrange("p (a b) -> p a b", a=shape[0], b=shape[1])
        elif len(shape) == 3:
            ap = ap.rearrange("p (a b c) -> p a b c", a=shape[0], b=shape[1], c=shape[2])
        elif len(shape) == 4:
            ap = ap.rearrange("p (a b c d) -> p a b c d", a=shape[0], b=shape[1], c=shape[2], d=shape[3])
        return ap


class Ring:
    def __init__(self, S, name, views, items):
        self.S = S
        self.views = views
        self.items = items
        self.n = len(views)
        self.ts = [T(f"{name}{i}") for i in range(self.n)]
        self.k = 0
        self.issued = 0
        self.rel = 0

    def _issue(self, i):
        slot = i % self.n
        src, n = self.items[i]
        dst = self.views[slot][:, 0:n]
        self.S.op("pool", lambda e: e.dma_start(out=dst, in_=src), writes=[self.ts[slot]],
                  dma=True, dma_tile=self.ts[slot], nobarrier=True)

    def _fill(self):
        lim = min(len(self.items), self.rel + self.n)
        while self.issued < lim:
            self._issue(self.issued)
            self.issued += 1

    def get(self):
        self._fill()
        i = self.k
        assert i < self.issued, "ring over-subscribed"
        self.k += 1
        slot = i % self.n
        return self.views[slot], self.ts[slot]

    def done(self, n=1):
        self.rel += n
        assert self.rel <= self.k
        self._fill()


FULL_PLAN = [(0, "ffn1"), (0, "mix"), (0, "ffn2"), (1, "ffn1"), (1, "mix"), (1, "ffn2")]


def build_program(nseq=NSEQ, plan=FULL_PLAN, NG=4, ND=2, dump_yc=False):
    nc = bass.Bass("TRN2", target_bir_lowering=False)
    L = DEPTH
    xT = nc.dram_tensor("xT", [nseq, 128, KC * SEQ], F32, kind="ExternalInput").ap()
    wgu = nc.dram_tensor("wgu", [L * 2 * NJ, 128, 2048], F32, kind="ExternalInput").ap()
    wdn = nc.dram_tensor("wdn", [L * 2 * KC, 128, NJ * 128], F32, kind="ExternalInput").ap()
    wmix = nc.dram_tensor("wmix", [L, 128, NMIX], F32, kind="ExternalInput").ap()
    par = nc.dram_tensor("par", [L, 128, NPAR], F32, kind="ExternalInput").ap()
    cst = nc.dram_tensor("cst", [128, 128], F32, kind="ExternalInput").ap()
    rope = nc.dram_tensor("rope", [128, 2 * SEQ], F32, kind="ExternalInput").ap()
    yT = nc.dram_tensor("yT", [nseq, 128, KC * SEQ], F32, kind="ExternalOutput").ap()

    with contextlib.ExitStack() as st:
        S = Sched(nc, st)
        ARENA_BYTES = 211968
        arena_t = st.enter_context(nc.sbuf_tensor("arena", [128, ARENA_BYTES // 4], F32))
        A = Arena(arena_t[:])
        PS = [st.enter_context(nc.psum_tensor(f"ps{i}", [128, 1024], F32)) for i in range(4)]
        PST = [T(f"bank{b}") for b in range(8)]

        def bank(b):
            return PS[b // 2][:, (b % 2) * 512:(b % 2) * 512 + 512]

        o_x = A.alloc(KC * SEQ * 4)
        X = A.view(o_x, F32, [KC, SEQ])
        XT = [[T(f"x{c}_{g}") for g in range(NTG)] for c in range(KC)]
        XLD = [[T(f"xld{c}_{hf}") for hf in range(2)] for c in range(KC)]
        XST = [[T(f"xst{c}_{hf}") for hf in range(2)] for c in range(KC)]
        o_par = A.alloc(L * NPAR * 4)
        PAR = A.view(o_par, F32, [L, NPAR])
        PART = T("par")
        o_id = A.alloc(128 * 4)
        IDENT = A.view(o_id, F32, [128])
        o_ones = A.alloc(128 * 2)
        ONES = A.view(o_ones, BF16, [128])
        o_mw = ARENA_BYTES - NMIXW * 2
        o_rope = o_mw - 2 * SEQ * 2
        MIX_LIMIT = o_rope
        ROPE = A.view(o_rope, BF16, [2, SEQ])
        CSTT = T("cst")
        ROPET = T("rope")
        o_rg = [A.alloc(4096) for _ in range(NG)]
        o_rd = [A.alloc(NJ * 128 * 2) for _ in range(ND)]
        MIXW = A.view(o_mw, BF16, [NMIXW])
        MWT = [T("wuq"), T("wk"), T("wv"), T("wst")]
        WUQ = MIXW[:, 0:NWUQ].rearrange("p (h a k c) -> p h a k c", h=HEADS, a=2, k=2, c=96)
        WK = MIXW[:, NWUQ:NWUQ + 512]
        WV = MIXW[:, NWUQ + 512:NWUQ + 1024]
        WST = MIXW[:, NWUQ + 1024:NWUQ + 1536].rearrange("p (g q) -> p g q", g=4, q=128)
        phase_base = A.off
        PHASE_BYTES = ARENA_BYTES - phase_base

        g_items, d_items = [], []
        for s in range(nseq):
            for (l, stg) in plan:
                if stg in ("ffn1", "ffn2"):
                    i = 0 if stg == "ffn1" else 1
                    for hf in range(2):
                        for j in range(NJ):
                            g_items.append((wgu[(l * 2 + i) * NJ + j], 2048))
                        for m in range(KC):
                            d_items.append((wdn[(l * 2 + i) * KC + m], NJ * 128))
                else:
                    off = 0
                    for n in WIN_ITEMS + [WO_ITEM] * 4:
                        g_items.append((wmix[l][:, off:off + n], n))
                        off += n
        RG = Ring(S, "rg", [A.view(o, BF16, [2048]) for o in o_rg], g_items)
        RD = Ring(S, "rd", [A.view(o, BF16, [NJ * 128]) for o in o_rd], d_items)

        S.op("sp", lambda e: e.dma_start(out=PAR, in_=par.rearrange("l p n -> p l n")), writes=[PART],
             dma=True, dma_tile=PART)
        S.op("sp", lambda e: e.dma_start(out=IDENT, in_=cst), writes=[CSTT], dma=True, dma_tile=CSTT)
        ONEST = T("ones")
        S.op("dve", lambda e: e.memset(ONES, 1.0), writes=[ONEST])
        COS = ROPE[:, 0, :]
        SIN = ROPE[:, 1, :]

        def pcol(l, c):
            return PAR[:, l, c:c + 1]

        state = {"sq": 0}

        def rstd_from_stats(stat_b, dst, dst_t, n, dim):
            src = bank(stat_b)[:, 0:n]
            S.op("act", lambda e: e.activation(out=dst, in_=src, func=AF.Ln, bias=EPSC, scale=1.0 / dim),
                 reads=[PST[stat_b], CSTT2], writes=[dst_t])
            S.op("act", lambda e: e.activation(out=dst, in_=dst, func=AF.Exp, scale=-0.5),
                 reads=[dst_t], writes=[dst_t])

        o_eps = A.alloc(64)
        phase_base = A.off
        PHASE_BYTES = ARENA_BYTES - phase_base
        EPSC = A.view(o_eps, F32, [1])
        CSTT2 = T("eps")
        S.op("dve", lambda e: e.memset(EPSC, EPS), writes=[CSTT2])

        def prenorm(l, gcol, t0, ntg, XN, XNT, RSTD, RSTDT, SQ, SQT):
            for tg in range(ntg):
                T0 = t0 + tg * TG
                gtg = T0 // TG
                sb = 6 + (tg % 2)
                for c in range(KC):
                    q = state["sq"] % len(SQ)
                    state["sq"] += 1
                    sqv, sqt = SQ[q], SQT[q]
                    xin = X[:, c, T0:T0 + TG]
                    S.op("act", lambda e, sqv=sqv, xin=xin: e.activation(out=sqv, in_=xin, func=AF.Square),
                         reads=[XT[c][gtg]], writes=[sqt])
                    S.op("pe", lambda e, sb=sb, sqv=sqv, c=c: e.matmul(bank(sb), lhsT=ONES, rhs=sqv,
                                                                         start=(c == 0), stop=(c == KC - 1)),
                         reads=[sqt, ONEST], writes=[PST[sb]])
                rs = RSTD[:, tg * TG:(tg + 1) * TG]
                rstd_from_stats(sb, rs, RSTDT[tg], TG, D_MODEL)
                for c in range(KC):
                    xin = X[:, c, T0:T0 + TG]
                    xo = XN[:, c, tg * TG:(tg + 1) * TG]
                    gc = pcol(l, gcol + c)
                    S.op("dve", lambda e, xo=xo, xin=xin, gc=gc, rs=rs: e.scalar_tensor_tensor(
                        out=xo, in0=xin, scalar=gc, in1=rs, op0=ALU.mult, op1=ALU.mult),
                        reads=[XT[c][gtg], RSTDT[tg], PART], writes=[XNT[tg]])

        def postnorm_apply(l, gcol, T0, DSB, DSBT, stat_b, R2, R2T, half):
            gtg = T0 // TG
            rstd_from_stats(stat_b, R2, R2T, TG, D_MODEL)
            for m in range(KC):
                dv = DSB[:, m, :]
                gc = pcol(l, gcol + m)
                S.op("dve", lambda e, dv=dv, gc=gc: e.scalar_tensor_tensor(
                    out=dv, in0=dv, scalar=gc, in1=R2, op0=ALU.mult, op1=ALU.mult),
                    reads=[R2T, PART] + DSBT, writes=DSBT)
                xv = X[:, m, T0:T0 + TG]
                S.op("dve", lambda e, dv=dv, xv=xv: e.scalar_tensor_tensor(
                    out=xv, in0=dv, scalar=half, in1=xv, op0=ALU.mult, op1=ALU.add),
                    reads=DSBT + [XT[m][gtg]], writes=[XT[m][gtg]])

        def ffn_chain(stages, on_post=None):
            from collections import deque
            S.barrier()
            A.off = phase_base
            o_xn = A.alloc(KC * 1024 * 2)
            o_h = A.alloc(NJ * 1024 * 2)
            o_sq = [A.alloc(TG * 2) for _ in range(4)]
            o_rstd = A.alloc(1024 * 4)
            o_r2 = [A.alloc(TG * 4) for _ in range(2)]
            o_st = [A.alloc(TG * 2) for _ in range(2)]
            o_dsb = [A.alloc(KC * TG * 4) for _ in range(2)]
            assert A.off <= ARENA_BYTES, A.off
            XN = A.view(o_xn, BF16, [KC, 1024])
            XNT = [T("xn0"), T("xn1")]
            H = A.view(o_h, BF16, [NJ, 1024])
            HT = [[T(f"h{j}_{g}") for g in range(2)] for j in range(NJ)]
            SQ = [A.view(o, BF16, [TG]) for o in o_sq]
            SQT = [T(f"sq{q}") for q in range(4)]
            RSTD = A.view(o_rstd, F32, [1024])
            RSTDT = [T("rstd0"), T("rstd1")]
            R2 = [A.view(o, F32, [TG]) for o in o_r2]
            R2T = [T("r2_0"), T("r2_1")]
            ST = [A.view(o, BF16, [TG]) for o in o_st]
            STT = [T("st0"), T("st1")]
            DSB = [A.view(o, F32, [KC, TG]) for o in o_dsb]
            DSBT = [[T("dsb0")], [T("dsb1")]]
            blocks = [(l, i, hf, s_, last_) for (l, i, s_, last_) in stages for hf in range(2)]
            nb_ = len(blocks)
            bg = deque()
            cnt = {"gu": 0, "d": 0}

            def drain(n):
                for _ in range(n):
                    if not bg:
                        return
                    bg.popleft()()

            def pre_stats(k):
                l, i, hf = blocks[k][:3]
                t0 = hf * 1024
                out = []
                for tg in range(2):
                    T0 = t0 + tg * TG
                    gtg = T0 // TG
                    sb = 6 + tg
                    for c in range(KC):
                        def f(c=c, T0=T0, gtg=gtg, sb=sb):
                            q = state["sq"] % 4
                            state["sq"] += 1
                            sqv, sqt = SQ[q], SQT[q]
                            xin = X[:, c, T0:T0 + TG]
                            S.op("act", lambda e: e.activation(out=sqv, in_=xin, func=AF.Square),
                                 reads=[XT[c][gtg]], writes=[sqt])
                            S.op("pe", lambda e: e.matmul(bank(sb), lhsT=ONES, rhs=sqv, start=(c == 0), stop=(c == KC - 1)),
                                 reads=[sqt, ONEST], writes=[PST[sb]])
                        out.append(f)

                    def g(tg=tg, sb=sb):
                        rstd_from_stats(sb, RSTD[:, tg * TG:(tg + 1) * TG], RSTDT[tg], TG, D_MODEL)
                    out.append(g)
                return out

            def pre_xn(k):
                l, i, hf = blocks[k][:3]
                gpre = 0 if i == 0 else 32
                t0 = hf * 1024
                for tg in range(2):
                    T0 = t0 + tg * TG
                    gtg = T0 // TG
                    rs = RSTD[:, tg * TG:(tg + 1) * TG]
                    for c in range(KC):
                        xin = X[:, c, T0:T0 + TG]
                        xo = XN[:, c, tg * TG:(tg + 1) * TG]
                        gc = pcol(l, gpre + c)
                        S.op("dve", lambda e, xo=xo, xin=xin, gc=gc, rs=rs: e.scalar_tensor_tensor(
                            out=xo, in0=xin, scalar=gc, in1=rs, op0=ALU.mult, op1=ALU.mult),
                            reads=[XT[c][gtg], RSTDT[tg], PART], writes=[XNT[tg]])

            def post_apply(k):
                l, i, hf = blocks[k][:3]
                gpost = 8 if i == 0 else 40
                out = []
                for tg in range(2):
                    T0 = hf * 1024 + tg * TG
                    gtg = T0 // TG

                    def g(tg=tg):
                        rstd_from_stats(6 + tg, R2[tg], R2T[tg], TG, D_MODEL)
                    out.append(g)
                    for m in range(KC):
                        def f(m=m, tg=tg, T0=T0, gtg=gtg):
                            dv = DSB[tg][:, m, :]
                            gc = pcol(l, gpost + m)
                            S.op("dve", lambda e: e.scalar_tensor_tensor(
                                out=dv, in0=dv, scalar=gc, in1=R2[tg], op0=ALU.mult, op1=ALU.mult),
                                reads=[R2T[tg], PART] + DSBT[tg], writes=DSBT[tg])
                            xv = X[:, m, T0:T0 + TG]
                            S.op("dve", lambda e: e.scalar_tensor_tensor(
                                out=xv, in0=dv, scalar=0.5, in1=xv, op0=ALU.mult, op1=ALU.add),
                                reads=DSBT[tg] + [XT[m][gtg]], writes=[XT[m][gtg]])
                        out.append(f)
                return out

            for f in pre_stats(0):
                f()
            pre_xn(0)
            for k in range(nb_):
                l, i, hf = blocks[k][:3]
                if k + 1 < nb_:
                    bg.extend(pre_stats(k + 1))
                for j in range(NJ):
                    wv, wt = RG.get()
                    W = wv.rearrange("p (a k m) -> p a k m", a=2, k=KC, m=128)
                    for tg in range(2):
                        pair = cnt["gu"] % 2
                        cnt["gu"] += 1
                        bg_, bu = 2 * pair, 2 * pair + 1
                        for a_, b in ((0, bg_), (1, bu)):
                            for kc in range(KC):
                                S.op("pe", lambda e, a_=a_, b=b, kc=kc, W=W, tg=tg: e.matmul(
                                    bank(b), lhsT=W[:, a_, kc, :], rhs=XN[:, kc, tg * TG:(tg + 1) * TG],
                                    start=(kc == 0), stop=(kc == KC - 1)),
                                    reads=[wt, XNT[tg]], writes=[PST[b]])
                        sv, stt = ST[pair], STT[pair]
                        S.op("act", lambda e, sv=sv, bg_=bg_: e.activation(out=sv, in_=bank(bg_), func=AF.Silu),
                             reads=[PST[bg_]], writes=[stt])
                        ho = H[:, j, tg * TG:(tg + 1) * TG]
                        S.op("dve", lambda e, ho=ho, sv=sv, bu=bu: e.tensor_tensor(
                            out=ho, in0=bank(bu), in1=sv, op=ALU.mult),
                            reads=[PST[bu], stt], writes=[HT[j][tg]])
                        drain(2)
                    RG.done()
                drain(10 ** 6)
                if k + 1 < nb_:
                    pre_xn(k + 1)
                pend = []
                for m in range(KC):
                    wv, wt = RD.get()
                    Wd = wv.rearrange("p (j c) -> p j c", j=NJ, c=128)
                    for tg in range(2):
                        b = 4 + cnt["d"] % 2
                        cnt["d"] += 1
                        for j in range(NJ):
                            S.op("pe", lambda e, b=b, j=j, Wd=Wd, tg=tg: e.matmul(
                                bank(b), lhsT=Wd[:, j, :], rhs=H[:, j, tg * TG:(tg + 1) * TG],
                                start=(j == 0), stop=(j == NJ - 1)),
                                reads=[wt, HT[j][tg]], writes=[PST[b]])
                        dv = DSB[tg][:, m, :]
                        S.op("act", lambda e, dv=dv, b=b: e.activation(out=dv, in_=bank(b), func=AF.Copy),
                             reads=[PST[b]], writes=DSBT[tg])
                        q = state["sq"] % 4
                        state["sq"] += 1
                        sqv, sqt = SQ[q], SQT[q]
                        S.op("act", lambda e, sqv=sqv, b=b: e.activation(out=sqv, in_=bank(b), func=AF.Square),
                             reads=[PST[b]], writes=[sqt])
                        for f in pend:
                            f()
                        pend = []
                        sb = 6 + tg

                        def mk(sb=sb, sqv=sqv, sqt=sqt, m=m):
                            S.op("pe", lambda e: e.matmul(bank(sb), lhsT=ONES, rhs=sqv,
                                                          start=(m == 0), stop=(m == KC - 1)),
                                 reads=[sqt, ONEST], writes=[PST[sb]])
                        pend.append(mk)
                    RD.done()
                for f in pend:
                    f()
                bg.extend(post_apply(k))
                if blocks[k][4]:
                    bg.append(lambda k=k: on_post(blocks[k][3], blocks[k][2]))
            drain(10 ** 6)

        def load_mixw(l):
            base = sum(WIN_ITEMS) + 4 * WO_ITEM
            segs = [(0, NWUQ), (NWUQ, 512), (NWUQ + 512, 512), (NWUQ + 1024, 512)]
            for (o, n), t in zip(segs, MWT):
                S.op("pool", lambda e, o=o, n=n: e.dma_start(out=MIXW[:, o:o + n], in_=wmix[l][:, base + o:base + o + n]),
                     writes=[t], dma=True, dma_tile=t)
            S.op("pool", lambda e: e.dma_start(out=ROPE, in_=rope.rearrange("p (a t) -> p a t", a=2)),
                 writes=[ROPET], dma=True, dma_tile=ROPET)

        def mix_stage(l):
            S.barrier()
            load_mixw(l)
            A.off = phase_base
            o_xn = A.alloc(32768)
            o_yc = A.alloc(32768)
            o_tmp = A.off
            XN = A.view(o_xn, BF16, [KC, SEQ])
            XNT = [T(f"mxn{g}") for g in range(NTG)]
            YC = A.view(o_yc, BF16, [KC, SEQ])
            YCT = [[T(f"yc{c}_{g}") for g in range(NTG)] for c in range(KC)]
            o_rstd = A.alloc(SEQ * 4)
            o_sq = [A.alloc(TG * 2) for _ in range(4)]
            RSTD = A.view(o_rstd, F32, [SEQ])
            RSTDT = [T(f"mrstd{g}") for g in range(NTG)]
            SQ = [A.view(o, BF16, [TG]) for o in o_sq]
            SQT = [T(f"msq{q}") for q in range(4)]
            prenorm(l, 16, 0, NTG, XN, XNT, RSTD, RSTDT, SQ, SQT)
            o_xcs = [A.alloc(TG * 4) for _ in range(2)]
            o_zc = A.alloc((SEQ + 2) * 2)
            o_gb = A.alloc(SEQ * 2)
            o_dg = A.alloc(6 * 128 * 2)
            assert A.off <= MIX_LIMIT
            XCS = [A.view(o, F32, [TG]) for o in o_xcs]
            XCST = [T("xcs0"), T("xcs1")]
            ZC = A.view(o_zc, BF16, [SEQ + 2])
            ZCT = [T(f"zc{g}") for g in range(NTG)]
            ZPAD = T("zpad")
            GB = A.view(o_gb, BF16, [SEQ])
            GBT = [T(f"gb{g}") for g in range(NTG)]
            DG = A.view(o_dg, BF16, [6, 128])
            DGT = T("diag")
            for q in range(6):
                S.op("dve", lambda e, q=q: e.tensor_scalar(out=DG[:, q, :], in0=IDENT, scalar1=pcol(l, 48 + q),
                                                           scalar2=None, op0=ALU.mult),
                     reads=[CSTT, PART], writes=[DGT])
            S.op("dve", lambda e: e.memset(ZC[:, 0:1], 0.0), writes=[ZPAD])
            S.op("dve", lambda e: e.memset(ZC[:, SEQ + 1:SEQ + 2], 0.0), writes=[ZPAD])
            items = [RG.get() for _ in range(3)]
            Wi = [(v.rearrange("p (k c) -> p k c", k=KC, c=256), t) for (v, t) in items]
            sel = {0: ((0, 0), (0, 128), (1, 0)), 1: ((1, 128), (2, 0), (2, 128))}
            cnt = 0
            for cc in range(2):
                for tg in range(NTG):
                    bset = (cnt % 2) * 3
                    cnt += 1
                    for r_, (it, co) in enumerate(sel[cc]):
                        W, wt = Wi[it]
                        b = bset + r_
                        for kc in range(KC):
                            S.op("pe", lambda e, b=b, W=W, co=co, kc=kc, tg=tg: e.matmul(
                                bank(b), lhsT=W[:, kc, co:co + 128], rhs=XN[:, kc, tg * TG:(tg + 1) * TG],
                                start=(kc == 0), stop=(kc == KC - 1)),
                                reads=[wt, XNT[tg]], writes=[PST[b]])
                    xs, xst = XCS[tg % 2], XCST[tg % 2]
                    S.op("act", lambda e, xs=xs, b=bset: e.activation(out=xs, in_=bank(b), func=AF.Copy),
                         reads=[PST[bset]], writes=[xst])
                    zo = ZC[:, 1 + tg * TG:1 + (tg + 1) * TG]
                    S.op("dve", lambda e, zo=zo, xs=xs, b=bset + 1: e.tensor_tensor(out=zo, in0=bank(b), in1=xs, op=ALU.mult),
                         reads=[PST[bset + 1], xst], writes=[ZCT[tg]])
                    go = GB[:, tg * TG:(tg + 1) * TG]
                    S.op("act", lambda e, go=go, b=bset + 2: e.activation(out=go, in_=bank(b), func=AF.Copy),
                         reads=[PST[bset + 2]], writes=[GBT[tg]])
                for tg in range(NTG):
                    b = 6 + tg % 2
                    rd = [ZCT[g] for g in (tg - 1, tg, tg + 1) if 0 <= g < NTG] + [ZPAD, DGT]
                    for k in range(3):
                        S.op("pe", lambda e, b=b, k=k, cc=cc, tg=tg: e.matmul(
                            bank(b), lhsT=DG[:, cc * 3 + k, :], rhs=ZC[:, tg * TG + k:tg * TG + k + TG],
                            start=(k == 0), stop=(k == 2)),
                            reads=rd, writes=[PST[b]])
                    yo = YC[:, cc, tg * TG:(tg + 1) * TG]
                    go = GB[:, tg * TG:(tg + 1) * TG]
                    S.op("dve", lambda e, yo=yo, go=go, b=b, cc=cc: e.scalar_tensor_tensor(
                        out=yo, in0=bank(b), scalar=pcol(l, 54 + cc), in1=go, op0=ALU.add, op1=ALU.mult),
                        reads=[PST[b], GBT[tg], PART], writes=[YCT[cc][tg]])
            RG.done(3)
            S.barrier()
            A.off = o_tmp
            o_gv = A.alloc(16 * 256 * 2)
            o_vt = A.alloc(16 * 256 * 2)
            o_ut = [A.alloc(TG * 2) for _ in range(2)]
            o_ss = A.alloc(64)
            o_rv = A.alloc(64)
            o_jk = A.alloc(256 * 2)
            o_tm = [A.alloc(TG * 4) for _ in range(2)]
            assert A.off <= MIX_LIMIT
            GV = A.view(o_gv, BF16, [16, 256])
            GVT = [T(f"gv{i}") for i in range(16)]
            VT_ = A.view(o_vt, BF16, [16, 256])
            VTT = [T(f"vt{i}") for i in range(16)]
            UT = [A.view(o, BF16, [TG]) for o in o_ut]
            UTT = [T("ut0"), T("ut1")]
            SS = A.view(o_ss, F32, [16])
            SST = T("ss")
            RV = A.view(o_rv, F32, [16])
            RVT = T("rv")
            JK = A.view(o_jk, BF16, [256])
            JKT = T("jk")
            TM = [A.view(o, F32, [TG]) for o in o_tm]
            TMT = [T("tm0"), T("tm1")]
            (i3v, i3t) = RG.get()
            (i4v, i4t) = RG.get()
            W3 = i3v.rearrange("p (k c) -> p k c", k=KC, c=256)
            W4 = i4v.rearrange("p (k c) -> p k c", k=KC, c=256)
            for tt in range(16):
                b = 4 + tt % 2
                for kc in range(KC):
                    S.op("pe", lambda e, b=b, kc=kc, tt=tt: e.matmul(
                        bank(b)[:, 0:256], lhsT=XN[:, kc, tt * 128:(tt + 1) * 128], rhs=W4[:, kc, :],
                        start=(kc == 0), stop=(kc == KC - 1)),
                        reads=[i4t, XNT[tt // 4]], writes=[PST[b]])
                S.op("act", lambda e, b=b, tt=tt: e.activation(out=GV[:, tt, :], in_=bank(b)[:, 0:256], func=AF.Gelu),
                     reads=[PST[b]], writes=[GVT[tt]])
                S.op("act", lambda e, tt=tt: e.activation(out=JK, in_=GV[:, tt, :], func=AF.Square,
                                                          accum_out=SS[:, tt:tt + 1]),
                     reads=[GVT[tt]], writes=[JKT, SST])
            S.op("act", lambda e: e.activation(out=RV, in_=SS, func=AF.Ln, bias=EPSC, scale=1.0 / 256),
                 reads=[SST, CSTT2], writes=[RVT])
            S.op("act", lambda e: e.activation(out=RV, in_=RV, func=AF.Exp, scale=-0.5), reads=[RVT], writes=[RVT])
            GN = PAR[:, l, 64:320]
            for tt in range(16):
                S.op("dve", lambda e, tt=tt: e.scalar_tensor_tensor(
                    out=VT_[:, tt, :], in0=GV[:, tt, :], scalar=RV[:, tt:tt + 1], in1=GN, op0=ALU.mult, op1=ALU.mult),
                    reads=[GVT[tt], RVT, PART], writes=[VTT[tt]])
            cnt = 0
            for fc in range(2):
                for tg in range(NTG):
                    bm = 6 + cnt % 2
                    bz = cnt % 4
                    sl = cnt % 2
                    cnt += 1
                    for ch in range(4):
                        c16 = tg * 4 + ch
                        for gi in range(2):
                            g = 2 * fc + gi
                            S.op("pe", lambda e, bm=bm, gi=gi, ch=ch, c16=c16, g=g: e.matmul(
                                bank(bm)[gi * 64:(gi + 1) * 64, ch * 128:(ch + 1) * 128],
                                lhsT=VT_[:, c16, g * 64:(g + 1) * 64], rhs=WST[:, g, :], start=True, stop=True),
                                reads=[VTT[c16], MWT[3]], writes=[PST[bm]])
                    for kc in range(KC):
                        S.op("pe", lambda e, bz=bz, kc=kc, fc=fc, tg=tg: e.matmul(
                            bank(bz), lhsT=W3[:, kc, fc * 128:(fc + 1) * 128], rhs=XN[:, kc, tg * TG:(tg + 1) * TG],
                            start=(kc == 0), stop=(kc == KC - 1)),
                            reads=[i3t, XNT[tg]], writes=[PST[bz]])
                    S.op("act", lambda e, bz=bz, sl=sl: e.activation(out=UT[sl], in_=bank(bz), func=AF.Gelu),
                         reads=[PST[bz]], writes=[UTT[sl]])
                    for ch in range(4):
                        S.op("dve", lambda e, bm=bm, sl=sl, ch=ch, fc=fc: e.tensor_tensor(
                            out=TM[sl][:, ch * 128:(ch + 1) * 128], in0=bank(bm)[:, ch * 128:(ch + 1) * 128],
                            in1=PAR[:, l, 320 + fc * 128:320 + (fc + 1) * 128], op=ALU.add),
                            reads=[PST[bm], PART], writes=[TMT[sl]])
                    yo = YC[:, 6 + fc, tg * TG:(tg + 1) * TG]
                    S.op("dve", lambda e, yo=yo, sl=sl: e.tensor_tensor(out=yo, in0=TM[sl], in1=UT[sl], op=ALU.mult),
                         reads=[TMT[sl], UTT[sl]], writes=[YCT[6 + fc][tg]])
            RG.done(2)
            S.barrier()
            A.off = o_tmp
            o_cqn = A.alloc(2 * SEQ * 2)
            o_ckvn = A.alloc(SEQ * 2)
            o_kro = A.alloc(SEQ * 2)
            o_rq = A.alloc(TG * 4)
            o_rkv = A.alloc(TG * 4)
            o_t1 = A.alloc(TG * 4)
            o_t2 = A.alloc(TG * 4)
            o_sq2 = [A.alloc(TG * 2) for _ in range(3)]
            assert A.off <= MIX_LIMIT, A.off
            CQN = A.view(o_cqn, BF16, [2, SEQ])
            CQNT = [T(f"cqn{g}") for g in range(NTG)]
            CKVN = A.view(o_ckvn, BF16, [SEQ])
            CKVNT = [T(f"ckvn{g}") for g in range(NTG)]
            KRO = A.view(o_kro, BF16, [SEQ])
            KROT = [T(f"kro{g}") for g in range(NTG)]
            RQ = A.view(o_rq, F32, [TG])
            RQT = T("rq")
            RKV = A.view(o_rkv, F32, [TG])
            RKVT = T("rkv")
            T1 = A.view(o_t1, F32, [TG])
            T1T = T("t1")
            T2 = A.view(o_t2, F32, [TG])
            T2T = T("t2")
            SQ2 = [A.view(o, BF16, [TG]) for o in o_sq2]
            SQ2T = [T(f"sq2_{q}") for q in range(3)]
            (i5v, i5t) = RG.get()
            (i6v, i6t) = RG.get()
            (i7v, i7t) = RG.get()
            W5 = i5v.rearrange("p (k c) -> p k c", k=KC, c=256)
            W6 = i6v[:, 0:KC * 224].rearrange("p (k c) -> p k c", k=KC, c=224)
            W7 = i7v[:, 0:KC * 96].rearrange("p (k c) -> p k c", k=KC, c=96)
            for tg in range(NTG):
                tsl = slice(tg * TG, (tg + 1) * TG)
                projs = [(0, W5, i5t, 0, 128), (1, W5, i5t, 128, 128), (2, W6, i6t, 0, 128),
                         (3, W6, i6t, 128, 96), (4, W7, i7t, 0, 96)]
                for (b, W, wt, co, mw) in projs:
                    for kc in range(KC):
                        S.op("pe", lambda e, b=b, W=W, co=co, mw=mw, kc=kc, tsl=tsl: e.matmul(
                            bank(b)[0:mw, :], lhsT=W[:, kc, co:co + mw], rhs=XN[:, kc, tsl],
                            start=(kc == 0), stop=(kc == KC - 1)),
                            reads=[wt, XNT[tg]], writes=[PST[b]])
                for c in range(3):
                    S.op("act", lambda e, c=c: e.activation(out=SQ2[c], in_=bank(c), func=AF.Square),
                         reads=[PST[c]], writes=[SQ2T[c]])
                for c in range(2):
                    S.op("pe", lambda e, c=c: e.matmul(bank(6), lhsT=ONES, rhs=SQ2[c], start=(c == 0), stop=(c == 1)),
                         reads=[SQ2T[c], ONEST], writes=[PST[6]])
                S.op("pe", lambda e: e.matmul(bank(7), lhsT=ONES, rhs=SQ2[2], start=True, stop=True),
                     reads=[SQ2T[2], ONEST], writes=[PST[7]])
                rstd_from_stats(6, RQ, RQT, TG, 256)
                rstd_from_stats(7, RKV, RKVT, TG, 128)
                for c in range(2):
                    S.op("dve", lambda e, c=c, tsl=tsl: e.scalar_tensor_tensor(
                        out=CQN[:, c, tsl], in0=bank(c), scalar=pcol(l, 56 + c), in1=RQ, op0=ALU.mult, op1=ALU.mult),
                        reads=[PST[c], RQT, PART], writes=[CQNT[tg]])
                S.op("dve", lambda e, tsl=tsl: e.scalar_tensor_tensor(
                    out=CKVN[:, tsl], in0=bank(2), scalar=pcol(l, 58), in1=RKV, op0=ALU.mult, op1=ALU.mult),
                    reads=[PST[2], RKVT, PART], writes=[CKVNT[tg]])
                S.op("dve", lambda e, tsl=tsl: e.tensor_tensor(out=T1[64:96, :], in0=bank(3)[64:96, :], in1=COS[64:96, tsl],
                                                               op=ALU.mult),
                     reads=[PST[3], ROPET], writes=[T1T])
                S.op("dve", lambda e, tsl=tsl: e.tensor_tensor(out=T2[64:96, :], in0=bank(4)[64:96, :], in1=SIN[64:96, tsl],
                                                               op=ALU.mult),
                     reads=[PST[4], ROPET], writes=[T2T])
                S.op("dve", lambda e, tsl=tsl: e.tensor_tensor(out=KRO[64:96, tsl], in0=T1[64:96, :], in1=T2[64:96, :],
                                                               op=ALU.add),
                     reads=[T1T, T2T], writes=[KROT[tg]])
            RG.done(3)
            S.barrier()
            A.off = o_xn
            o_va = [A.alloc(16 * 128 * 2) for _ in range(2)]
            o_qh = [A.alloc(SEQ * 2) for _ in range(2)]
            o_kh = [A.alloc(SEQ * 2) for _ in range(2)]
            o_pt = [A.alloc(1024 * 2) for _ in range(3)]
            o_r = A.alloc(TG * 4)
            assert A.off <= o_yc, A.off
            VA = [A.view(o, BF16, [16, 128]) for o in o_va]
            VAT = [T("va0"), T("va1")]
            QH = [A.view(o, BF16, [SEQ]) for o in o_qh]
            QHT = [[T(f"qh{p}_{g}") for g in range(NTG)] for p in range(2)]
            KH = [A.view(o, BF16, [SEQ]) for o in o_kh]
            KHT = [[T(f"kh{p}_{g}") for g in range(NTG)] for p in range(2)]
            PTB = [A.view(o, BF16, [1024]) for o in o_pt]
            PTT = [T(f"pt{i}") for i in range(3)]
            R = A.view(o_r, F32, [TG])
            RT = T("r")
            S.op("dve", lambda e: e.memset(VA[0][:, :, 64:128], 1.0), writes=[VAT[0]])
            S.op("dve", lambda e: e.memset(VA[1][:, :, 0:64], 1.0), writes=[VAT[1]])
            pcnt = {"s": 0, "o": 0}

            def head_prep(h):
                par_ = h % 2
                out = []
                for tg in range(NTG):
                    tsl = slice(tg * TG, (tg + 1) * TG)

                    def f_a(tg=tg, tsl=tsl):
                        for kc in range(2):
                            S.op("pe", lambda e, kc=kc: e.matmul(
                                bank(4)[0:96, :], lhsT=WUQ[:, h, 0, kc, :], rhs=CQN[:, kc, tsl],
                                start=(kc == 0), stop=(kc == 1)),
                                reads=[MWT[0], CQNT[tg]], writes=[PST[4]])
                        S.op("dve", lambda e: e.tensor_copy(out=QH[par_][0:64, tsl], in_=bank(4)[0:64, :]),
                             reads=[PST[4]], writes=[QHT[par_][tg]])
                        S.op("dve", lambda e: e.tensor_tensor(out=T1[64:96, :], in0=bank(4)[64:96, :],
                                                              in1=COS[64:96, tsl], op=ALU.mult),
                             reads=[PST[4], ROPET], writes=[T1T])

                    def f_b(tg=tg, tsl=tsl):
                        for kc in range(2):
                            S.op("pe", lambda e, kc=kc: e.matmul(
                                bank(5)[0:96, :], lhsT=WUQ[:, h, 1, kc, :], rhs=CQN[:, kc, tsl],
                                start=(kc == 0), stop=(kc == 1)),
                                reads=[MWT[0], CQNT[tg]], writes=[PST[5]])
                        S.op("dve", lambda e: e.tensor_tensor(out=T2[64:96, :], in0=bank(5)[64:96, :],
                                                              in1=SIN[64:96, tsl], op=ALU.mult),
                             reads=[PST[5], ROPET], writes=[T2T])
                        S.op("dve", lambda e: e.tensor_tensor(out=QH[par_][64:96, tsl], in0=T1[64:96, :], in1=T2[64:96, :],
                                                              op=ALU.add),
                             reads=[T1T, T2T], writes=[QHT[par_][tg]])

                    def f_k(tg=tg, tsl=tsl):
                        S.op("pe", lambda e: e.matmul(
                            bank(4)[0:64, :], lhsT=WK[:, h * 64:(h + 1) * 64], rhs=CKVN[:, tsl], start=True, stop=True),
                            reads=[MWT[1], CKVNT[tg]], writes=[PST[4]])
                        S.op("dve", lambda e: e.tensor_copy(out=KH[par_][0:64, tsl], in_=bank(4)[0:64, :]),
                             reads=[PST[4]], writes=[KHT[par_][tg]])
                        S.op("dve", lambda e: e.tensor_copy(out=KH[par_][64:96, tsl], in_=KRO[64:96, tsl]),
                             reads=[KROT[tg]], writes=[KHT[par_][tg]])
                    out += [f_a, f_b, f_k]
                vc0 = 0 if par_ == 0 else 64
                for half8 in range(2):
                    def f_v(half8=half8):
                        for t8 in range(8):
                            tt = half8 * 8 + t8
                            S.op("pe", lambda e, t8=t8, tt=tt: e.matmul(
                                bank(5)[:, t8 * 64:(t8 + 1) * 64], lhsT=CKVN[:, tt * 128:(tt + 1) * 128],
                                rhs=WV[:, h * 64:(h + 1) * 64], start=True, stop=True),
                                reads=[CKVNT[tt // 4], MWT[2]], writes=[PST[5]])
                        dst = VA[par_][:, half8 * 8:(half8 + 1) * 8, vc0:vc0 + 64]
                        srcv = bank(5).rearrange("p (a b) -> p a b", a=8, b=64)
                        S.op("dve", lambda e: e.tensor_copy(out=dst, in_=srcv), reads=[PST[5]], writes=[VAT[par_]])
                    out.append(f_v)
                return out

            for f in head_prep(0):
                f()
            gsteps = [(h, qg, kp) for h in range(HEADS) for qg in range(NTG) for kp in range(8)]
            ng = len(gsteps)

            def rec_S(g):
                h, qg, kp = gsteps[g]
                par_ = h % 2
                sl = g % 2
                for hf in range(2):
                    kc = 2 * kp + hf
                    S.op("pe", lambda e, sl=sl, hf=hf, kc=kc, qg=qg, par_=par_: e.matmul(
                        PS[sl][:, hf * TG:(hf + 1) * TG], lhsT=KH[par_][0:96, kc * 128:(kc + 1) * 128],
                        rhs=QH[par_][0:96, qg * TG:(qg + 1) * TG], start=True, stop=True),
                        reads=[KHT[par_][kc // 4], QHT[par_][qg]], writes=[PST[2 * sl + hf]])

            def rec_exp(g):
                sl = g % 2
                ptsl = g % 3
                S.op("act", lambda e, sl=sl, ptsl=ptsl: e.activation(out=PTB[ptsl], in_=PS[sl][:, 0:1024], func=AF.Exp,
                                                                   scale=SCALE),
                     reads=[PST[2 * sl], PST[2 * sl + 1]], writes=[PTT[ptsl]])

            def rec_pv(g):
                h, qg, kp = gsteps[g]
                ptsl = g % 3
                ob = 6 + (g // 8) % 2
                for hf in range(2):
                    kc = 2 * kp + hf
                    S.op("pe", lambda e, kc=kc, hf=hf, ptsl=ptsl, ob=ob, h=h, kp=kp: e.matmul(
                        bank(ob), lhsT=VA[h % 2][:, kc, :], rhs=PTB[ptsl][:, hf * TG:(hf + 1) * TG],
                        start=(kp == 0 and hf == 0), stop=(kp == 7 and hf == 1)),
                        reads=[VAT[h % 2], PTT[ptsl]], writes=[PST[ob]])
                if kp == 7:
                    tsl = slice(qg * TG, (qg + 1) * TG)
                    lo, hi = (slice(0, 64), slice(64, 128)) if h % 2 == 0 else (slice(64, 128), slice(0, 64))
                    S.op("act", lambda e, ob=ob, lo=lo, hi=hi: e.activation(out=R[lo, :], in_=bank(ob)[hi, :], func=AF.Ln),
                         reads=[PST[ob]], writes=[RT])
                    S.op("act", lambda e, lo=lo: e.activation(out=R[lo, :], in_=R[lo, :], func=AF.Exp, scale=-1.0),
                         reads=[RT], writes=[RT])
                    S.op("dve", lambda e, ob=ob, lo=lo, tsl=tsl, h=h: e.tensor_tensor(
                        out=YC[lo, 2 + h // 2, tsl], in0=bank(ob)[lo, :], in1=R[lo, :], op=ALU.mult),
                        reads=[PST[ob], RT], writes=[YCT[2 + h // 2][qg]])

            nxt = []
            rec_S(0)
            for g in range(ng):
                h, qg, kp = gsteps[g]
                if qg == 0 and kp == 0:
                    for f in nxt:
                        f()
                    nxt = head_prep(h + 1) if h + 1 < HEADS else []
                if g + 1 < ng:
                    if gsteps[g + 1][0] != h:
                        for f in nxt:
                            f()
                        nxt = []
                    rec_S(g + 1)
                rec_exp(g)
                if g >= 1:
                    rec_pv(g - 1)
                if g % 2 == 1 and nxt:
                    nxt.pop(0)()
            rec_pv(ng - 1)
            if dump_yc:
                S.barrier()
                for c in range(KC):
                    for tg in range(NTG):
                        tsl = slice(tg * TG, (tg + 1) * TG)
                        S.op("dve", lambda e, c=c, tsl=tsl: e.tensor_copy(out=X[:, c, tsl], in_=YC[:, c, tsl]),
                             reads=[YCT[c][tg]], writes=[XT[c][tg]])
                return
            S.barrier()
            A.off = o_xn
            o_dsb = [A.alloc(KC * TG * 4) for _ in range(2)]
            DSB = [A.view(o, F32, [KC, TG]) for o in o_dsb]
            DSBT = [[T("mdsb0")], [T("mdsb1")]]
            A.off = o_tmp
            o_r2 = [A.alloc(TG * 4) for _ in range(2)]
            o_sq3 = [A.alloc(TG * 2) for _ in range(4)]
            R2 = [A.view(o, F32, [TG]) for o in o_r2]
            R2T = [T("mr2_0"), T("mr2_1")]
            SQ3 = [A.view(o, BF16, [TG]) for o in o_sq3]
            SQ3T = [T(f"sq3_{q}") for q in range(4)]
            wo = [RG.get() for _ in range(4)]
            WO = [(v.rearrange("p (k c) -> p k c", k=KC, c=256), t) for (v, t) in wo]
            cnt = 0
            for tg in range(NTG):
                tsl = slice(tg * TG, (tg + 1) * TG)
                sb = 6 + tg % 2
                pend = []
                for m in range(KC):
                    W, wt = WO[m // 2]
                    co = (m % 2) * 128
                    b = cnt % 4
                    cnt += 1
                    for kc in range(KC):
                        S.op("pe", lambda e, b=b, W=W, co=co, kc=kc, tsl=tsl: e.matmul(
                            bank(b), lhsT=W[:, kc, co:co + 128], rhs=YC[:, kc, tsl], start=(kc == 0), stop=(kc == KC - 1)),
                            reads=[wt, YCT[kc][tg]], writes=[PST[b]])
                    dv = DSB[tg % 2][:, m, :]
                    S.op("act", lambda e, dv=dv, b=b: e.activation(out=dv, in_=bank(b), func=AF.Copy),
                         reads=[PST[b]], writes=DSBT[tg % 2])
                    q = cnt % 4
                    sqv, sqt = SQ3[q], SQ3T[q]
                    S.op("act", lambda e, sqv=sqv, b=b: e.activation(out=sqv, in_=bank(b), func=AF.Square),
                         reads=[PST[b]], writes=[sqt])
                    for f in pend:
                        f()
                    pend = []

                    def mk(sb=sb, sqv=sqv, sqt=sqt, m=m):
                        S.op("pe", lambda e: e.matmul(bank(sb), lhsT=ONES, rhs=sqv, start=(m == 0), stop=(m == KC - 1)),
                             reads=[sqt, ONEST], writes=[PST[sb]])
                    pend.append(mk)
                for f in pend:
                    f()
                postnorm_apply(l, 24, tg * TG, DSB[tg % 2], DSBT[tg % 2], sb, R2[tg % 2], R2T[tg % 2], 1.0)
            RG.done(4)

        all_stores = []

        def x_load(s, hf):
            for c in range(KC):
                S.op("sp", lambda e, s=s, c=c, hf=hf: e.dma_start(
                    out=X[:, c, hf * 1024:(hf + 1) * 1024], in_=xT[s][:, c * SEQ + hf * 1024:c * SEQ + (hf + 1) * 1024]),
                    writes=XT[c][2 * hf:2 * hf + 2], dma=True, dma_tile=XLD[c][hf], nobarrier=True)

        def x_store(s, hf):
            for c in range(KC):
                all_stores.append(S.op("sp", lambda e, s=s, c=c, hf=hf: e.dma_start(
                    out=yT[s][:, c * SEQ + hf * 1024:c * SEQ + (hf + 1) * 1024], in_=X[:, c, hf * 1024:(hf + 1) * 1024]),
                    reads=XT[c][2 * hf:2 * hf + 2], dma=True, dma_tile=XST[c][hf], nobarrier=True))

        x_load(0, 0)
        x_load(0, 1)

        def on_post(s, hf):
            x_store(s, hf)
            if s + 1 < nseq:
                x_load(s + 1, hf)

        glist = []
        for s in range(nseq):
            for pi, (l, stg) in enumerate(plan):
                glist.append((s, l, stg, pi == len(plan) - 1))
        gi = 0
        while gi < len(glist):
            s, l, stg, last = glist[gi]
            if stg == "mix":
                mix_stage(l)
                gi += 1
                if last:
                    S.barrier()
                    for hf in range(2):
                        on_post(s, hf)
            else:
                chain = []
                while gi < len(glist) and glist[gi][2] != "mix":
                    s2, l2, stg2, last2 = glist[gi]
                    chain.append((l2, 0 if stg2 == "ffn1" else 1, s2, last2))
                    gi += 1
                ffn_chain(chain, on_post)
        S.op("sp", lambda e: e.nop(), after=all_stores)
        S.run()
    return nc


def _prep_shared(inp):
    L = DEPTH
    f = lambda a: np.ascontiguousarray(np.asarray(a, dtype=np.float32))
    wgu = np.empty((L, 2, NJ, 128, 2, KC, 128), np.float32)
    wdn = np.empty((L, 2, KC, 128, NJ, 128), np.float32)
    for i, (ngu, ndn) in enumerate((("ffn1_w_gu", "ffn1_w_down"), ("ffn2_w_gu", "ffn2_w_down"))):
        g = f(inp[ngu]).reshape(L, KC, 128, 2, NJ, 128)
        wgu[:, i] = g.transpose(0, 4, 2, 3, 1, 5)
        d = f(inp[ndn]).reshape(L, NJ, 128, KC, 128)
        wdn[:, i] = d.transpose(0, 3, 2, 1, 4)
    wgu = wgu.reshape(L * 2 * NJ, 128, 2048)
    wdn = wdn.reshape(L * 2 * KC, 128, NJ * 128)
    cst = np.eye(128, dtype=np.float32)

    w_in = f(inp["w_in"])
    r = np.arange
    kr = 1152
    col_sets = [
        np.concatenate([r(0, 128), r(512, 640)]),
        np.concatenate([r(256, 384), r(128, 256)]),
        np.concatenate([r(640, 768), r(384, 512)]),
        r(1184, 1440),
        r(1440, 1696),
        r(768, 1024),
        np.concatenate([r(1024, 1152)] + [r(kr, kr + 32)] * 3),
        np.concatenate([np.concatenate([r(kr + 16, kr + 32), r(kr, kr + 16)])] * 3),
    ]
    wmix = np.zeros((L, 128, NMIX), np.float32)
    off = 0
    for cs in col_sets:
        blk = w_in[:, :, cs].reshape(L, KC, 128, len(cs)).transpose(0, 2, 1, 3)
        n = KC * len(cs)
        wmix[:, :, off:off + n] = blk.reshape(L, 128, n)
        off += n
    w_out = f(inp["w_out"])
    for mp in range(4):
        blk = w_out[:, :, mp * 256:(mp + 1) * 256].reshape(L, KC, 128, 256).transpose(0, 2, 1, 3)
        wmix[:, :, off:off + WO_ITEM] = blk.reshape(L, 128, WO_ITEM)
        off += WO_ITEM
    w_uq = f(inp["w_uq"]).reshape(L, 2, 128, HEADS, QK_DIM)
    a0 = w_uq
    rot = np.concatenate([r(0, 64), r(80, 96), r(64, 80)])
    a1 = w_uq[..., rot]
    wuq = np.stack([a0, a1], 0).transpose(1, 3, 4, 0, 2, 5)
    wmix[:, :, off:off + NWUQ] = wuq.reshape(L, 128, NWUQ)
    off += NWUQ
    w_ukv = f(inp["w_ukv"]).reshape(L, 128, HEADS, 128)
    wmix[:, :, off:off + 512] = w_ukv[..., :64].reshape(L, 128, 512)
    off += 512
    wmix[:, :, off:off + 512] = w_ukv[..., 64:].reshape(L, 128, 512)
    off += 512
    ws = f(inp["gmlp_ws"])
    wmix[:, :, off:off + 512] = ws.transpose(0, 3, 1, 2).reshape(L, 128, 512)
    off += 512
    assert off == NMIX

    par = np.zeros((L, 128, NPAR), np.float32)
    for ci, nm in enumerate(("ffn1_pre_g", "ffn1_post_g", "mix_pre_g", "mix_post_g", "ffn2_pre_g", "ffn2_post_g")):
        par[:, :, ci * 8:(ci + 1) * 8] = f(inp[nm]).reshape(L, KC, 128).transpose(0, 2, 1)
    cw = f(inp["conv_w"]).reshape(L, 3, 2, 128)
    par[:, :, 48:54] = cw.transpose(0, 3, 2, 1).reshape(L, 128, 6)
    par[:, :, 54:56] = f(inp["conv_b"]).reshape(L, 2, 128).transpose(0, 2, 1)
    par[:, :, 56:58] = f(inp["q_norm_g"]).reshape(L, 2, 128).transpose(0, 2, 1)
    par[:, :, 58] = f(inp["kv_norm_g"])
    par[:, :, 64:320] = f(inp["gmlp_norm_g"])[:, None, :]
    gb = f(inp["gmlp_b"])
    for fc in range(2):
        par[:, 0:64, 320 + fc * 128:320 + (fc + 1) * 128] = gb[:, 2 * fc, None, :]
        par[:, 64:128, 320 + fc * 128:320 + (fc + 1) * 128] = gb[:, 2 * fc + 1, None, :]

    pos = np.arange(SEQ, dtype=np.float32)
    inv = (1.0 / (np.float32(10000.0) ** (np.arange(0, 32, 2, dtype=np.float32) / np.float32(32)))).astype(np.float32)
    ang = (pos[:, None] * inv[None, :]).astype(np.float32)
    cos = np.cos(ang).astype(np.float32).T
    sin = np.sin(ang).astype(np.float32).T
    rope = np.zeros((128, 2, SEQ), np.float32)
    rope[64:80, 0] = cos
    rope[80:96, 0] = cos
    rope[64:80, 1] = -sin
    rope[80:96, 1] = sin
    rope = rope.reshape(128, 2 * SEQ)
    return {"wgu": wgu, "wdn": wdn, "cst": cst, "wmix": wmix, "par": par, "rope": rope}


def _prep_x(x):
    xt = np.asarray(x, np.float32).reshape(BATCH, SEQ, KC, 128).transpose(0, 3, 2, 1)
    return np.ascontiguousarray(xt).reshape(NCORES, NSEQ, 128, KC * SEQ)


def _unprep_y(ys):
    y = np.stack(ys, 0).reshape(BATCH, 128, KC, SEQ).transpose(0, 3, 2, 1)
    return np.ascontiguousarray(y).reshape(BATCH, SEQ, D_MODEL)


def kernel(**inputs):
    shared = _prep_shared(inputs)
    xs = _prep_x(inputs["x"])
    nc = build_program()
    in_maps = [dict(shared, xT=xs[c]) for c in range(NCORES)]
    res = run_bass_kernel_spmd(nc, in_maps, core_ids=list(range(NCORES)))
    return _unprep_y([res.results[c]["yT"] for c in range(NCORES)])
```

```python
import contextlib
import math
import numpy as np
import concourse.bass as bass
import concourse.mybir as mybir
from concourse.bass_utils import run_bass_kernel_spmd

F32 = mybir.dt.float32
BF16 = mybir.dt.bfloat16
AF = mybir.ActivationFunctionType
ALU = mybir.AluOpType

D_MODEL = 1024
BATCH = 32
SEQ = 2048
DEPTH = 2
D_FF = 2816
EPS = 1e-6
NCORES = 8
NSEQ = BATCH // NCORES
KC = D_MODEL // 128
NJ = D_FF // 128
HEADS = 8
QK_DIM = 96
SCALE = 1.0 / math.sqrt(QK_DIM)
TG = 512
NTG = SEQ // TG

WIN_ITEMS = [8 * 256, 8 * 256, 8 * 256, 8 * 256, 8 * 256, 8 * 256, 8 * 224, 8 * 96]
WO_ITEM = 8 * 256
NWUQ = HEADS * 2 * 2 * 96
NMIXW = NWUQ + 512 + 512 + 512
NMIX = sum(WIN_ITEMS) + 4 * WO_ITEM + NMIXW
NPAR = 576
SEM_CAP = 16000
ENGINES = ("pe", "act", "dve", "pool", "sp")


class T:
    __slots__ = ("name", "w", "rs", "rdma", "sem", "cnt")

    def __init__(self, name):
        self.name = name
        self.w = None
        self.rs = {}
        self.rdma = []
        self.sem = None
        self.cnt = 0


class Op:
    __slots__ = ("eng", "fn", "deps", "signal", "sem", "val", "dma")

    def __init__(self, eng, fn, dma):
        self.eng = eng
        self.fn = fn
        self.deps = []
        self.signal = False
        self.sem = None
        self.val = None
        self.dma = dma


class Sched:
    def __init__(self, nc, stack):
        self.nc = nc
        self.stack = stack
        self.ops = {e: [] for e in ENGINES}
        self.nsem = 0
        self.eng_sems = {e: [] for e in ENGINES}
        self.last = {e: None for e in ENGINES}
        self.bar = []

    def new_sem(self, name):
        self.nsem += 1
        return self.stack.enter_context(self.nc.semaphore(f"s{self.nsem}_{name}"))

    def _dep(self, op, other):
        if other is None or other is op:
            return
        if other.eng == "pe" and op.eng == "pe" and not other.dma and not op.dma:
            return
        for d in op.deps:
            if d is other:
                return
        op.deps.append(other)
        other.signal = True

    def op(self, eng, fn, reads=(), writes=(), dma=False, dma_tile=None, nobarrier=False, after=()):
        o = Op(eng, fn, dma)
        if not nobarrier:
            for b in self.bar:
                self._dep(o, b)
        for a in after:
            self._dep(o, a)
        for t in reads:
            self._dep(o, t.w)
        for t in writes:
            self._dep(o, t.w)
            for r in t.rs.values():
                self._dep(o, r)
            for r in t.rdma:
                self._dep(o, r)
        for t in reads:
            if dma:
                t.rdma.append(o)
            else:
                t.rs[eng] = o
        for t in writes:
            t.w = o
            t.rs = {}
            t.rdma = []
        if dma:
            t = dma_tile
            if t.sem is None:
                t.sem = self.new_sem("d_" + t.name)
            t.cnt += 16
            o.sem = t.sem
            o.val = t.cnt
            o.signal = True
        else:
            self.last[eng] = o
        self.ops[eng].append(o)
        return o

    def barrier(self):
        self.bar = [self.last[e] for e in ("pe", "act", "dve") if self.last[e] is not None]

    def finalize(self):
        for e in ENGINES:
            n = 0
            for o in self.ops[e]:
                if o.dma or not o.signal:
                    continue
                k = n // SEM_CAP
                while len(self.eng_sems[e]) <= k:
                    self.eng_sems[e].append(self.new_sem(f"{e}{len(self.eng_sems[e])}"))
                o.sem = self.eng_sems[e][k]
                o.val = n % SEM_CAP + 1
                n += 1

    def emit(self, eng_name, eng):
        waited = {}
        for o in self.ops[eng_name]:
            for d in o.deps:
                key = d.sem.num
                if waited.get(key, 0) >= d.val:
                    continue
                eng.wait_ge(d.sem, d.val)
                waited[key] = d.val
            ins = o.fn(eng)
            if o.signal:
                ins.then_inc(o.sem, 16 if o.dma else 1)

    def run(self):
        self.finalize()
        with self.nc.Block() as block:
            @block.tensor
            def _(e):
                self.emit("pe", e)

            @block.scalar
            def _(e):
                self.emit("act", e)

            @block.vector
            def _(e):
                self.emit("dve", e)

            @block.gpsimd
            def _(e):
                self.emit("pool", e)

            @block.sync
            def _(e):
                self.emit("sp", e)


class Arena:
    def __init__(self, base_f32):
        self.base = base_f32
        self.off = 0

    def alloc(self, nbytes):
        o = self.off
        self.off += (nbytes + 63) // 64 * 64
        return o

    def view(self, off, dtype, shape):
        n = 1
        for s in shape:
            n *= s
        esz = 2 if dtype == BF16 else 4
        assert off % 4 == 0
        w0 = off // 4
        w1 = (off + n * esz + 3) // 4
        ap = self.base[:, w0:w1]
        if dtype == BF16:
            ap = ap.bitcast(BF16)
        ap = ap[:, 0:n]
        if len(shape) == 2:
            ap = ap.rearrange("p (a b) -> p a b", a=shape[0], b=shape[1])
        elif len(shape) == 3:
            ap = ap.rearrange("p (a b c) -> p a b c", a=shape[0], b=shape[1], c=shape[2])
        elif len(shape) == 4:
            ap = ap.rearrange("p (a b c d) -> p a b c d", a=shape[0], b=shape[1], c=shape[2], d=shape[3])
        return ap


class Ring:
    def __init__(self, S, name, views, items):
        self.S = S
        self.views = views
        self.items = items
        self.n = len(views)
        self.ts = [T(f"{name}{i}") for i in range(self.n)]
        self.k = 0
        self.issued = 0
        self.rel = 0

    def _issue(self, i):
        slot = i % self.n
        src, n = self.items[i]
        dst = self.views[slot][:, 0:n]
        self.S.op("pool", lambda e: e.dma_start(out=dst, in_=src), writes=[self.ts[slot]],
                  dma=True, dma_tile=self.ts[slot], nobarrier=True)

    def _fill(self):
        lim = min(len(self.items), self.rel + self.n)
        while self.issued < lim:
            self._issue(self.issued)
            self.issued += 1

    def get(self):
        self._fill()
        i = self.k
        assert i < self.issued, "ring over-subscribed"
        self.k += 1
        slot = i % self.n
        return self.views[slot], self.ts[slot]

    def done(self, n=1):
        self.rel += n
        assert self.rel <= self.k
        self._fill()


FULL_PLAN = [(0, "ffn1"), (0, "mix"), (0, "ffn2"), (1, "ffn1"), (1, "mix"), (1, "ffn2")]


def build_program(nseq=NSEQ, plan=FULL_PLAN, NG=4, ND=2, dump_yc=False):
    nc = bass.Bass("TRN2", target_bir_lowering=False)
    L = DEPTH
    xT = nc.dram_tensor("xT", [nseq, 128, KC * SEQ], F32, kind="ExternalInput").ap()
    wgu = nc.dram_tensor("wgu", [L * 2 * NJ, 128, 2048], F32, kind="ExternalInput").ap()
    wdn = nc.dram_tensor("wdn", [L * 2 * KC, 128, NJ * 128], F32, kind="ExternalInput").ap()
    wmix = nc.dram_tensor("wmix", [L, 128, NMIX], F32, kind="ExternalInput").ap()
    par = nc.dram_tensor("par", [L, 128, NPAR], F32, kind="ExternalInput").ap()
    cst = nc.dram_tensor("cst", [128, 128], F32, kind="ExternalInput").ap()
    rope = nc.dram_tensor("rope", [128, 2 * SEQ], F32, kind="ExternalInput").ap()
    yT = nc.dram_tensor("yT", [nseq, 128, KC * SEQ], F32, kind="ExternalOutput").ap()

    with contextlib.ExitStack() as st:
        S = Sched(nc, st)
        ARENA_BYTES = 211968
        arena_t = st.enter_context(nc.sbuf_tensor("arena", [128, ARENA_BYTES // 4], F32))
        A = Arena(arena_t[:])
        PS = [st.enter_context(nc.psum_tensor(f"ps{i}", [128, 1024], F32)) for i in range(4)]
        PST = [T(f"bank{b}") for b in range(8)]

        def bank(b):
            return PS[b // 2][:, (b % 2) * 512:(b % 2) * 512 + 512]

        o_x = A.alloc(KC * SEQ * 4)
        X = A.view(o_x, F32, [KC, SEQ])
        XT = [[T(f"x{c}_{g}") for g in range(NTG)] for c in range(KC)]
        XLD = [[T(f"xld{c}_{hf}") for hf in range(2)] for c in range(KC)]
        XST = [[T(f"xst{c}_{hf}") for hf in range(2)] for c in range(KC)]
        o_par = A.alloc(L * NPAR * 4)
        PAR = A.view(o_par, F32, [L, NPAR])
        PART = T("par")
        o_id = A.alloc(128 * 4)
        IDENT = A.view(o_id, F32, [128])
        o_ones = A.alloc(128 * 2)
        ONES = A.view(o_ones, BF16, [128])
        o_mw = ARENA_BYTES - NMIXW * 2
        o_rope = o_mw - 2 * SEQ * 2
        MIX_LIMIT = o_rope
        ROPE = A.view(o_rope, BF16, [2, SEQ])
        CSTT = T("cst")
        ROPET = T("rope")
        o_rg = [A.alloc(4096) for _ in range(NG)]
        o_rd = [A.alloc(NJ * 128 * 2) for _ in range(ND)]
        MIXW = A.view(o_mw, BF16, [NMIXW])
        MWT = [T("wuq"), T("wk"), T("wv"), T("wst")]
        WUQ = MIXW[:, 0:NWUQ].rearrange("p (h a k c) -> p h a k c", h=HEADS, a=2, k=2, c=96)
        WK = MIXW[:, NWUQ:NWUQ + 512]
        WV = MIXW[:, NWUQ + 512:NWUQ + 1024]
        WST = MIXW[:, NWUQ + 1024:NWUQ + 1536].rearrange("p (g q) -> p g q", g=4, q=128)
        phase_base = A.off
        PHASE_BYTES = ARENA_BYTES - phase_base

        g_items, d_items = [], []
        for s in range(nseq):
            for (l, stg) in plan:
                if stg in ("ffn1", "ffn2"):
                    i = 0 if stg == "ffn1" else 1
                    for hf in range(2):
                        for j in range(NJ):
                            g_items.append((wgu[(l * 2 + i) * NJ + j], 2048))
                        for m in range(KC):
                            d_items.append((wdn[(l * 2 + i) * KC + m], NJ * 128))
                else:
                    off = 0
                    for n in WIN_ITEMS + [WO_ITEM] * 4:
                        g_items.append((wmix[l][:, off:off + n], n))
                        off += n
        RG = Ring(S, "rg", [A.view(o, BF16, [2048]) for o in o_rg], g_items)
        RD = Ring(S, "rd", [A.view(o, BF16, [NJ * 128]) for o in o_rd], d_items)

        S.op("sp", lambda e: e.dma_start(out=PAR, in_=par.rearrange("l p n -> p l n")), writes=[PART],
             dma=True, dma_tile=PART)
        S.op("sp", lambda e: e.dma_start(out=IDENT, in_=cst), writes=[CSTT], dma=True, dma_tile=CSTT)
        ONEST = T("ones")
        S.op("dve", lambda e: e.memset(ONES, 1.0), writes=[ONEST])
        COS = ROPE[:, 0, :]
        SIN = ROPE[:, 1, :]

        def pcol(l, c):
            return PAR[:, l, c:c + 1]

        state = {"sq": 0}

        def rstd_from_stats(stat_b, dst, dst_t, n, dim):
            src = bank(stat_b)[:, 0:n]
            S.op("act", lambda e: e.activation(out=dst, in_=src, func=AF.Ln, bias=EPSC, scale=1.0 / dim),
                 reads=[PST[stat_b], CSTT2], writes=[dst_t])
            S.op("act", lambda e: e.activation(out=dst, in_=dst, func=AF.Exp, scale=-0.5),
                 reads=[dst_t], writes=[dst_t])

        o_eps = A.alloc(64)
        phase_base = A.off
        PHASE_BYTES = ARENA_BYTES - phase_base
        EPSC = A.view(o_eps, F32, [1])
        CSTT2 = T("eps")
        S.op("dve", lambda e: e.memset(EPSC, EPS), writes=[CSTT2])

        def prenorm(l, gcol, t0, ntg, XN, XNT, RSTD, RSTDT, SQ, SQT):
            for tg in range(ntg):
                T0 = t0 + tg * TG
                gtg = T0 // TG
                sb = 6 + (tg % 2)
                for c in range(KC):
                    q = state["sq"] % len(SQ)
                    state["sq"] += 1
                    sqv, sqt = SQ[q], SQT[q]
                    xin = X[:, c, T0:T0 + TG]
                    S.op("act", lambda e, sqv=sqv, xin=xin: e.activation(out=sqv, in_=xin, func=AF.Square),
                         reads=[XT[c][gtg]], writes=[sqt])
                    S.op("pe", lambda e, sb=sb, sqv=sqv, c=c: e.matmul(bank(sb), lhsT=ONES, rhs=sqv,
                                                                         start=(c == 0), stop=(c == KC - 1)),
                         reads=[sqt, ONEST], writes=[PST[sb]])
                rs = RSTD[:, tg * TG:(tg + 1) * TG]
                rstd_from_stats(sb, rs, RSTDT[tg], TG, D_MODEL)
                for c in range(KC):
                    xin = X[:, c, T0:T0 + TG]
                    xo = XN[:, c, tg * TG:(tg + 1) * TG]
                    gc = pcol(l, gcol + c)
                    S.op("dve", lambda e, xo=xo, xin=xin, gc=gc, rs=rs: e.scalar_tensor_tensor(
                        out=xo, in0=xin, scalar=gc, in1=rs, op0=ALU.mult, op1=ALU.mult),
                        reads=[XT[c][gtg], RSTDT[tg], PART], writes=[XNT[tg]])

        def postnorm_apply(l, gcol, T0, DSB, DSBT, stat_b, R2, R2T, half):
            gtg = T0 // TG
            rstd_from_stats(stat_b, R2, R2T, TG, D_MODEL)
            for m in range(KC):
                dv = DSB[:, m, :]
                gc = pcol(l, gcol + m)
                S.op("dve", lambda e, dv=dv, gc=gc: e.scalar_tensor_tensor(
                    out=dv, in0=dv, scalar=gc, in1=R2, op0=ALU.mult, op1=ALU.mult),
                    reads=[R2T, PART] + DSBT, writes=DSBT)
                xv = X[:, m, T0:T0 + TG]
                S.op("dve", lambda e, dv=dv, xv=xv: e.scalar_tensor_tensor(
                    out=xv, in0=dv, scalar=half, in1=xv, op0=ALU.mult, op1=ALU.add),
                    reads=DSBT + [XT[m][gtg]], writes=[XT[m][gtg]])

        def ffn_chain(stages, on_post=None):
            from collections import deque
            S.barrier()
            A.off = phase_base
            o_xn = A.alloc(KC * 1024 * 2)
            o_h = A.alloc(NJ * 1024 * 2)
            o_sq = [A.alloc(TG * 2) for _ in range(4)]
            o_rstd = A.alloc(1024 * 4)
            o_r2 = [A.alloc(TG * 4) for _ in range(2)]
            o_st = [A.alloc(TG * 2) for _ in range(2)]
            o_dsb = [A.alloc(KC * TG * 4) for _ in range(2)]
            assert A.off <= ARENA_BYTES, A.off
            XN = A.view(o_xn, BF16, [KC, 1024])
            XNT = [T("xn0"), T("xn1")]
            H = A.view(o_h, BF16, [NJ, 1024])
            HT = [[T(f"h{j}_{g}") for g in range(2)] for j in range(NJ)]
            SQ = [A.view(o, BF16, [TG]) for o in o_sq]
            SQT = [T(f"sq{q}") for q in range(4)]
            RSTD = A.view(o_rstd, F32, [1024])
            RSTDT = [T("rstd0"), T("rstd1")]
            R2 = [A.view(o, F32, [TG]) for o in o_r2]
            R2T = [T("r2_0"), T("r2_1")]
            ST = [A.view(o, BF16, [TG]) for o in o_st]
            STT = [T("st0"), T("st1")]
            DSB = [A.view(o, F32, [KC, TG]) for o in o_dsb]
            DSBT = [[T("dsb0")], [T("dsb1")]]
            blocks = [(l, i, hf, s_, last_) for (l, i, s_, last_) in stages for hf in range(2)]
            nb_ = len(blocks)
            bg = deque()
            cnt = {"gu": 0, "d": 0}

            def drain(n):
                for _ in range(n):
                    if not bg:
                        return
                    bg.popleft()()

            def pre_stats(k):
                l, i, hf = blocks[k][:3]
                t0 = hf * 1024
                out = []
                for tg in range(2):
                    T0 = t0 + tg * TG
                    gtg = T0 // TG
                    sb = 6 + tg
                    for c in range(KC):
                        def f(c=c, T0=T0, gtg=gtg, sb=sb):
                            q = state["sq"] % 4
                            state["sq"] += 1
                            sqv, sqt = SQ[q], SQT[q]
                            xin = X[:, c, T0:T0 + TG]
                            S.op("act", lambda e: e.activation(out=sqv, in_=xin, func=AF.Square),
                                 reads=[XT[c][gtg]], writes=[sqt])
                            S.op("pe", lambda e: e.matmul(bank(sb), lhsT=ONES, rhs=sqv, start=(c == 0), stop=(c == KC - 1)),
                                 reads=[sqt, ONEST], writes=[PST[sb]])
                        out.append(f)

                    def g(tg=tg, sb=sb):
                        rstd_from_stats(sb, RSTD[:, tg * TG:(tg + 1) * TG], RSTDT[tg], TG, D_MODEL)
                    out.append(g)
                return out

            def pre_xn(k):
                l, i, hf = blocks[k][:3]
                gpre = 0 if i == 0 else 32
                t0 = hf * 1024
                for tg in range(2):
                    T0 = t0 + tg * TG
                    gtg = T0 // TG
                    rs = RSTD[:, tg * TG:(tg + 1) * TG]
                    for c in range(KC):
                        xin = X[:, c, T0:T0 + TG]
                        xo = XN[:, c, tg * TG:(tg + 1) * TG]
                        gc = pcol(l, gpre + c)
                        S.op("dve", lambda e, xo=xo, xin=xin, gc=gc, rs=rs: e.scalar_tensor_tensor(
                            out=xo, in0=xin, scalar=gc, in1=rs, op0=ALU.mult, op1=ALU.mult),
                            reads=[XT[c][gtg], RSTDT[tg], PART], writes=[XNT[tg]])

            def post_apply(k):
                l, i, hf = blocks[k][:3]
                gpost = 8 if i == 0 else 40
                out = []
                for tg in range(2):
                    T0 = hf * 1024 + tg * TG
                    gtg = T0 // TG

                    def g(tg=tg):
                        rstd_from_stats(6 + tg, R2[tg], R2T[tg], TG, D_MODEL)
                    out.append(g)
                    for m in range(KC):
                        def f(m=m, tg=tg, T0=T0, gtg=gtg):
                            dv = DSB[tg][:, m, :]
                            gc = pcol(l, gpost + m)
                            S.op("dve", lambda e: e.scalar_tensor_tensor(
                                out=dv, in0=dv, scalar=gc, in1=R2[tg], op0=ALU.mult, op1=ALU.mult),
                                reads=[R2T[tg], PART] + DSBT[tg], writes=DSBT[tg])
                            xv = X[:, m, T0:T0 + TG]
                            S.op("dve", lambda e: e.scalar_tensor_tensor(
                                out=xv, in0=dv, scalar=0.5, in1=xv, op0=ALU.mult, op1=ALU.add),
                                reads=DSBT[tg] + [XT[m][gtg]], writes=[XT[m][gtg]])
                        out.append(f)
                return out

            for f in pre_stats(0):
                f()
            pre_xn(0)
            for k in range(nb_):
                l, i, hf = blocks[k][:3]
                if k + 1 < nb_:
                    bg.extend(pre_stats(k + 1))
                for j in range(NJ):
                    wv, wt = RG.get()
                    W = wv.rearrange("p (a k m) -> p a k m", a=2, k=KC, m=128)
                    for tg in range(2):
                        pair = cnt["gu"] % 2
                        cnt["gu"] += 1
                        bg_, bu = 2 * pair, 2 * pair + 1
                        for a_, b in ((0, bg_), (1, bu)):
                            for kc in range(KC):
                                S.op("pe", lambda e, a_=a_, b=b, kc=kc, W=W, tg=tg: e.matmul(
                                    bank(b), lhsT=W[:, a_, kc, :], rhs=XN[:, kc, tg * TG:(tg + 1) * TG],
                                    start=(kc == 0), stop=(kc == KC - 1)),
                                    reads=[wt, XNT[tg]], writes=[PST[b]])
                        sv, stt = ST[pair], STT[pair]
                        S.op("act", lambda e, sv=sv, bg_=bg_: e.activation(out=sv, in_=bank(bg_), func=AF.Silu),
                             reads=[PST[bg_]], writes=[stt])
                        ho = H[:, j, tg * TG:(tg + 1) * TG]
                        S.op("dve", lambda e, ho=ho, sv=sv, bu=bu: e.tensor_tensor(
                            out=ho, in0=bank(bu), in1=sv, op=ALU.mult),
                            reads=[PST[bu], stt], writes=[HT[j][tg]])
                        drain(2)
                    RG.done()
                drain(10 ** 6)
                if k + 1 < nb_:
                    pre_xn(k + 1)
                pend = []
                for m in range(KC):
                    wv, wt = RD.get()
                    Wd = wv.rearrange("p (j c) -> p j c", j=NJ, c=128)
                    for tg in range(2):
                        b = 4 + cnt["d"] % 2
                        cnt["d"] += 1
                        for j in range(NJ):
                            S.op("pe", lambda e, b=b, j=j, Wd=Wd, tg=tg: e.matmul(
                                bank(b), lhsT=Wd[:, j, :], rhs=H[:, j, tg * TG:(tg + 1) * TG],
                                start=(j == 0), stop=(j == NJ - 1)),
                                reads=[wt, HT[j][tg]], writes=[PST[b]])
                        dv = DSB[tg][:, m, :]
                        S.op("act", lambda e, dv=dv, b=b: e.activation(out=dv, in_=bank(b), func=AF.Copy),
                             reads=[PST[b]], writes=DSBT[tg])
                        q = state["sq"] % 4
                        state["sq"] += 1
                        sqv, sqt = SQ[q], SQT[q]
                        S.op("act", lambda e, sqv=sqv, b=b: e.activation(out=sqv, in_=bank(b), func=AF.Square),
                             reads=[PST[b]], writes=[sqt])
                        for f in pend:
                            f()
                        pend = []
                        sb = 6 + tg

                        def mk(sb=sb, sqv=sqv, sqt=sqt, m=m):
                            S.op("pe", lambda e: e.matmul(bank(sb), lhsT=ONES, rhs=sqv,
                                                          start=(m == 0), stop=(m == KC - 1)),
                                 reads=[sqt, ONEST], writes=[PST[sb]])
                        pend.append(mk)
                    RD.done()
                for f in pend:
                    f()
                bg.extend(post_apply(k))
                if blocks[k][4]:
                    bg.append(lambda k=k: on_post(blocks[k][3], blocks[k][2]))
            drain(10 ** 6)

        def load_mixw(l):
            base = sum(WIN_ITEMS) + 4 * WO_ITEM
            segs = [(0, NWUQ), (NWUQ, 512), (NWUQ + 512, 512), (NWUQ + 1024, 512)]
            for (o, n), t in zip(segs, MWT):
                S.op("pool", lambda e, o=o, n=n: e.dma_start(out=MIXW[:, o:o + n], in_=wmix[l][:, base + o:base + o + n]),
                     writes=[t], dma=True, dma_tile=t)
            S.op("pool", lambda e: e.dma_start(out=ROPE, in_=rope.rearrange("p (a t) -> p a t", a=2)),
                 writes=[ROPET], dma=True, dma_tile=ROPET)

        def mix_stage(l):
            S.barrier()
            load_mixw(l)
            A.off = phase_base
            o_xn = A.alloc(32768)
            o_yc = A.alloc(32768)
            o_tmp = A.off
            XN = A.view(o_xn, BF16, [KC, SEQ])
            XNT = [T(f"mxn{g}") for g in range(NTG)]
            YC = A.view(o_yc, BF16, [KC, SEQ])
            YCT = [[T(f"yc{c}_{g}") for g in range(NTG)] for c in range(KC)]
            o_rstd = A.alloc(SEQ * 4)
            o_sq = [A.alloc(TG * 2) for _ in range(4)]
            RSTD = A.view(o_rstd, F32, [SEQ])
            RSTDT = [T(f"mrstd{g}") for g in range(NTG)]
            SQ = [A.view(o, BF16, [TG]) for o in o_sq]
            SQT = [T(f"msq{q}") for q in range(4)]
            prenorm(l, 16, 0, NTG, XN, XNT, RSTD, RSTDT, SQ, SQT)
            o_xcs = [A.alloc(TG * 4) for _ in range(2)]
            o_zc = A.alloc((SEQ + 2) * 2)
            o_gb = A.alloc(SEQ * 2)
            o_dg = A.alloc(6 * 128 * 2)
            assert A.off <= MIX_LIMIT
            XCS = [A.view(o, F32, [TG]) for o in o_xcs]
            XCST = [T("xcs0"), T("xcs1")]
            ZC = A.view(o_zc, BF16, [SEQ + 2])
            ZCT = [T(f"zc{g}") for g in range(NTG)]
            ZPAD = T("zpad")
            GB = A.view(o_gb, BF16, [SEQ])
            GBT = [T(f"gb{g}") for g in range(NTG)]
            DG = A.view(o_dg, BF16, [6, 128])
            DGT = T("diag")
            for q in range(6):
                S.op("dve", lambda e, q=q: e.tensor_scalar(out=DG[:, q, :], in0=IDENT, scalar1=pcol(l, 48 + q),
                                                           scalar2=None, op0=ALU.mult),
                     reads=[CSTT, PART], writes=[DGT])
            S.op("dve", lambda e: e.memset(ZC[:, 0:1], 0.0), writes=[ZPAD])
            S.op("dve", lambda e: e.memset(ZC[:, SEQ + 1:SEQ + 2], 0.0), writes=[ZPAD])
            items = [RG.get() for _ in range(3)]
            Wi = [(v.rearrange("p (k c) -> p k c", k=KC, c=256), t) for (v, t) in items]
            sel = {0: ((0, 0), (0, 128), (1, 0)), 1: ((1, 128), (2, 0), (2, 128))}
            cnt = 0
            for cc in range(2):
                for tg in range(NTG):
                    bset = (cnt % 2) * 3
                    cnt += 1
                    for r_, (it, co) in enumerate(sel[cc]):
                        W, wt = Wi[it]
                        b = bset + r_
                        for kc in range(KC):
                            S.op("pe", lambda e, b=b, W=W, co=co, kc=kc, tg=tg: e.matmul(
                                bank(b), lhsT=W[:, kc, co:co + 128], rhs=XN[:, kc, tg * TG:(tg + 1) * TG],
                                start=(kc == 0), stop=(kc == KC - 1)),
                                reads=[wt, XNT[tg]], writes=[PST[b]])
                    xs, xst = XCS[tg % 2], XCST[tg % 2]
                    S.op("act", lambda e, xs=xs, b=bset: e.activation(out=xs, in_=bank(b), func=AF.Copy),
                         reads=[PST[bset]], writes=[xst])
                    zo = ZC[:, 1 + tg * TG:1 + (tg + 1) * TG]
                    S.op("dve", lambda e, zo=zo, xs=xs, b=bset + 1: e.tensor_tensor(out=zo, in0=bank(b), in1=xs, op=ALU.mult),
                         reads=[PST[bset + 1], xst], writes=[ZCT[tg]])
                    go = GB[:, tg * TG:(tg + 1) * TG]
                    S.op("act", lambda e, go=go, b=bset + 2: e.activation(out=go, in_=bank(b), func=AF.Copy),
                         reads=[PST[bset + 2]], writes=[GBT[tg]])
                for tg in range(NTG):
                    b = 6 + tg % 2
                    rd = [ZCT[g] for g in (tg - 1, tg, tg + 1) if 0 <= g < NTG] + [ZPAD, DGT]
                    for k in range(3):
                        S.op("pe", lambda e, b=b, k=k, cc=cc, tg=tg: e.matmul(
                            bank(b), lhsT=DG[:, cc * 3 + k, :], rhs=ZC[:, tg * TG + k:tg * TG + k + TG],
                            start=(k == 0), stop=(k == 2)),
                            reads=rd, writes=[PST[b]])
                    yo = YC[:, cc, tg * TG:(tg + 1) * TG]
                    go = GB[:, tg * TG:(tg + 1) * TG]
                    S.op("dve", lambda e, yo=yo, go=go, b=b, cc=cc: e.scalar_tensor_tensor(
                        out=yo, in0=bank(b), scalar=pcol(l, 54 + cc), in1=go, op0=ALU.add, op1=ALU.mult),
                        reads=[PST[b], GBT[tg], PART], writes=[YCT[cc][tg]])
            RG.done(3)
            S.barrier()
            A.off = o_tmp
            o_gv = A.alloc(16 * 256 * 2)
            o_vt = A.alloc(16 * 256 * 2)
            o_ut = [A.alloc(TG * 2) for _ in range(2)]
            o_ss = A.alloc(64)
            o_rv = A.alloc(64)
            o_jk = A.alloc(256 * 2)
            o_tm = [A.alloc(TG * 4) for _ in range(2)]
            assert A.off <= MIX_LIMIT
            GV = A.view(o_gv, BF16, [16, 256])
            GVT = [T(f"gv{i}") for i in range(16)]
            VT_ = A.view(o_vt, BF16, [16, 256])
            VTT = [T(f"vt{i}") for i in range(16)]
            UT = [A.view(o, BF16, [TG]) for o in o_ut]
            UTT = [T("ut0"), T("ut1")]
            SS = A.view(o_ss, F32, [16])
            SST = T("ss")
            RV = A.view(o_rv, F32, [16])
            RVT = T("rv")
            JK = A.view(o_jk, BF16, [256])
            JKT = T("jk")
            TM = [A.view(o, F32, [TG]) for o in o_tm]
            TMT = [T("tm0"), T("tm1")]
            (i3v, i3t) = RG.get()
            (i4v, i4t) = RG.get()
            W3 = i3v.rearrange("p (k c) -> p k c", k=KC, c=256)
            W4 = i4v.rearrange("p (k c) -> p k c", k=KC, c=256)
            for tt in range(16):
                b = 4 + tt % 2
                for kc in range(KC):
                    S.op("pe", lambda e, b=b, kc=kc, tt=tt: e.matmul(
                        bank(b)[:, 0:256], lhsT=XN[:, kc, tt * 128:(tt + 1) * 128], rhs=W4[:, kc, :],
                        start=(kc == 0), stop=(kc == KC - 1)),
                        reads=[i4t, XNT[tt // 4]], writes=[PST[b]])
                S.op("act", lambda e, b=b, tt=tt: e.activation(out=GV[:, tt, :], in_=bank(b)[:, 0:256], func=AF.Gelu),
                     reads=[PST[b]], writes=[GVT[tt]])
                S.op("act", lambda e, tt=tt: e.activation(out=JK, in_=GV[:, tt, :], func=AF.Square,
                                                          accum_out=SS[:, tt:tt + 1]),
                     reads=[GVT[tt]], writes=[JKT, SST])
            S.op("act", lambda e: e.activation(out=RV, in_=SS, func=AF.Ln, bias=EPSC, scale=1.0 / 256),
                 reads=[SST, CSTT2], writes=[RVT])
            S.op("act", lambda e: e.activation(out=RV, in_=RV, func=AF.Exp, scale=-0.5), reads=[RVT], writes=[RVT])
            GN = PAR[:, l, 64:320]
            for tt in range(16):
                S.op("dve", lambda e, tt=tt: e.scalar_tensor_tensor(
                    out=VT_[:, tt, :], in0=GV[:, tt, :], scalar=RV[:, tt:tt + 1], in1=GN, op0=ALU.mult, op1=ALU.mult),
                    reads=[GVT[tt], RVT, PART], writes=[VTT[tt]])
            cnt = 0
            for fc in range(2):
                for tg in range(NTG):
                    bm = 6 + cnt % 2
                    bz = cnt % 4
                    sl = cnt % 2
                    cnt += 1
                    for ch in range(4):
                        c16 = tg * 4 + ch
                        for gi in range(2):
                            g = 2 * fc + gi
                            S.op("pe", lambda e, bm=bm, gi=gi, ch=ch, c16=c16, g=g: e.matmul(
                                bank(bm)[gi * 64:(gi + 1) * 64, ch * 128:(ch + 1) * 128],
                                lhsT=VT_[:, c16, g * 64:(g + 1) * 64], rhs=WST[:, g, :], start=True, stop=True),
                                reads=[VTT[c16], MWT[3]], writes=[PST[bm]])
                    for kc in range(KC):
                        S.op("pe", lambda e, bz=bz, kc=kc, fc=fc, tg=tg: e.matmul(
                            bank(bz), lhsT=W3[:, kc, fc * 128:(fc + 1) * 128], rhs=XN[:, kc, tg * TG:(tg + 1) * TG],
                            start=(kc == 0), stop=(kc == KC - 1)),
                            reads=[i3t, XNT[tg]], writes=[PST[bz]])
                    S.op("act", lambda e, bz=bz, sl=sl: e.activation(out=UT[sl], in_=bank(bz), func=AF.Gelu),
                         reads=[PST[bz]], writes=[UTT[sl]])
                    for ch in range(4):
                        S.op("dve", lambda e, bm=bm, sl=sl, ch=ch, fc=fc: e.tensor_tensor(
                            out=TM[sl][:, ch * 128:(ch + 1) * 128], in0=bank(bm)[:, ch * 128:(ch + 1) * 128],
                            in1=PAR[:, l, 320 + fc * 128:320 + (fc + 1) * 128], op=ALU.add),
                            reads=[PST[bm], PART], writes=[TMT[sl]])
                    yo = YC[:, 6 + fc, tg * TG:(tg + 1) * TG]
                    S.op("dve", lambda e, yo=yo, sl=sl: e.tensor_tensor(out=yo, in0=TM[sl], in1=UT[sl], op=ALU.mult),
                         reads=[TMT[sl], UTT[sl]], writes=[YCT[6 + fc][tg]])
            RG.done(2)
            S.barrier()
            A.off = o_tmp
            o_cqn = A.alloc(2 * SEQ * 2)
            o_ckvn = A.alloc(SEQ * 2)
            o_kro = A.alloc(SEQ * 2)
            o_rq = A.alloc(TG * 4)
            o_rkv = A.alloc(TG * 4)
            o_t1 = A.alloc(TG * 4)
            o_t2 = A.alloc(TG * 4)
            o_sq2 = [A.alloc(TG * 2) for _ in range(3)]
            assert A.off <= MIX_LIMIT, A.off
            CQN = A.view(o_cqn, BF16, [2, SEQ])
            CQNT = [T(f"cqn{g}") for g in range(NTG)]
            CKVN = A.view(o_ckvn, BF16, [SEQ])
            CKVNT = [T(f"ckvn{g}") for g in range(NTG)]
            KRO = A.view(o_kro, BF16, [SEQ])
            KROT = [T(f"kro{g}") for g in range(NTG)]
            RQ = A.view(o_rq, F32, [TG])
            RQT = T("rq")
            RKV = A.view(o_rkv, F32, [TG])
            RKVT = T("rkv")
            T1 = A.view(o_t1, F32, [TG])
            T1T = T("t1")
            T2 = A.view(o_t2, F32, [TG])
            T2T = T("t2")
            SQ2 = [A.view(o, BF16, [TG]) for o in o_sq2]
            SQ2T = [T(f"sq2_{q}") for q in range(3)]
            (i5v, i5t) = RG.get()
            (i6v, i6t) = RG.get()
            (i7v, i7t) = RG.get()
            W5 = i5v.rearrange("p (k c) -> p k c", k=KC, c=256)
            W6 = i6v[:, 0:KC * 224].rearrange("p (k c) -> p k c", k=KC, c=224)
            W7 = i7v[:, 0:KC * 96].rearrange("p (k c) -> p k c", k=KC, c=96)
            for tg in range(NTG):
                tsl = slice(tg * TG, (tg + 1) * TG)
                projs = [(0, W5, i5t, 0, 128), (1, W5, i5t, 128, 128), (2, W6, i6t, 0, 128),
                         (3, W6, i6t, 128, 96), (4, W7, i7t, 0, 96)]

                def do_proj(pi_):
                    (b, W, wt, co, mw) = projs[pi_]
                    for kc in range(KC):
                        S.op("pe", lambda e, b=b, W=W, co=co, mw=mw, kc=kc, tsl=tsl: e.matmul(
                            bank(b)[0:mw, :], lhsT=W[:, kc, co:co + mw], rhs=XN[:, kc, tsl],
                            start=(kc == 0), stop=(kc == KC - 1)),
                            reads=[wt, XNT[tg]], writes=[PST[b]])

                def do_sq(c):
                    S.op("act", lambda e, c=c: e.activation(out=SQ2[c], in_=bank(c), func=AF.Square),
                         reads=[PST[c]], writes=[SQ2T[c]])

                do_proj(0)
                do_sq(0)
                do_proj(1)
                do_sq(1)
                do_proj(2)
                do_sq(2)
                for c in range(2):
                    S.op("pe", lambda e, c=c: e.matmul(bank(6), lhsT=ONES, rhs=SQ2[c], start=(c == 0), stop=(c == 1)),
                         reads=[SQ2T[c], ONEST], writes=[PST[6]])
                do_proj(3)
                S.op("pe", lambda e: e.matmul(bank(7), lhsT=ONES, rhs=SQ2[2], start=True, stop=True),
                     reads=[SQ2T[2], ONEST], writes=[PST[7]])
                do_proj(4)
                rstd_from_stats(6, RQ, RQT, TG, 256)
                rstd_from_stats(7, RKV, RKVT, TG, 128)
                for c in range(2):
                    S.op("dve", lambda e, c=c, tsl=tsl: e.scalar_tensor_tensor(
                        out=CQN[:, c, tsl], in0=bank(c), scalar=pcol(l, 56 + c), in1=RQ, op0=ALU.mult, op1=ALU.mult),
                        reads=[PST[c], RQT, PART], writes=[CQNT[tg]])
                S.op("dve", lambda e, tsl=tsl: e.scalar_tensor_tensor(
                    out=CKVN[:, tsl], in0=bank(2), scalar=pcol(l, 58), in1=RKV, op0=ALU.mult, op1=ALU.mult),
                    reads=[PST[2], RKVT, PART], writes=[CKVNT[tg]])
                S.op("dve", lambda e, tsl=tsl: e.tensor_tensor(out=T1[64:96, :], in0=bank(3)[64:96, :], in1=COS[64:96, tsl],
                                                               op=ALU.mult),
                     reads=[PST[3], ROPET], writes=[T1T])
                S.op("dve", lambda e, tsl=tsl: e.tensor_tensor(out=T2[64:96, :], in0=bank(4)[64:96, :], in1=SIN[64:96, tsl],
                                                               op=ALU.mult),
                     reads=[PST[4], ROPET], writes=[T2T])
                S.op("dve", lambda e, tsl=tsl: e.tensor_tensor(out=KRO[64:96, tsl], in0=T1[64:96, :], in1=T2[64:96, :],
                                                               op=ALU.add),
                     reads=[T1T, T2T], writes=[KROT[tg]])
            RG.done(3)
            S.barrier()
            A.off = o_xn
            o_va = [A.alloc(16 * 128 * 2) for _ in range(2)]
            o_qh = [A.alloc(SEQ * 2) for _ in range(2)]
            o_kh = [A.alloc(SEQ * 2) for _ in range(2)]
            o_pt = [A.alloc(1024 * 2) for _ in range(3)]
            o_r = A.alloc(TG * 4)
            assert A.off <= o_yc, A.off
            VA = [A.view(o, BF16, [16, 128]) for o in o_va]
            VAT = [T("va0"), T("va1")]
            QH = [A.view(o, BF16, [SEQ]) for o in o_qh]
            QHT = [[T(f"qh{p}_{g}") for g in range(NTG)] for p in range(2)]
            KH = [A.view(o, BF16, [SEQ]) for o in o_kh]
            KHT = [[T(f"kh{p}_{g}") for g in range(NTG)] for p in range(2)]
            PTB = [A.view(o, BF16, [1024]) for o in o_pt]
            PTT = [T(f"pt{i}") for i in range(3)]
            R = A.view(o_r, F32, [TG])
            RT = T("r")
            S.op("dve", lambda e: e.memset(VA[0][:, :, 64:128], 1.0), writes=[VAT[0]])
            S.op("dve", lambda e: e.memset(VA[1][:, :, 0:64], 1.0), writes=[VAT[1]])
            pcnt = {"s": 0, "o": 0}

            def head_prep(h):
                par_ = h % 2
                out = []
                for tg in range(NTG):
                    tsl = slice(tg * TG, (tg + 1) * TG)

                    def f_a(tg=tg, tsl=tsl):
                        for kc in range(2):
                            S.op("pe", lambda e, kc=kc: e.matmul(
                                bank(4)[0:96, :], lhsT=WUQ[:, h, 0, kc, :], rhs=CQN[:, kc, tsl],
                                start=(kc == 0), stop=(kc == 1)),
                                reads=[MWT[0], CQNT[tg]], writes=[PST[4]])
                        if h == 0:
                            S.op("act", lambda e: e.activation(out=QH[par_][0:64, tsl], in_=bank(4)[0:64, :], func=AF.Copy),
                                 reads=[PST[4]], writes=[QHT[par_][tg]])
                        else:
                            S.op("dve", lambda e: e.tensor_copy(out=QH[par_][0:64, tsl], in_=bank(4)[0:64, :]),
                                 reads=[PST[4]], writes=[QHT[par_][tg]])
                        S.op("dve", lambda e: e.tensor_tensor(out=T1[64:96, :], in0=bank(4)[64:96, :],
                                                              in1=COS[64:96, tsl], op=ALU.mult),
                             reads=[PST[4], ROPET], writes=[T1T])

                    def f_b(tg=tg, tsl=tsl):
                        for kc in range(2):
                            S.op("pe", lambda e, kc=kc: e.matmul(
                                bank(5)[0:96, :], lhsT=WUQ[:, h, 1, kc, :], rhs=CQN[:, kc, tsl],
                                start=(kc == 0), stop=(kc == 1)),
                                reads=[MWT[0], CQNT[tg]], writes=[PST[5]])
                        S.op("dve", lambda e: e.tensor_tensor(out=T2[64:96, :], in0=bank(5)[64:96, :],
                                                              in1=SIN[64:96, tsl], op=ALU.mult),
                             reads=[PST[5], ROPET], writes=[T2T])
                        S.op("dve", lambda e: e.tensor_tensor(out=QH[par_][64:96, tsl], in0=T1[64:96, :], in1=T2[64:96, :],
                                                              op=ALU.add),
                             reads=[T1T, T2T], writes=[QHT[par_][tg]])

                    def f_k(tg=tg, tsl=tsl):
                        S.op("pe", lambda e: e.matmul(
                            bank(4)[0:64, :], lhsT=WK[:, h * 64:(h + 1) * 64], rhs=CKVN[:, tsl], start=True, stop=True),
                            reads=[MWT[1], CKVNT[tg]], writes=[PST[4]])
                        if h == 0:
                            S.op("act", lambda e: e.activation(out=KH[par_][0:64, tsl], in_=bank(4)[0:64, :], func=AF.Copy),
                                 reads=[PST[4]], writes=[KHT[par_][tg]])
                        else:
                            S.op("dve", lambda e: e.tensor_copy(out=KH[par_][0:64, tsl], in_=bank(4)[0:64, :]),
                                 reads=[PST[4]], writes=[KHT[par_][tg]])
                        S.op("dve", lambda e: e.tensor_copy(out=KH[par_][64:96, tsl], in_=KRO[64:96, tsl]),
                             reads=[KROT[tg]], writes=[KHT[par_][tg]])
                    out += [f_a, f_b, f_k]
                vc0 = 0 if par_ == 0 else 64
                for half8 in range(2):
                    def f_v(half8=half8):
                        for t8 in range(8):
                            tt = half8 * 8 + t8
                            S.op("pe", lambda e, t8=t8, tt=tt: e.matmul(
                                bank(5)[:, t8 * 64:(t8 + 1) * 64], lhsT=CKVN[:, tt * 128:(tt + 1) * 128],
                                rhs=WV[:, h * 64:(h + 1) * 64], start=True, stop=True),
                                reads=[CKVNT[tt // 4], MWT[2]], writes=[PST[5]])
                        dst = VA[par_][:, half8 * 8:(half8 + 1) * 8, vc0:vc0 + 64]
                        srcv = bank(5).rearrange("p (a b) -> p a b", a=8, b=64)
                        if h == 0:
                            S.op("act", lambda e: e.activation(out=dst, in_=srcv, func=AF.Copy), reads=[PST[5]], writes=[VAT[par_]])
                        else:
                            S.op("dve", lambda e: e.tensor_copy(out=dst, in_=srcv), reads=[PST[5]], writes=[VAT[par_]])
                    out.append(f_v)
                return out

            for f in head_prep(0):
                f()
            gsteps = [(h, qg, kp) for h in range(HEADS) for qg in range(NTG) for kp in range(8)]
            ng = len(gsteps)

            def rec_S(g):
                h, qg, kp = gsteps[g]
                par_ = h % 2
                sl = g % 2
                for hf in range(2):
                    kc = 2 * kp + hf
                    S.op("pe", lambda e, sl=sl, hf=hf, kc=kc, qg=qg, par_=par_: e.matmul(
                        PS[sl][:, hf * TG:(hf + 1) * TG], lhsT=KH[par_][0:96, kc * 128:(kc + 1) * 128],
                        rhs=QH[par_][0:96, qg * TG:(qg + 1) * TG], start=True, stop=True),
                        reads=[KHT[par_][kc // 4], QHT[par_][qg]], writes=[PST[2 * sl + hf]])

            def rec_exp(g):
                sl = g % 2
                ptsl = g % 3
                S.op("act", lambda e, sl=sl, ptsl=ptsl: e.activation(out=PTB[ptsl], in_=PS[sl][:, 0:1024], func=AF.Exp,
                                                                   scale=SCALE),
                     reads=[PST[2 * sl], PST[2 * sl + 1]], writes=[PTT[ptsl]])

            def rec_pv(g):
                h, qg, kp = gsteps[g]
                ptsl = g % 3
                ob = 6 + (g // 8) % 2
                for hf in range(2):
                    kc = 2 * kp + hf
                    S.op("pe", lambda e, kc=kc, hf=hf, ptsl=ptsl, ob=ob, h=h, kp=kp: e.matmul(
                        bank(ob), lhsT=VA[h % 2][:, kc, :], rhs=PTB[ptsl][:, hf * TG:(hf + 1) * TG],
                        start=(kp == 0 and hf == 0), stop=(kp == 7 and hf == 1)),
                        reads=[VAT[h % 2], PTT[ptsl]], writes=[PST[ob]])
                if kp == 7:
                    tsl = slice(qg * TG, (qg + 1) * TG)
                    lo, hi = (slice(0, 64), slice(64, 128)) if h % 2 == 0 else (slice(64, 128), slice(0, 64))
                    S.op("act", lambda e, ob=ob, lo=lo, hi=hi: e.activation(out=R[lo, :], in_=bank(ob)[hi, :], func=AF.Ln),
                         reads=[PST[ob]], writes=[RT])
                    S.op("act", lambda e, lo=lo: e.activation(out=R[lo, :], in_=R[lo, :], func=AF.Exp, scale=-1.0),
                         reads=[RT], writes=[RT])
                    S.op("dve", lambda e, ob=ob, lo=lo, tsl=tsl, h=h: e.tensor_tensor(
                        out=YC[lo, 2 + h // 2, tsl], in0=bank(ob)[lo, :], in1=R[lo, :], op=ALU.mult),
                        reads=[PST[ob], RT], writes=[YCT[2 + h // 2][qg]])

            nxt = []
            rec_S(0)
            for g in range(ng):
                h, qg, kp = gsteps[g]
                if qg == 0 and kp == 0:
                    for f in nxt:
                        f()
                    nxt = head_prep(h + 1) if h + 1 < HEADS else []
                if g + 1 < ng:
                    if gsteps[g + 1][0] != h:
                        for f in nxt:
                            f()
                        nxt = []
                    rec_S(g + 1)
                rec_exp(g)
                if g >= 1:
                    rec_pv(g - 1)
                if g % 2 == 1 and nxt:
                    nxt.pop(0)()
            rec_pv(ng - 1)
            if dump_yc:
                S.barrier()
                for c in range(KC):
                    for tg in range(NTG):
                        tsl = slice(tg * TG, (tg + 1) * TG)
                        S.op("dve", lambda e, c=c, tsl=tsl: e.tensor_copy(out=X[:, c, tsl], in_=YC[:, c, tsl]),
                             reads=[YCT[c][tg]], writes=[XT[c][tg]])
                return
            S.barrier()
            A.off = o_xn
            o_dsb = [A.alloc(KC * TG * 4) for _ in range(2)]
            DSB = [A.view(o, F32, [KC, TG]) for o in o_dsb]
            DSBT = [[T("mdsb0")], [T("mdsb1")]]
            A.off = o_tmp
            o_r2 = [A.alloc(TG * 4) for _ in range(2)]
            o_sq3 = [A.alloc(TG * 2) for _ in range(4)]
            R2 = [A.view(o, F32, [TG]) for o in o_r2]
            R2T = [T("mr2_0"), T("mr2_1")]
            SQ3 = [A.view(o, BF16, [TG]) for o in o_sq3]
            SQ3T = [T(f"sq3_{q}") for q in range(4)]
            wo = [RG.get() for _ in range(4)]
            WO = [(v.rearrange("p (k c) -> p k c", k=KC, c=256), t) for (v, t) in wo]
            cnt = 0
            for tg in range(NTG):
                tsl = slice(tg * TG, (tg + 1) * TG)
                sb = 6 + tg % 2
                pend = []
                for m in range(KC):
                    W, wt = WO[m // 2]
                    co = (m % 2) * 128
                    b = cnt % 4
                    cnt += 1
                    for kc in range(KC):
                        S.op("pe", lambda e, b=b, W=W, co=co, kc=kc, tsl=tsl: e.matmul(
                            bank(b), lhsT=W[:, kc, co:co + 128], rhs=YC[:, kc, tsl], start=(kc == 0), stop=(kc == KC - 1)),
                            reads=[wt, YCT[kc][tg]], writes=[PST[b]])
                    dv = DSB[tg % 2][:, m, :]
                    S.op("act", lambda e, dv=dv, b=b: e.activation(out=dv, in_=bank(b), func=AF.Copy),
                         reads=[PST[b]], writes=DSBT[tg % 2])
                    q = cnt % 4
                    sqv, sqt = SQ3[q], SQ3T[q]
                    S.op("act", lambda e, sqv=sqv, b=b: e.activation(out=sqv, in_=bank(b), func=AF.Square),
                         reads=[PST[b]], writes=[sqt])
                    for f in pend:
                        f()
                    pend = []

                    def mk(sb=sb, sqv=sqv, sqt=sqt, m=m):
                        S.op("pe", lambda e: e.matmul(bank(sb), lhsT=ONES, rhs=sqv, start=(m == 0), stop=(m == KC - 1)),
                             reads=[sqt, ONEST], writes=[PST[sb]])
                    pend.append(mk)
                for f in pend:
                    f()
                postnorm_apply(l, 24, tg * TG, DSB[tg % 2], DSBT[tg % 2], sb, R2[tg % 2], R2T[tg % 2], 1.0)
            RG.done(4)

        all_stores = []

        def x_load(s, hf):
            for c in range(KC):
                S.op("sp", lambda e, s=s, c=c, hf=hf: e.dma_start(
                    out=X[:, c, hf * 1024:(hf + 1) * 1024], in_=xT[s][:, c * SEQ + hf * 1024:c * SEQ + (hf + 1) * 1024]),
                    writes=XT[c][2 * hf:2 * hf + 2], dma=True, dma_tile=XLD[c][hf], nobarrier=True)

        def x_store(s, hf):
            for c in range(KC):
                all_stores.append(S.op("sp", lambda e, s=s, c=c, hf=hf: e.dma_start(
                    out=yT[s][:, c * SEQ + hf * 1024:c * SEQ + (hf + 1) * 1024], in_=X[:, c, hf * 1024:(hf + 1) * 1024]),
                    reads=XT[c][2 * hf:2 * hf + 2], dma=True, dma_tile=XST[c][hf], nobarrier=True))

        x_load(0, 0)
        x_load(0, 1)

        def on_post(s, hf):
            x_store(s, hf)
            if s + 1 < nseq:
                x_load(s + 1, hf)

        glist = []
        for s in range(nseq):
            for pi, (l, stg) in enumerate(plan):
                glist.append((s, l, stg, pi == len(plan) - 1))
        gi = 0
        while gi < len(glist):
            s, l, stg, last = glist[gi]
            if stg == "mix":
                mix_stage(l)
                gi += 1
                if last:
                    S.barrier()
                    for hf in range(2):
                        on_post(s, hf)
            else:
                chain = []
                while gi < len(glist) and glist[gi][2] != "mix":
                    s2, l2, stg2, last2 = glist[gi]
                    chain.append((l2, 0 if stg2 == "ffn1" else 1, s2, last2))
                    gi += 1
                ffn_chain(chain, on_post)
        S.op("sp", lambda e: e.nop(), after=all_stores)
        S.run()
    return nc


def _prep_shared(inp):
    L = DEPTH
    f = lambda a: np.ascontiguousarray(np.asarray(a, dtype=np.float32))
    wgu = np.empty((L, 2, NJ, 128, 2, KC, 128), np.float32)
    wdn = np.empty((L, 2, KC, 128, NJ, 128), np.float32)
    for i, (ngu, ndn) in enumerate((("ffn1_w_gu", "ffn1_w_down"), ("ffn2_w_gu", "ffn2_w_down"))):
        g = f(inp[ngu]).reshape(L, KC, 128, 2, NJ, 128)
        wgu[:, i] = g.transpose(0, 4, 2, 3, 1, 5)
        d = f(inp[ndn]).reshape(L, NJ, 128, KC, 128)
        wdn[:, i] = d.transpose(0, 3, 2, 1, 4)
    wgu = wgu.reshape(L * 2 * NJ, 128, 2048)
    wdn = wdn.reshape(L * 2 * KC, 128, NJ * 128)
    cst = np.eye(128, dtype=np.float32)

    w_in = f(inp["w_in"])
    r = np.arange
    kr = 1152
    col_sets = [
        np.concatenate([r(0, 128), r(512, 640)]),
        np.concatenate([r(256, 384), r(128, 256)]),
        np.concatenate([r(640, 768), r(384, 512)]),
        r(1184, 1440),
        r(1440, 1696),
        r(768, 1024),
        np.concatenate([r(1024, 1152)] + [r(kr, kr + 32)] * 3),
        np.concatenate([np.concatenate([r(kr + 16, kr + 32), r(kr, kr + 16)])] * 3),
    ]
    wmix = np.zeros((L, 128, NMIX), np.float32)
    off = 0
    for cs in col_sets:
        blk = w_in[:, :, cs].reshape(L, KC, 128, len(cs)).transpose(0, 2, 1, 3)
        n = KC * len(cs)
        wmix[:, :, off:off + n] = blk.reshape(L, 128, n)
        off += n
    w_out = f(inp["w_out"])
    for mp in range(4):
        blk = w_out[:, :, mp * 256:(mp + 1) * 256].reshape(L, KC, 128, 256).transpose(0, 2, 1, 3)
        wmix[:, :, off:off + WO_ITEM] = blk.reshape(L, 128, WO_ITEM)
        off += WO_ITEM
    w_uq = f(inp["w_uq"]).reshape(L, 2, 128, HEADS, QK_DIM)
    a0 = w_uq
    rot = np.concatenate([r(0, 64), r(80, 96), r(64, 80)])
    a1 = w_uq[..., rot]
    wuq = np.stack([a0, a1], 0).transpose(1, 3, 4, 0, 2, 5)
    wmix[:, :, off:off + NWUQ] = wuq.reshape(L, 128, NWUQ)
    off += NWUQ
    w_ukv = f(inp["w_ukv"]).reshape(L, 128, HEADS, 128)
    wmix[:, :, off:off + 512] = w_ukv[..., :64].reshape(L, 128, 512)
    off += 512
    wmix[:, :, off:off + 512] = w_ukv[..., 64:].reshape(L, 128, 512)
    off += 512
    ws = f(inp["gmlp_ws"])
    wmix[:, :, off:off + 512] = ws.transpose(0, 3, 1, 2).reshape(L, 128, 512)
    off += 512
    assert off == NMIX

    par = np.zeros((L, 128, NPAR), np.float32)
    for ci, nm in enumerate(("ffn1_pre_g", "ffn1_post_g", "mix_pre_g", "mix_post_g", "ffn2_pre_g", "ffn2_post_g")):
        par[:, :, ci * 8:(ci + 1) * 8] = f(inp[nm]).reshape(L, KC, 128).transpose(0, 2, 1)
    cw = f(inp["conv_w"]).reshape(L, 3, 2, 128)
    par[:, :, 48:54] = cw.transpose(0, 3, 2, 1).reshape(L, 128, 6)
    par[:, :, 54:56] = f(inp["conv_b"]).reshape(L, 2, 128).transpose(0, 2, 1)
    par[:, :, 56:58] = f(inp["q_norm_g"]).reshape(L, 2, 128).transpose(0, 2, 1)
    par[:, :, 58] = f(inp["kv_norm_g"])
    par[:, :, 64:320] = f(inp["gmlp_norm_g"])[:, None, :]
    gb = f(inp["gmlp_b"])
    for fc in range(2):
        par[:, 0:64, 320 + fc * 128:320 + (fc + 1) * 128] = gb[:, 2 * fc, None, :]
        par[:, 64:128, 320 + fc * 128:320 + (fc + 1) * 128] = gb[:, 2 * fc + 1, None, :]

    pos = np.arange(SEQ, dtype=np.float32)
    inv = (1.0 / (np.float32(10000.0) ** (np.arange(0, 32, 2, dtype=np.float32) / np.float32(32)))).astype(np.float32)
    ang = (pos[:, None] * inv[None, :]).astype(np.float32)
    cos = np.cos(ang).astype(np.float32).T
    sin = np.sin(ang).astype(np.float32).T
    rope = np.zeros((128, 2, SEQ), np.float32)
    rope[64:80, 0] = cos
    rope[80:96, 0] = cos
    rope[64:80, 1] = -sin
    rope[80:96, 1] = sin
    rope = rope.reshape(128, 2 * SEQ)
    return {"wgu": wgu, "wdn": wdn, "cst": cst, "wmix": wmix, "par": par, "rope": rope}


def _prep_x(x):
    xt = np.asarray(x, np.float32).reshape(BATCH, SEQ, KC, 128).transpose(0, 3, 2, 1)
    return np.ascontiguousarray(xt).reshape(NCORES, NSEQ, 128, KC * SEQ)


def _unprep_y(ys):
    y = np.stack(ys, 0).reshape(BATCH, 128, KC, SEQ).transpose(0, 3, 2, 1)
    return np.ascontiguousarray(y).reshape(BATCH, SEQ, D_MODEL)


def kernel(**inputs):
    shared = _prep_shared(inputs)
    xs = _prep_x(inputs["x"])
    nc = build_program()
    in_maps = [dict(shared, xT=xs[c]) for c in range(NCORES)]
    res = run_bass_kernel_spmd(nc, in_maps, core_ids=list(range(NCORES)))
    return _unprep_y([res.results[c]["yT"] for c in range(NCORES)])
```
